# Optimizing a Trainium2 kernel written in Bass

```python
import math
import jax, jax.numpy as jnp
from jax import lax
import numpy as np

D_MODEL = 2048
BATCH = 4
SEQ = 8192
DEPTH = 4

N_MIXERS = 3
N_RWKV = (DEPTH + 2) // 3
N_MLSTM = (DEPTH + 1) // 3
N_DSA = DEPTH // 3

RMS_EPS = 1e-6
OUT_SCALE = (2 * DEPTH) ** -0.5

D_FF = -(-8 * D_MODEL // (3 * 256)) * 256

RWKV_HEAD = 64
RWKV_HEADS = D_MODEL // RWKV_HEAD
RWKV_DECAY_LORA = max(32, int(round(1.8 * D_MODEL ** 0.5 / 32)) * 32)
RWKV_AAA_LORA = max(32, int(round(1.8 * D_MODEL ** 0.5 / 32)) * 32)
RWKV_MV_LORA = max(32, int(round(1.3 * D_MODEL ** 0.5 / 32)) * 32)
RWKV_GATE_LORA = max(32, int(round(0.6 * D_MODEL ** 0.8 / 32)) * 32)
RWKV_DECAY_SCALE = math.exp(-0.5)
RWKV_GN_EPS = 64e-5

MLSTM_HEADS = 4
MLSTM_DQK = D_MODEL // 2 // MLSTM_HEADS
MLSTM_DV = D_MODEL // MLSTM_HEADS
MLSTM_CHUNK = 64
MLSTM_GATE_CAP = 15.0

DSA_HEADS = 16
DSA_HEAD_DIM = 128
IDX_HEADS = 16
IDX_DIM = 64
DSA_TOPK_MAX = 256
DSA_Q_BLOCK = 128
T5_BUCKETS = 32
T5_MAX_DISTANCE = 128

kernel_name = 'hybrid_rwkv7_mlstm_dsa_trunk'


def _rmsnorm(x, g):
    xf = x.astype(jnp.float32)
    y = xf * lax.rsqrt(jnp.mean(xf * xf, axis=-1, keepdims=True) + RMS_EPS)
    return (y * g.astype(jnp.float32)).astype(x.dtype)


def _swiglu(x, w_gate, w_up, w_down):
    return (jax.nn.silu(x @ w_gate) * (x @ w_up)) @ w_down


def _split_last(t, sizes):
    cuts = np.cumsum(sizes)[:-1].tolist()
    return jnp.split(t, cuts, axis=-1)


def _rwkv7_scan(r, w, k, v, a, b):
    bsz, _, h, n = r.shape

    def step(state, inp):
        r_t, w_t, k_t, v_t, a_t, b_t = inp
        sa = jnp.einsum('bhvk,bhk->bhv', state, a_t)
        state = (state * w_t[:, :, None, :] + sa[..., None] * b_t[:, :, None, :]
                 + v_t[..., None] * k_t[:, :, None, :])
        return state, jnp.einsum('bhvk,bhk->bhv', state, r_t)

    xs = tuple(jnp.moveaxis(t, 1, 0) for t in (r, w, k, v, a, b))
    state0 = jnp.zeros((bsz, h, n, n), jnp.float32)
    _, ys = lax.scan(step, state0, xs)
    return jnp.moveaxis(ys, 0, 1)


def _rwkv7_mix(x, mu, w_rkv, w0, w1, w2, a0, a1, a2, g1, g2, k_k, k_a, r_k,
               lnx_g, lnx_b, w_o, v_first, vres):
    bsz, seq, d = x.shape
    H, N = RWKV_HEADS, RWKV_HEAD
    f32 = jnp.float32
    x_prev = jnp.pad(x, ((0, 0), (1, 0), (0, 0)))[:, :-1]
    xx = x_prev - x
    xr, xw, xk, xv, xa, xg = (x + xx * mu[i] for i in range(6))
    r = xr @ w_rkv[0]
    k = xk @ w_rkv[1]
    v = xv @ w_rkv[2]
    if vres is None:
        v_first = v
    else:
        v0, v1, v2 = vres
        v = v + (v_first - v) * jax.nn.sigmoid(v0 + (xv @ v1) @ v2)
    decay = jnp.exp(-RWKV_DECAY_SCALE * jax.nn.sigmoid((w0 + jnp.tanh(xw @ w1) @ w2).astype(f32)))
    a = jax.nn.sigmoid(a0 + (xa @ a1) @ a2).astype(f32)
    g = jax.nn.sigmoid(xg @ g1) @ g2
    heads = lambda t: t.astype(f32).reshape(bsz, seq, H, N)
    kk = heads(k * k_k)
    kk = kk / jnp.maximum(jnp.sqrt(jnp.sum(kk * kk, axis=-1, keepdims=True)), 1e-12)
    k = k.astype(f32) * (1.0 + (a - 1.0) * k_a.astype(f32))
    rh, wh, kh, vh, ah = heads(r), heads(decay), heads(k), heads(v), heads(a)
    y = _rwkv7_scan(rh, wh, kh, vh, -kk, kk * ah)
    mean = jnp.mean(y, axis=-1, keepdims=True)
    var = jnp.mean(jnp.square(y - mean), axis=-1, keepdims=True)
    y = ((y - mean) * lax.rsqrt(var + RWKV_GN_EPS) * lnx_g.astype(f32).reshape(H, N)
         + lnx_b.astype(f32).reshape(H, N))
    bonus = jnp.sum(rh * kh * r_k.astype(f32), axis=-1, keepdims=True) * vh
    out = ((y + bonus).reshape(bsz, seq, d) * g.astype(f32)).astype(x.dtype)
    return out @ w_o, v_first


def _soft_cap(z):
    return MLSTM_GATE_CAP * jnp.tanh(z / MLSTM_GATE_CAP)


def _mlstm_chunkwise(q, k, v, li, lf):
    bsz, h, seq, dk = q.shape
    dv = v.shape[-1]
    L = MLSTM_CHUNK
    nc = seq // L
    chunks = lambda t: jnp.moveaxis(t.reshape(bsz, h, nc, L, *t.shape[3:]), 2, 0)
    causal = jnp.tril(jnp.ones((L, L), dtype=bool))

    def step(carry, inp):
        c_st, n_st, m_st = carry
        qc, kc, vc, lic, lfc = inp
        bcum = jnp.cumsum(lfc, axis=-1)
        dmat = jnp.where(causal, bcum[..., :, None] - bcum[..., None, :] + lic[..., None, :], -jnp.inf)
        inter = bcum + m_st[..., None]
        m_t = jnp.maximum(inter, jnp.max(dmat, axis=-1))
        sc = jnp.einsum('bhtd,bhsd->bhts', qc, kc) * jnp.exp(dmat - m_t[..., None])
        w_inter = jnp.exp(inter - m_t)
        num = (jnp.einsum('bhts,bhsv->bhtv', sc, vc)
               + w_inter[..., None] * jnp.einsum('bhtd,bhdv->bhtv', qc, c_st))
        den = jnp.sum(sc, axis=-1) + w_inter * jnp.einsum('bhtd,bhd->bht', qc, n_st)
        h_c = num / jnp.maximum(jnp.abs(den), jnp.exp(-m_t))[..., None]
        b_tot = bcum[..., -1]
        log_wk = b_tot[..., None] - bcum + lic
        m_new = jnp.maximum(b_tot + m_st, jnp.max(log_wk, axis=-1))
        carry_decay = jnp.exp(b_tot + m_st - m_new)
        wk = jnp.exp(log_wk - m_new[..., None])
        c_new = carry_decay[..., None, None] * c_st + jnp.einsum('bhs,bhsd,bhsv->bhdv', wk, kc, vc)
        n_new = carry_decay[..., None] * n_st + jnp.einsum('bhs,bhsd->bhd', wk, kc)
        return (c_new, n_new, m_new), h_c

    carry0 = (jnp.zeros((bsz, h, dk, dv), jnp.float32),
              jnp.zeros((bsz, h, dk), jnp.float32),
              jnp.zeros((bsz, h), jnp.float32))
    _, hs = lax.scan(step, carry0, tuple(chunks(t) for t in (q, k, v, li, lf)))
    return jnp.moveaxis(hs, 0, 2).reshape(bsz, h, seq, dv)


def _mlstm_mix(x, w_in, b_if, norm_g, w_o):
    bsz, seq, _ = x.shape
    H, DK, DV = MLSTM_HEADS, MLSTM_DQK, MLSTM_DV
    f32 = jnp.float32
    q, k, v, o, ig, fg = _split_last(x @ w_in, [H * DK, H * DK, H * DV, H * DV, H, H])
    to_heads = lambda t, dd: jnp.transpose(t.astype(f32).reshape(bsz, seq, H, dd), (0, 2, 1, 3))
    li = jnp.transpose(_soft_cap(ig.astype(f32) + b_if[0].astype(f32)), (0, 2, 1))
    lf = jnp.transpose(jax.nn.log_sigmoid(_soft_cap(fg.astype(f32) + b_if[1].astype(f32))), (0, 2, 1))
    h = _mlstm_chunkwise(to_heads(q, DK) * DK ** -0.5, to_heads(k, DK), to_heads(v, DV), li, lf)
    h = _rmsnorm(jnp.transpose(h, (0, 2, 1, 3)), norm_g.reshape(H, DV))
    out = (h.reshape(bsz, seq, H * DV) * jax.nn.sigmoid(o.astype(f32))).astype(x.dtype)
    return out @ w_o


def _t5_bucket(rel):
    n = jnp.maximum(rel, 0)
    exact = T5_BUCKETS // 2
    nf = jnp.maximum(n, exact).astype(jnp.float32)
    large = exact + (jnp.log(nf / exact) / math.log(T5_MAX_DISTANCE / exact)
                     * (T5_BUCKETS - exact)).astype(jnp.int32)
    return jnp.where(n < exact, n, jnp.minimum(large, T5_BUCKETS - 1))


def _dsa_mix(x, w_in, q_norm_g, k_norm_g, t5_table, w_o):
    bsz, seq, _ = x.shape
    H, DH, HI, DI = DSA_HEADS, DSA_HEAD_DIM, IDX_HEADS, IDX_DIM
    f32 = jnp.float32
    q, k, v, qi, ki, wi = _split_last(x @ w_in, [H * DH, DH, DH, HI * DI, DI, HI])
    q = _rmsnorm(q.reshape(bsz, seq, H, DH), q_norm_g)
    k = _rmsnorm(k, k_norm_g)
    qi = qi.reshape(bsz, seq, HI, DI)
    wi = wi * HI ** -0.5
    n_sel = min(DSA_TOPK_MAX, seq // 4)
    nb = seq // DSA_Q_BLOCK
    pos = jnp.arange(seq, dtype=jnp.int32)
    blocks = lambda t: jnp.moveaxis(t.reshape(bsz, nb, DSA_Q_BLOCK, *t.shape[2:]), 1, 0)

    def attend_block(inp):
        qb, qib, wib, tpos = inp
        idx_logits = jnp.einsum('bthd,bsd->bths', qib, ki) * DI ** -0.5
        score = jnp.einsum('bths,bth->bts', jax.nn.relu(idx_logits), wib).astype(f32)
        score = jnp.where((pos[None, :] <= tpos[:, None])[None], score, -jnp.inf)
        _, sel = lax.top_k(score, n_sel)
        k_sel = jax.vmap(lambda kb, ib: kb[ib])(k, sel)
        v_sel = jax.vmap(lambda vb, ib: vb[ib])(v, sel)
        rel = tpos[None, :, None] - sel
        bias = jnp.moveaxis(t5_table[_t5_bucket(rel)], -1, 2).astype(f32)
        logits = jnp.einsum('bthd,btjd->bthj', qb, k_sel).astype(f32) * DH ** -0.5 + bias
        logits = jnp.where((rel >= 0)[:, :, None, :], logits, -jnp.inf)
        p = jax.nn.softmax(logits, axis=-1).astype(v.dtype)
        return jnp.einsum('bthj,btjd->bthd', p, v_sel).reshape(bsz, DSA_Q_BLOCK, H * DH)

    out = lax.map(attend_block, (blocks(q), blocks(qi), blocks(wi), pos.reshape(nb, DSA_Q_BLOCK)))
    out = jnp.moveaxis(out, 0, 1).reshape(bsz, seq, H * DH)
    return out @ w_o


def setup_inputs(seed: int = 0) -> dict:
    key = jax.random.key(seed)
    keys = jax.random.split(key, 64)
    counter = [0]

    def nxt():
        kk = keys[counter[0]]
        counter[0] += 1
        return kk

    def nrm(shape, scale):
        return jax.random.normal(nxt(), shape, jnp.float32) * scale

    def gain(shape):
        return 1.0 + nrm(shape, 0.05)

    D = D_MODEL
    nA, nB, nC = N_RWKV, N_MLSTM, N_DSA
    nV = max(N_RWKV - 1, 0)
    inp = {}
    inp['x'] = nrm((BATCH, SEQ, D), 1.0)
    inp['rwkv_mu'] = jax.random.uniform(nxt(), (nA, 6, D), jnp.float32)
    inp['rwkv_w_rkv'] = nrm((nA, 3, D, D), D ** -0.5)
    inp['rwkv_w0'] = nrm((nA, D), 1.0)
    inp['rwkv_w1'] = nrm((nA, D, RWKV_DECAY_LORA), D ** -0.5)
    inp['rwkv_w2'] = nrm((nA, RWKV_DECAY_LORA, D), 0.1 * RWKV_DECAY_LORA ** -0.5)
    inp['rwkv_a0'] = nrm((nA, D), 0.1)
    inp['rwkv_a1'] = nrm((nA, D, RWKV_AAA_LORA), D ** -0.5)
    inp['rwkv_a2'] = nrm((nA, RWKV_AAA_LORA, D), 0.1 * RWKV_AAA_LORA ** -0.5)
    inp['rwkv_v0'] = nrm((nV, D), 0.1)
    inp['rwkv_v1'] = nrm((nV, D, RWKV_MV_LORA), D ** -0.5)
    inp['rwkv_v2'] = nrm((nV, RWKV_MV_LORA, D), 0.1 * RWKV_MV_LORA ** -0.5)
    inp['rwkv_g1'] = nrm((nA, D, RWKV_GATE_LORA), D ** -0.5)
    inp['rwkv_g2'] = nrm((nA, RWKV_GATE_LORA, D), RWKV_GATE_LORA ** -0.5)
    inp['rwkv_k_k'] = 0.85 + nrm((nA, D), 0.05)
    inp['rwkv_k_a'] = gain((nA, D))
    inp['rwkv_r_k'] = nrm((nA, RWKV_HEADS, RWKV_HEAD), 0.1)
    inp['rwkv_lnx_g'] = gain((nA, D))
    inp['rwkv_lnx_b'] = nrm((nA, D), 0.02)
    inp['rwkv_w_o'] = nrm((nA, D, D), D ** -0.5 * OUT_SCALE)
    p_m = 2 * MLSTM_HEADS * MLSTM_DQK + 2 * MLSTM_HEADS * MLSTM_DV + 2 * MLSTM_HEADS
    inp['mlstm_w_in'] = nrm((nB, D, p_m), D ** -0.5)
    f_bias = jnp.linspace(3.0, 6.0, MLSTM_HEADS, dtype=jnp.float32)
    inp['mlstm_b_if'] = jnp.stack([nrm((nB, MLSTM_HEADS), 0.1),
                                   f_bias + nrm((nB, MLSTM_HEADS), 0.1)], axis=1)
    inp['mlstm_norm_g'] = gain((nB, MLSTM_HEADS * MLSTM_DV))
    inp['mlstm_w_o'] = nrm((nB, MLSTM_HEADS * MLSTM_DV, D), (MLSTM_HEADS * MLSTM_DV) ** -0.5 * OUT_SCALE)
    p_c = DSA_HEADS * DSA_HEAD_DIM + 2 * DSA_HEAD_DIM + IDX_HEADS * IDX_DIM + IDX_DIM + IDX_HEADS
    inp['dsa_w_in'] = nrm((nC, D, p_c), D ** -0.5)
    inp['dsa_q_norm_g'] = gain((nC, DSA_HEAD_DIM))
    inp['dsa_k_norm_g'] = gain((nC, DSA_HEAD_DIM))
    inp['dsa_w_o'] = nrm((nC, DSA_HEADS * DSA_HEAD_DIM, D), (DSA_HEADS * DSA_HEAD_DIM) ** -0.5 * OUT_SCALE)
    inp['t5_bias'] = nrm((T5_BUCKETS, DSA_HEADS), 0.5)
    inp['mix_norm_g'] = gain((DEPTH, D))
    inp['ffn_norm_g'] = gain((DEPTH, D))
    inp['ffn_w_gate'] = nrm((DEPTH, D, D_FF), D ** -0.5)
    inp['ffn_w_up'] = nrm((DEPTH, D, D_FF), D ** -0.5)
    inp['ffn_w_down'] = nrm((DEPTH, D_FF, D), D_FF ** -0.5 * OUT_SCALE)
    return inp


def reference(x, rwkv_mu, rwkv_w_rkv, rwkv_w0, rwkv_w1, rwkv_w2, rwkv_a0, rwkv_a1, rwkv_a2,
              rwkv_v0, rwkv_v1, rwkv_v2, rwkv_g1, rwkv_g2, rwkv_k_k, rwkv_k_a, rwkv_r_k,
              rwkv_lnx_g, rwkv_lnx_b, rwkv_w_o, mlstm_w_in, mlstm_b_if, mlstm_norm_g, mlstm_w_o,
              dsa_w_in, dsa_q_norm_g, dsa_k_norm_g, dsa_w_o, t5_bias, mix_norm_g, ffn_norm_g,
              ffn_w_gate, ffn_w_up, ffn_w_down):
    v_first = None
    for i in range(DEPTH):
        kind, j = i % N_MIXERS, i // N_MIXERS
        h = _rmsnorm(x, mix_norm_g[i])
        if kind == 0:
            vres = None if j == 0 else (rwkv_v0[j - 1], rwkv_v1[j - 1], rwkv_v2[j - 1])
            y, v_first = _rwkv7_mix(h, rwkv_mu[j], rwkv_w_rkv[j], rwkv_w0[j], rwkv_w1[j], rwkv_w2[j],
                                    rwkv_a0[j], rwkv_a1[j], rwkv_a2[j], rwkv_g1[j], rwkv_g2[j],
                                    rwkv_k_k[j], rwkv_k_a[j], rwkv_r_k[j], rwkv_lnx_g[j],
                                    rwkv_lnx_b[j], rwkv_w_o[j], v_first, vres)
        elif kind == 1:
            y = _mlstm_mix(h, mlstm_w_in[j], mlstm_b_if[j], mlstm_norm_g[j], mlstm_w_o[j])
        else:
            y = _dsa_mix(h, dsa_w_in[j], dsa_q_norm_g[j], dsa_k_norm_g[j], t5_bias, dsa_w_o[j])
        x = x + y.astype(x.dtype)
        h = _rmsnorm(x, ffn_norm_g[i])
        x = x + _swiglu(h, ffn_w_gate[i], ffn_w_up[i], ffn_w_down[i])
    return x
```

```python
import contextlib
import numpy as np
import concourse.bass as bass
import concourse.mybir as mybir
from concourse.bass_utils import run_bass_kernel_spmd

F32 = mybir.dt.float32
BF16 = mybir.dt.bfloat16
ALU = mybir.AluOpType
AF = mybir.ActivationFunctionType
AX = mybir.AxisListType


class T:
    __slots__ = ("ap", "lw", "rd", "name")

    def __init__(self, ap, name=""):
        self.ap = ap
        self.lw = None
        self.rd = {}
        self.name = name

    def __getitem__(self, idx):
        return self.ap[idx]


class Prog:
    ENGS = ("pe", "dve", "act", "pool", "sp")

    def __init__(self, name="k", n_dma_sems=12, same_engine_sync=True):
        self.nc = bass.Bass("TRN2", target_bir_lowering=False)
        self.es = contextlib.ExitStack()
        self.q = {e: [] for e in self.ENGS}
        self.cnt = {e: 0 for e in self.ENGS}
        self.seen = {e: {} for e in self.ENGS}
        self.sem = {}
        for e in self.ENGS:
            self.sem[e] = self.es.enter_context(self.nc.semaphore("s_" + e))
        self.dma_pool = {}
        self.dma_rr = {}
        self.dma_cnt = {}
        for qe in ("sp", "pool", "act"):
            names = []
            for j in range(n_dma_sems):
                k = "d_%s_%d" % (qe, j)
                self.sem[k] = self.es.enter_context(self.nc.semaphore(k))
                self.dma_cnt[k] = 0
                names.append(k)
            self.dma_pool[qe] = names
            self.dma_rr[qe] = 0
        self.same_engine_sync = same_engine_sync
        self.n_inst = 0
        self._uid = 0
        self.scopes = []

    def uid(self, p):
        self._uid += 1
        return "%s%d" % (p, self._uid)

    def push(self):
        self.scopes.append(contextlib.ExitStack())

    def pop(self):
        self.barrier()
        self.scopes.pop().close()

    def sb(self, shape, dt=F32, name=None):
        es = self.scopes[-1] if self.scopes else self.es
        nm = self.uid(name or "sb")
        h = es.enter_context(self.nc.sbuf_tensor(nm, list(shape), dt))
        return T(h, nm)

    def ps(self, shape, dt=F32, name=None):
        es = self.scopes[-1] if self.scopes else self.es
        nm = self.uid(name or "ps")
        h = es.enter_context(self.nc.psum_tensor(nm, list(shape), dt))
        return T(h, nm)

    def barrier(self):
        targets = {}
        for e in self.ENGS:
            if self.cnt[e]:
                targets[e] = self.cnt[e]
        for k, c in self.dma_cnt.items():
            if c:
                targets[k] = 16 * c
        for e in self.ENGS:
            for k, v in targets.items():
                if self.seen[e].get(k, 0) < v:
                    self.q[e].append(("w", k, v))
                    self.seen[e][k] = v

    def dram_in(self, name, shape, dt=F32):
        return self.nc.dram_tensor(name, list(shape), dt, kind="ExternalInput").ap()

    def dram_out(self, name, shape, dt=F32):
        return self.nc.dram_tensor(name, list(shape), dt, kind="ExternalOutput").ap()

    def dram_tmp(self, name, shape, dt=F32):
        return self.nc.dram_tensor(name, list(shape), dt, kind="Internal").ap()

    def _deps(self, e, reads, writes):
        deps = {}

        def add(t):
            if t is None:
                return
            k, v = t
            if deps.get(k, 0) < v:
                deps[k] = v

        for r in reads:
            add(r.lw)
        for w in writes:
            add(w.lw)
            for t in w.rd.values():
                add(t)
        for k, v in deps.items():
            if k == e and (e == "pe" or not self.same_engine_sync):
                continue
            if self.seen[e].get(k, 0) < v:
                self.q[e].append(("w", k, v))
                self.seen[e][k] = v

    def _mark(self, ticket, reads, writes):
        for r in reads:
            r.rd[ticket[0]] = ticket
        for w in writes:
            w.lw = ticket
            w.rd = {}

    def op(self, e, meth, reads=(), writes=(), **kw):
        self._deps(e, reads, writes)
        self.cnt[e] += 1
        ticket = (e, self.cnt[e])
        self.q[e].append(("o", (meth, kw), e, 1))
        self._mark(ticket, reads, writes)
        self.n_inst += 1

    def dma(self, qe, out, in_, reads=(), writes=(), **kw):
        pool = self.dma_pool[qe]
        k = pool[self.dma_rr[qe] % len(pool)]
        self.dma_rr[qe] += 1
        self._deps(qe, reads, writes)
        prev = 16 * self.dma_cnt[k]
        if prev and self.seen[qe].get(k, 0) < prev:
            self.q[qe].append(("w", k, prev))
            self.seen[qe][k] = prev
        self.dma_cnt[k] += 1
        ticket = (k, 16 * self.dma_cnt[k])
        kw = dict(kw); kw["out"] = out; kw["in_"] = in_
        self.q[qe].append(("o", ("dma_start", kw), k, 16))
        self._mark(ticket, reads, writes)
        self.n_inst += 1

    def mm(self, out_t, out_ap, lhsT_t, lhsT_ap, rhs_t, rhs_ap, start=True, stop=True):
        self.op("pe", "matmul", reads=[lhsT_t, rhs_t], writes=[out_t],
                out=out_ap, lhsT=lhsT_ap, rhs=rhs_ap, start=start, stop=stop)

    def finish(self):
        for qe, pool in self.dma_pool.items():
            for k in pool:
                v = 16 * self.dma_cnt[k]
                if v and self.seen[qe].get(k, 0) < v:
                    self.q[qe].append(("w", k, v))
                    self.seen[qe][k] = v
        nc = self.nc
        engmap = {"pe": "tensor", "dve": "vector", "act": "scalar", "pool": "gpsimd", "sp": "sync"}
        with nc.Block() as block:
            for e in self.ENGS:
                items = self.q[e]
                if not items:
                    continue

                def body(eng, items=items):
                    for it in items:
                        if it[0] == "w":
                            eng.wait_ge(self.sem[it[1]], it[2])
                        else:
                            try:
                                ins = getattr(eng, it[1][0])(**it[1][1])
                            except Exception:
                                print("FAILED INSTR", it[1][0], {k: (str(v)[:300]) for k, v in it[1][1].items()})
                                raise
                            ins.then_inc(self.sem[it[2]], it[3])

                getattr(block, engmap[e])(body)
        self.es.close()
        return nc


D = 2048
DFF = 5632
KT = 16
EPS = 1e-6


class Ctx:
    pass


def setup_consts(p):
    c = Ctx()
    c.p = p
    c.ident = p.sb([128, 128], F32, "ident")
    p.op("pool", "memset", writes=[c.ident], ap=c.ident[:], constant=1.0)
    p.op("pool", "affine_select", reads=[c.ident], writes=[c.ident], out=c.ident[:], in_=c.ident[:],
         pattern=[[-1, 128]], compare_op=ALU.is_equal, fill=0.0, base=0, channel_multiplier=1)
    c.PS = [p.ps([128, 512], F32, "bank%d" % i) for i in range(8)]
    c.rr = 0
    return c


def bcast_row(p, dst, src_ap_1d, n, q="sp"):
    p.dma(q, dst[:, 0:n], src_ap_1d.partition_broadcast(128), writes=[dst])


def rmsnorm_tile(p, x_t, x_ap, g_t, out_t, out_ap, tmp_t, ss_t, d=D, eps=EPS):
    p.op("act", "activation", reads=[x_t], writes=[tmp_t], out=tmp_t[:, 0:d], in_=x_ap, func=AF.Square)
    p.op("dve", "reduce_sum", reads=[tmp_t], writes=[ss_t], out=ss_t[:, 0:1], in_=tmp_t[:, 0:d], axis=AX.X)
    p.op("dve", "tensor_scalar", reads=[ss_t], writes=[ss_t], out=ss_t[:, 1:2], in0=ss_t[:, 0:1],
         scalar1=1.0 / d, scalar2=eps, op0=ALU.mult, op1=ALU.add)
    p.op("act", "activation", reads=[ss_t], writes=[ss_t], out=ss_t[:, 1:2], in_=ss_t[:, 1:2], func=AF.Sqrt)
    p.op("dve", "reciprocal", reads=[ss_t], writes=[ss_t], out=ss_t[:, 0:1], in_=ss_t[:, 1:2])
    p.op("dve", "scalar_tensor_tensor", reads=[x_t, ss_t, g_t], writes=[out_t], out=out_ap, in0=x_ap,
         scalar=ss_t[:, 0:1], in1=g_t[:, 0:d], op0=ALU.mult, op1=ALU.mult)


def to_fm(c, src_t, src_ap_fn, nkt, dstT, col0, evac_alt=0):
    p = c.p
    for g in range(0, nkt, 4):
        n = min(4, nkt - g)
        ps = c.PS[6 + (c.rr % 2)]
        c.rr += 1
        for j in range(n):
            p.op("pe", "transpose", reads=[src_t, c.ident], writes=[ps], out=ps[:, j * 128:(j + 1) * 128],
                 in_=src_ap_fn(g + j), identity=c.ident[:])
        eng = "dve" if (c.rr % 2) else "act"
        src = ps[:, 0:n * 128].rearrange("p (a b) -> p a b", a=n)
        dst = dstT[:, g:g + n, col0:col0 + 128]
        if eng == "dve":
            p.op("dve", "tensor_copy", reads=[ps], writes=[dstT], out=dst, in_=src)
        else:
            p.op("act", "copy", reads=[ps], writes=[dstT], out=dst, in_=src)


class WLoader:
    def __init__(self, c, nbuf=2):
        p = c.p
        self.c = c
        self.st = [p.sb([128, 8, 512], F32, "wst") for _ in range(2)]
        self.wb = [p.sb([128, 16, 512], BF16, "wb") for _ in range(nbuf)]
        self.i = 0
        self.j = 0

    def load(self, W, k0, ksz, n0, nsz):
        p = self.c.p
        wb = self.wb[self.i % len(self.wb)]
        self.i += 1
        nk = ksz // 128
        for h in range(0, nk, 8):
            hk = min(8, nk - h)
            st = self.st[self.j % 2]
            self.j += 1
            p.dma("sp", st[:, 0:hk, 0:nsz],
                  W[k0 + h * 128:k0 + (h + hk) * 128, n0:n0 + nsz].rearrange("(kt p) n -> p kt n", p=128),
                  writes=[st])
            if self.j % 2:
                p.op("act", "copy", reads=[st], writes=[wb], out=wb[:, h:h + hk, 0:nsz], in_=st[:, 0:hk, 0:nsz])
            else:
                p.op("pool", "tensor_copy", reads=[st], writes=[wb], out=wb[:, h:h + hk, 0:nsz], in_=st[:, 0:hk, 0:nsz])
        return wb


def ffn_phase(c, X, S, gvec, Wg, Wu, Wd):
    p = c.p
    TG = min(512, S)
    NT = TG // 128
    NF = DFF // 128
    p.push()
    wl = WLoader(c)
    g_bc = p.sb([128, D], F32, "g_bc")
    bcast_row(p, g_bc, gvec, D)
    xres = [p.sb([128, D], F32, "xres") for _ in range(NT)]
    hn = p.sb([128, D], F32, "hn")
    tmp = p.sb([128, D], F32, "tmp")
    ss = p.sb([128, 2], F32, "ss")
    hT = p.sb([128, KT, TG], BF16, "hT")
    hidT = p.sb([128, NF, TG], BF16, "hidT")
    sg = [p.sb([128, TG], F32, "sg") for _ in range(2)]
    ot = [p.sb([128, 512], F32, "ot") for _ in range(2)]
    for t0 in range(0, S, TG):
        for ti in range(NT):
            p.dma("sp", xres[ti][:], X[t0 + ti * 128:t0 + (ti + 1) * 128, :], writes=[xres[ti]])
            rmsnorm_tile(p, xres[ti], xres[ti][:], g_bc, hn, hn[:], tmp, ss)
            to_fm(c, hn, lambda kt: hn[:, kt * 128:(kt + 1) * 128], KT, hT, ti * 128)
        for fb in range(0, NF, 4):
            nf = min(4, NF - fb)
            wg = wl.load(Wg, 0, D, fb * 128, nf * 128)
            wu = wl.load(Wu, 0, D, fb * 128, nf * 128)
            for j in range(nf):
                fc = fb + j
                pg = c.PS[(fc % 2) * 2]
                pu = c.PS[(fc % 2) * 2 + 1]
                for kt in range(KT):
                    p.mm(pg, pg[:, 0:TG], wg, wg[:, kt, j * 128:(j + 1) * 128], hT, hT[:, kt, :], start=(kt == 0), stop=(kt == KT - 1))
                for kt in range(KT):
                    p.mm(pu, pu[:, 0:TG], wu, wu[:, kt, j * 128:(j + 1) * 128], hT, hT[:, kt, :], start=(kt == 0), stop=(kt == KT - 1))
                s_ = sg[fc % 2]
                p.op("act", "activation", reads=[pg], writes=[s_], out=s_[:, 0:TG], in_=pg[:, 0:TG], func=AF.Silu)
                p.op("dve", "tensor_tensor", reads=[s_, pu], writes=[hidT], out=hidT[:, fc, :], in0=s_[:, 0:TG], in1=pu[:, 0:TG], op=ALU.mult)
        for nch in range(D // 512):
            for kb in range(0, NF, 16):
                nk = min(16, NF - kb)
                wd = wl.load(Wd, kb * 128, nk * 128, nch * 512, 512)
                for ti in range(NT):
                    ps = c.PS[ti]
                    for kt in range(nk):
                        p.mm(ps, ps[:, :], hidT, hidT[:, kb + kt, ti * 128:(ti + 1) * 128], wd, wd[:, kt, :],
                             start=(kb + kt == 0), stop=(kb + kt == NF - 1))
            for ti in range(NT):
                o = ot[ti % 2]
                p.op("dve", "tensor_tensor", reads=[c.PS[ti], xres[ti]], writes=[o], out=o[:], in0=c.PS[ti][:],
                     in1=xres[ti][:, nch * 512:(nch + 1) * 512], op=ALU.add)
                p.dma("pool", X[t0 + ti * 128:t0 + (ti + 1) * 128, nch * 512:(nch + 1) * 512], o[:], reads=[o])
    p.pop()


NH = 32
HS = 64
DECAY_SCALE = 0.6065306597126334


def evac(p, eng, ps_t, src, dst_t, dst):
    if eng == "act":
        p.op("act", "copy", reads=[ps_t], writes=[dst_t], out=dst, in_=src)
    else:
        p.op("dve", "tensor_copy", reads=[ps_t], writes=[dst_t], out=dst, in_=src)


def rwkv_prep(c, X, S, W, sc, first):
    p = c.p
    p.push()
    wl = WLoader(c)
    names = ["r", "w", "k", "v", "a", "g"]
    g_bc = p.sb([128, D], F32, "g_bc")
    bcast_row(p, g_bc, W["mix_g"], D)
    mu_fm = p.sb([128, 6 * KT], F32, "mu_fm")
    p.dma("sp", mu_fm[:], W["mu_fm"], writes=[mu_fm])
    vnames = ["w0", "a0", "k_k", "k_a", "r_k"] + ([] if first else ["v0"])
    vch = {nm: p.sb([128, 512], F32, "vc_" + nm) for nm in vnames}
    xT = {nm: p.sb([128, KT, 128], BF16, "xT_" + nm) for nm in names}
    xt = p.sb([128, D], F32, "xt")
    hn = p.sb([128, D], F32, "hn")
    ss = p.sb([128, 2], F32, "ss")
    hT = p.sb([128, KT, 128], F32, "hT")
    xxT = p.sb([128, KT, 128], F32, "xxT")
    last = p.sb([128, KT, 1], F32, "last")
    p.op("pool", "memset", writes=[last], ap=last[:], constant=0.0)
    lor = {"w": 96, "a": 96, "g": 256}
    if not first:
        lor["v"] = 64
    lT = {nm: p.sb([128, 2, 128], BF16, "lT_" + nm) for nm in lor}
    l2 = {nm: p.sb([128, 2, 512], BF16, "l2_" + nm) for nm in lor}
    l2s = p.sb([128, 2, 512], F32, "l2s")
    E = {nm: p.sb([128, 512], F32, "E_" + nm) for nm in ["r", "k", "v", "w", "a", "g", "kk", "t1", "t2", "vf"]}
    st8 = p.sb([128, 16], F32, "st8")
    l1w = {"w": W["w1"], "a": W["a1"], "g": W["g1"]}
    l2w = {"w": W["w2"], "a": W["a2"], "g": W["g2"]}
    if not first:
        l1w["v"] = W["v1"]
        l2w["v"] = W["v2"]
    lfn = {"w": AF.Tanh, "a": AF.Copy, "g": AF.Sigmoid, "v": AF.Copy}
    PS = c.PS
    for t in range(S // 128):
        rows = slice(t * 128, (t + 1) * 128)
        p.dma("sp", xt[:], X[rows, :], writes=[xt])
        rmsnorm_tile(p, xt, xt[:], g_bc, hn, hn[:], hn, ss)
        to_fm(c, hn, lambda kt: hn[:, kt * 128:(kt + 1) * 128], KT, hT, 0)
        p.op("dve", "tensor_tensor", reads=[hT], writes=[xxT], out=xxT[:, :, 1:128], in0=hT[:, :, 0:127], in1=hT[:, :, 1:128], op=ALU.subtract)
        p.op("dve", "tensor_tensor", reads=[hT, last], writes=[xxT], out=xxT[:, :, 0:1], in0=last[:], in1=hT[:, :, 0:1], op=ALU.subtract)
        p.op("pool", "tensor_copy", reads=[hT], writes=[last], out=last[:], in_=hT[:, :, 127:128])
        for i, nm in enumerate(names):
            for kt in range(KT):
                eng = "dve"
                p.op(eng, "scalar_tensor_tensor", reads=[xxT, mu_fm, hT], writes=[xT[nm]], out=xT[nm][:, kt, :], in0=xxT[:, kt, :],
                     scalar=mu_fm[:, i * KT + kt:i * KT + kt + 1], in1=hT[:, kt, :], op0=ALU.mult, op1=ALU.add)
        for nm, ld in lor.items():
            wb = wl.load(l1w[nm], 0, D, 0, ld)
            for j in range(0, ld, 128):
                lsz = min(128, ld - j)
                ps = PS[4]
                for kt in range(KT):
                    p.mm(ps, ps[0:lsz, 0:128], wb, wb[:, kt, j:j + lsz], xT[nm], xT[nm][:, kt, :], start=(kt == 0), stop=(kt == KT - 1))
                p.op("act", "activation", reads=[ps], writes=[lT[nm]], out=lT[nm][0:lsz, j // 128, :], in_=ps[0:lsz, 0:128], func=lfn[nm])
        for nch in range(4):
            cs = slice(nch * 512, (nch + 1) * 512)
            for nm in vnames:
                p.dma("sp", vch[nm][:], W[nm][cs].partition_broadcast(128), writes=[vch[nm]])
            for nm, ld in lor.items():
                for j in range(0, ld, 128):
                    lsz = min(128, ld - j)
                    p.dma("sp", l2s[0:lsz, j // 128, :], l2w[nm][j:j + lsz, cs], writes=[l2s])
                    p.op("pool", "tensor_copy", reads=[l2s], writes=[l2[nm]], out=l2[nm][0:lsz, j // 128, :], in_=l2s[0:lsz, j // 128, :])
            for i, nm in enumerate(["r", "k", "v"]):
                wb = wl.load(W["w_rkv"][i], 0, D, nch * 512, 512)
                ps = PS[i]
                for kt in range(KT):
                    p.mm(ps, ps[:, :], xT[nm], xT[nm][:, kt, :], wb, wb[:, kt, :], start=(kt == 0), stop=(kt == KT - 1))
                evac(p, "act", ps, ps[:, :], E[nm], E[nm][:])
            lps = {"w": PS[3], "a": PS[4], "g": PS[5], "v": PS[0]}
            for nm, ld in lor.items():
                nj = (ld + 127) // 128
                for jj in range(nj):
                    lsz = min(128, ld - jj * 128)
                    p.mm(lps[nm], lps[nm][:, :], lT[nm], lT[nm][0:lsz, jj, :], l2[nm], l2[nm][0:lsz, jj, :], start=(jj == 0), stop=(jj == nj - 1))
            def tt(eng, out_t, a_t, a, b_t, b, op, o=None):
                p.op(eng, "tensor_tensor", reads=[a_t, b_t], writes=[out_t], out=(o if o is not None else out_t[:]), in0=a, in1=b, op=op)
            tt("dve", E["w"], lps["w"], lps["w"][:], vch["w0"], vch["w0"][:], ALU.add)
            p.op("act", "activation", reads=[E["w"]], writes=[E["w"]], out=E["w"][:], in_=E["w"][:], func=AF.Sigmoid)
            p.op("pool", "tensor_scalar", reads=[E["w"]], writes=[E["w"]], out=E["w"][:], in0=E["w"][:], scalar1=-DECAY_SCALE, scalar2=None, op0=ALU.mult)
            p.dma("pool", sc["LW"][rows, cs], E["w"][:], reads=[E["w"]])
            tt("dve", E["a"], lps["a"], lps["a"][:], vch["a0"], vch["a0"][:], ALU.add)
            p.op("act", "activation", reads=[E["a"]], writes=[E["a"]], out=E["a"][:], in_=E["a"][:], func=AF.Sigmoid)
            evac(p, "act", lps["g"], lps["g"][:], E["g"], E["g"][:])
            p.dma("pool", sc["G"][rows, cs], E["g"][:], reads=[E["g"]])
            if first:
                p.dma("pool", sc["VF"][rows, cs], E["v"][:], reads=[E["v"]])
            else:
                p.dma("sp", E["vf"][:], sc["VF"][rows, cs], writes=[E["vf"]])
                tt("dve", E["t1"], lps["v"], lps["v"][:], vch["v0"], vch["v0"][:], ALU.add)
                p.op("act", "activation", reads=[E["t1"]], writes=[E["t1"]], out=E["t1"][:], in_=E["t1"][:], func=AF.Sigmoid)
                tt("pool", E["vf"], E["vf"], E["vf"][:], E["v"], E["v"][:], ALU.subtract)
                tt("dve", E["vf"], E["vf"], E["vf"][:], E["t1"], E["t1"][:], ALU.mult)
                tt("dve", E["v"], E["v"], E["v"][:], E["vf"], E["vf"][:], ALU.add)
            p.dma("pool", sc["V"][rows, cs], E["v"][:], reads=[E["v"]])
            p.dma("pool", sc["R"][rows, cs], E["r"][:], reads=[E["r"]])
            tt("pool", E["kk"], E["k"], E["k"][:], vch["k_k"], vch["k_k"][:], ALU.mult)
            tt("dve", E["t1"], E["kk"], E["kk"][:], E["kk"], E["kk"][:], ALU.mult)
            p.op("dve", "reduce_sum", reads=[E["t1"]], writes=[st8], out=st8[:, 0:8], in_=E["t1"][:].rearrange("p (h k) -> p h k", k=HS), axis=AX.X)
            p.op("act", "activation", reads=[st8], writes=[st8], out=st8[:, 0:8], in_=st8[:, 0:8], func=AF.Sqrt)
            p.op("dve", "tensor_scalar", reads=[st8], writes=[st8], out=st8[:, 0:8], in0=st8[:, 0:8], scalar1=1e-12, scalar2=None, op0=ALU.max)
            p.op("dve", "reciprocal", reads=[st8], writes=[st8], out=st8[:, 8:16], in_=st8[:, 0:8])
            p.op("dve", "tensor_tensor", reads=[E["kk"], st8], writes=[E["kk"]], out=E["kk"][:].rearrange("p (h k) -> p h k", k=HS),
                 in0=E["kk"][:].rearrange("p (h k) -> p h k", k=HS), in1=st8[:, 8:16].unsqueeze(2).to_broadcast([128, 8, HS]), op=ALU.mult)
            p.op("pool", "tensor_scalar", reads=[E["a"]], writes=[E["t1"]], out=E["t1"][:], in0=E["a"][:], scalar1=-1.0, scalar2=None, op0=ALU.add)
            tt("pool", E["t1"], E["t1"], E["t1"][:], vch["k_a"], vch["k_a"][:], ALU.mult)
            p.op("pool", "tensor_scalar", reads=[E["t1"]], writes=[E["t1"]], out=E["t1"][:], in0=E["t1"][:], scalar1=1.0, scalar2=None, op0=ALU.add)
            tt("dve", E["k"], E["k"], E["k"][:], E["t1"], E["t1"][:], ALU.mult)
            p.dma("pool", sc["KP"][rows, cs], E["k"][:], reads=[E["k"]])
            tt("dve", E["t2"], E["kk"], E["kk"][:], E["a"], E["a"][:], ALU.mult)
            p.dma("pool", sc["B"][rows, cs], E["t2"][:], reads=[E["t2"]])
            p.op("pool", "tensor_scalar", reads=[E["kk"]], writes=[E["kk"]], out=E["kk"][:], in0=E["kk"][:], scalar1=-1.0, scalar2=None, op0=ALU.mult)
            p.dma("pool", sc["A"][rows, cs], E["kk"][:], reads=[E["kk"]])
            tt("dve", E["t1"], E["r"], E["r"][:], E["k"], E["k"][:], ALU.mult)
            tt("dve", E["t1"], E["t1"], E["t1"][:], vch["r_k"], vch["r_k"][:], ALU.mult)
            p.op("dve", "reduce_sum", reads=[E["t1"]], writes=[st8], out=st8[:, 0:8], in_=E["t1"][:].rearrange("p (h k) -> p h k", k=HS), axis=AX.X)
            p.op("dve", "tensor_tensor", reads=[E["v"], st8], writes=[E["t1"]], out=E["t1"][:].rearrange("p (h k) -> p h k", k=HS),
                 in0=E["v"][:].rearrange("p (h k) -> p h k", k=HS), in1=st8[:, 0:8].unsqueeze(2).to_broadcast([128, 8, HS]), op=ALU.mult)
            p.dma("pool", sc["BON"][rows, cs], E["t1"][:], reads=[E["t1"]])
    p.pop()


def rwkv_scan(c, S, sc):
    p = c.p
    p.push()
    PS = c.PS
    ident = c.ident
    ule = p.sb([128, 128], F32, "ule")
    p.op("pool", "memset", writes=[ule], ap=ule[:], constant=1.0)
    p.op("pool", "affine_select", reads=[ule], writes=[ule], out=ule[:], in_=ule[:], pattern=[[1, 128]],
         compare_op=ALU.is_ge, fill=0.0, base=0, channel_multiplier=-1)
    su = p.sb([128, 128], F32, "su")
    p.op("pool", "memset", writes=[su], ap=su[:], constant=1.0)
    p.op("pool", "affine_select", reads=[su], writes=[su], out=su[:], in_=su[:], pattern=[[1, 128]],
         compare_op=ALU.is_gt, fill=0.0, base=0, channel_multiplier=-1)
    sl = p.sb([128, 128], F32, "sl")
    p.op("pool", "memset", writes=[sl], ap=sl[:], constant=1.0)
    p.op("pool", "affine_select", reads=[sl], writes=[sl], out=sl[:], in_=sl[:], pattern=[[-1, 128]],
         compare_op=ALU.is_gt, fill=0.0, base=0, channel_multiplier=1)
    mask4 = p.sb([128, 512], F32, "mask4")
    for i, m in enumerate([su, ule, su, ule]):
        p.op("pool", "tensor_copy", reads=[m], writes=[mask4], out=mask4[:, i * 128:(i + 1) * 128], in_=m[:])
    ones_col = p.sb([128, 1], F32, "ones_col")
    p.op("pool", "memset", writes=[ones_col], ap=ones_col[:], constant=1.0)
    raw = {nm: p.sb([128, D], F32, "raw_" + nm) for nm in ["R", "LW", "KP", "V", "A", "B"]}
    Gc = p.sb([128, D], F32, "Gc")
    Et = p.sb([128, D], F32, "Et")
    Rt = p.sb([128, D], F32, "Rt")
    At = p.sb([128, D], F32, "At")
    Bt = p.sb([128, D], F32, "Bt")
    Kt = p.sb([128, D], F32, "Kt")
    GL = p.sb([64, NH], F32, "GL")
    GH = 4
    fm = [p.sb([64, 512], F32, "fm") for _ in range(GH)]
    GM = [p.sb([128, 512], F32, "GM") for _ in range(GH)]
    A0 = [p.sb([128, 128], F32, "A0") for _ in range(2)]
    AA = [[p.sb([128, 256], F32, "AA") for _ in range(2)] for _ in range(2)]
    PTb = [[p.sb([128, 128], F32, "PT") for _ in range(2)] for _ in range(GH)]
    NV = p.sb([128, 256], F32, "NV")
    KV = p.sb([64, 256], F32, "KV")
    Xs = p.sb([128, 256], F32, "Xs")
    Us = p.sb([128, 256], F32, "Us")
    S1 = p.sb([64, 256], F32, "S1")
    S2 = p.sb([64, 256], F32, "S2")
    Tst = [p.sb([64, GH * HS], F32, "Tst") for _ in range(NH // GH)]
    for t_ in Tst:
        p.op("pool", "memset", writes=[t_], ap=t_[:], constant=0.0)
    Yt = p.sb([128, D], F32, "Yt")

    def sub(t, a, b, rows=128):
        return T(t.ap[0:rows, a:b], t.name + "_%d" % a)
    sq = [sub(PS[0], 0, 256), sub(PS[0], 256, 512)]
    pp = [sub(PS[1], 0, 128), sub(PS[1], 128, 256)]
    pn = sub(PS[1], 256, 384)
    pg = PS[2]
    pnv = sub(PS[3], 0, 256)
    pY = sub(PS[3], 256, 512)
    pkv = sub(PS[4], 0, 256)
    p3 = sub(PS[4], 256, 512)
    p1 = sub(PS[5], 0, 256)
    p2 = sub(PS[5], 256, 512)

    for ch in range(S // 128):
        rows = slice(ch * 128, (ch + 1) * 128)
        for nm in raw:
            p.dma("sp", raw[nm][:], sc[nm][rows, :], writes=[raw[nm]])
        LW = raw["LW"]
        V = raw["V"]
        for n4 in range(4):
            ps = PS[6 + n4 % 2]
            p.mm(ps, ps[:, :], ule, ule[:], LW, LW[:, n4 * 512:(n4 + 1) * 512])
            evac(p, "act" if n4 % 2 else "dve", ps, ps[:, :], Gc, Gc[:, n4 * 512:(n4 + 1) * 512])
        ps = PS[6]
        for h in range(NH):
            p.mm(ps, ps[0:64, h:h + 1], LW, LW[:, h * 64:(h + 1) * 64], ones_col, ones_col[:])
        p.op("act", "activation", reads=[ps], writes=[GL], out=GL[:], in_=ps[0:64, 0:NH], func=AF.Exp)
        p.op("act", "activation", reads=[Gc], writes=[Et], out=Et[:], in_=Gc[:], func=AF.Exp)
        p.op("dve", "tensor_tensor", reads=[raw["R"], Et], writes=[Rt], out=Rt[:], in0=raw["R"][:], in1=Et[:], op=ALU.mult)
        p.op("act", "activation", reads=[Gc], writes=[Et], out=Et[:], in_=Gc[:], func=AF.Exp, scale=-1.0)
        p.op("dve", "tensor_tensor", reads=[raw["B"], Et], writes=[Bt], out=Bt[:], in0=raw["B"][:], in1=Et[:], op=ALU.mult)
        p.op("pool", "tensor_tensor", reads=[raw["KP"], Et], writes=[Kt], out=Kt[:], in0=raw["KP"][:], in1=Et[:], op=ALU.mult)
        p.op("pool", "tensor_tensor", reads=[Gc, LW], writes=[Gc], out=Gc[:], in0=Gc[:], in1=LW[:], op=ALU.subtract)
        p.op("act", "activation", reads=[Gc], writes=[Et], out=Et[:], in_=Gc[:], func=AF.Exp)
        p.op("dve", "tensor_tensor", reads=[raw["A"], Et], writes=[At], out=At[:], in0=raw["A"][:], in1=Et[:], op=ALU.mult)
        for g in range(NH // GH):
            gcols = slice(g * GH * HS, (g + 1) * GH * HS)
            for j in range(GH):
                h = g * GH + j
                hc = slice(h * HS, (h + 1) * HS)
                jc = slice(j * HS, (j + 1) * HS)
                pst = PS[6 + h % 2]
                for qi, Q in enumerate([At, Rt, Bt, Kt]):
                    p.op("pe", "transpose", reads=[Q, ident], writes=[pst], out=pst[0:64, qi * 128:(qi + 1) * 128], in_=Q[:, hc], identity=ident[:])
                evac(p, "act" if h % 2 else "dve", pst, pst[0:64, :], fm[j], fm[j][:])
                f = fm[j]
                p.mm(pg, pg[:, 0:256], f, f[:, 256:384], f, f[:, 0:256])
                p.mm(pg, pg[:, 256:512], f, f[:, 384:512], f, f[:, 0:256])
                p.op("dve", "tensor_tensor", reads=[pg, mask4], writes=[GM[j]], out=GM[j][:], in0=pg[:], in1=mask4[:], op=ALU.mult)
                p.mm(pn, pn[:], f, f[:, 0:128], f, f[:, 256:384])
                a0 = A0[h % 2]
                p.op("dve", "tensor_tensor", reads=[pn, sl], writes=[a0], out=a0[:], in0=pn[:], in1=sl[:], op=ALU.mult)
                A_t, A_ap = a0, a0[:]
                AT_t, AT_ap = GM[j], GM[j][:, 0:128]
                PT = PTb[j][0]
                p.op("pool", "tensor_tensor", reads=[GM[j], ident], writes=[PT], out=PT[:], in0=GM[j][:, 0:128], in1=ident[:], op=ALU.add)
                for i in range(1, 7):
                    s_ = sq[i % 2]
                    p.mm(s_, s_[:, 0:128], AT_t, AT_ap, A_t, A_ap)
                    if i < 6:
                        p.mm(s_, s_[:, 128:256], A_t, A_ap, AT_t, AT_ap)
                    aa = AA[h % 2][i % 2]
                    w_ = 256 if i < 6 else 128
                    evac(p, "act", s_, s_[:, 0:w_], aa, aa[:, 0:w_])
                    A_t, A_ap = aa, aa[:, 0:128]
                    AT_t, AT_ap = aa, aa[:, 128:256]
                    q_ = pp[i % 2]
                    p.mm(q_, q_[:], A_t, A_ap, PT, PT[:])
                    PTn = PTb[j][i % 2]
                    p.op("dve", "tensor_tensor", reads=[q_, PT], writes=[PTn], out=PTn[:], in0=q_[:], in1=PT[:], op=ALU.add)
                    PT = PTn
                p.mm(pnv, pnv[:, jc], GM[j], GM[j][:, 256:384], V, V[:, hc])
                p.mm(pkv, pkv[0:64, jc], Kt, Kt[:, hc], V, V[:, hc])
            evac(p, "act", pnv, pnv[:], NV, NV[:])
            evac(p, "act", pkv, pkv[0:64, :], KV, KV[:])
            Tg = Tst[g]
            for j in range(GH):
                jc = slice(j * HS, (j + 1) * HS)
                p.mm(p1, p1[:, jc], fm[j], fm[j][:, 0:128], Tg, Tg[:, jc])
            p.op("dve", "tensor_tensor", reads=[p1, NV], writes=[Xs], out=Xs[:], in0=p1[:], in1=NV[:], op=ALU.add)
            for j in range(GH):
                jc = slice(j * HS, (j + 1) * HS)
                p.mm(p2, p2[:, jc], PTb[j][0], PTb[j][0][:], Xs, Xs[:, jc])
            evac(p, "act", p2, p2[:], Us, Us[:])
            for j in range(GH):
                h = g * GH + j
                hc = slice(h * HS, (h + 1) * HS)
                jc = slice(j * HS, (j + 1) * HS)
                p.mm(pY, pY[:, jc], fm[j], fm[j][:, 128:256], Tg, Tg[:, jc], start=True, stop=False)
                p.mm(pY, pY[:, jc], GM[j], GM[j][:, 128:256], Us, Us[:, jc], start=False, stop=False)
                p.mm(pY, pY[:, jc], GM[j], GM[j][:, 384:512], V, V[:, hc], start=False, stop=True)
            evac(p, "act", pY, pY[:], Yt, Yt[:, gcols])
            for j in range(GH):
                h = g * GH + j
                hc = slice(h * HS, (h + 1) * HS)
                jc = slice(j * HS, (j + 1) * HS)
                p.mm(p3, p3[0:64, jc], Bt, Bt[:, hc], Us, Us[:, jc])
            p.op("pool", "tensor_tensor", reads=[Tg, KV], writes=[S1], out=S1[:], in0=Tg[:], in1=KV[:], op=ALU.add)
            p.op("dve", "tensor_tensor", reads=[p3, S1], writes=[S2], out=S2[:], in0=p3[0:64, :], in1=S1[:], op=ALU.add)
            p.op("dve", "tensor_tensor", reads=[S2, GL], writes=[Tg], out=Tg[:].rearrange("p (h v) -> p h v", v=HS),
                 in0=S2[:].rearrange("p (h v) -> p h v", v=HS),
                 in1=GL[:, g * GH:(g + 1) * GH].unsqueeze(2).to_broadcast([64, GH, HS]), op=ALU.mult)
        p.dma("pool", sc["Y"][rows, :], Yt[:], reads=[Yt])
    p.pop()


def out_proj_phase(c, X, S, make_tile, Wo, extra_alloc=None):
    p = c.p
    TG = min(512, S)
    NT = TG // 128
    wl = WLoader(c)
    oT = p.sb([128, KT, TG], BF16, "oT")
    mo = p.sb([128, D], F32, "mo")
    xres = [p.sb([128, D], F32, "xres") for _ in range(NT)]
    ot = [p.sb([128, 512], F32, "ot") for _ in range(2)]
    for t0 in range(0, S, TG):
        for ti in range(NT):
            r0 = t0 + ti * 128
            make_tile(r0, mo)
            to_fm(c, mo, lambda kt: mo[:, kt * 128:(kt + 1) * 128], KT, oT, ti * 128)
            p.dma("sp", xres[ti][:], X[r0:r0 + 128, :], writes=[xres[ti]])
        for nch in range(4):
            wb = wl.load(Wo, 0, D, nch * 512, 512)
            for ti in range(NT):
                ps = c.PS[ti]
                for kt in range(KT):
                    p.mm(ps, ps[:, :], oT, oT[:, kt, ti * 128:(ti + 1) * 128], wb, wb[:, kt, :], start=(kt == 0), stop=(kt == KT - 1))
                o = ot[ti % 2]
                p.op("dve", "tensor_tensor", reads=[ps, xres[ti]], writes=[o], out=o[:], in0=ps[:], in1=xres[ti][:, nch * 512:(nch + 1) * 512], op=ALU.add)
                p.dma("pool", X[t0 + ti * 128:t0 + (ti + 1) * 128, nch * 512:(nch + 1) * 512], o[:], reads=[o])


def rwkv_post(c, X, S, W, sc):
    p = c.p
    p.push()
    Yt = p.sb([128, D], F32, "Yt")
    Bo = p.sb([128, D], F32, "Bo")
    Gt = p.sb([128, D], F32, "Gt")
    sq = p.sb([128, D], F32, "sq")
    lg = p.sb([128, D], F32, "lg")
    lb = p.sb([128, D], F32, "lb")
    bcast_row(p, lg, W["lnx_g"], D)
    bcast_row(p, lb, W["lnx_b"], D)
    st = p.sb([128, 64], F32, "st")

    def v3(t):
        return t[:].rearrange("p (h k) -> p h k", k=HS)

    def make_tile(r0, mo):
        rows = slice(r0, r0 + 128)
        p.dma("sp", Yt[:], sc["Y"][rows, :], writes=[Yt])
        p.dma("sp", Bo[:], sc["BON"][rows, :], writes=[Bo])
        p.dma("sp", Gt[:], sc["G"][rows, :], writes=[Gt])
        p.op("dve", "reduce_sum", reads=[Yt], writes=[st], out=st[:, 0:32], in_=v3(Yt), axis=AX.X)
        p.op("dve", "tensor_scalar", reads=[st], writes=[st], out=st[:, 0:32], in0=st[:, 0:32], scalar1=1.0 / HS, scalar2=None, op0=ALU.mult)
        p.op("dve", "tensor_tensor", reads=[Yt, st], writes=[Yt], out=v3(Yt), in0=v3(Yt), in1=st[:, 0:32].unsqueeze(2).to_broadcast([128, NH, HS]), op=ALU.subtract)
        p.op("act", "activation", reads=[Yt], writes=[sq], out=sq[:], in_=Yt[:], func=AF.Square)
        p.op("dve", "reduce_sum", reads=[sq], writes=[st], out=st[:, 32:64], in_=v3(sq), axis=AX.X)
        p.op("dve", "tensor_scalar", reads=[st], writes=[st], out=st[:, 32:64], in0=st[:, 32:64], scalar1=1.0 / HS, scalar2=64e-5, op0=ALU.mult, op1=ALU.add)
        p.op("act", "activation", reads=[st], writes=[st], out=st[:, 32:64], in_=st[:, 32:64], func=AF.Sqrt)
        p.op("dve", "reciprocal", reads=[st], writes=[st], out=st[:, 0:32], in_=st[:, 32:64])
        p.op("dve", "tensor_tensor", reads=[Yt, st], writes=[Yt], out=v3(Yt), in0=v3(Yt), in1=st[:, 0:32].unsqueeze(2).to_broadcast([128, NH, HS]), op=ALU.mult)
        p.op("pool", "tensor_tensor", reads=[Yt, lg], writes=[Yt], out=Yt[:], in0=Yt[:], in1=lg[:], op=ALU.mult)
        p.op("pool", "tensor_tensor", reads=[Yt, lb], writes=[Yt], out=Yt[:], in0=Yt[:], in1=lb[:], op=ALU.add)
        p.op("dve", "tensor_tensor", reads=[Yt, Bo], writes=[Yt], out=Yt[:], in0=Yt[:], in1=Bo[:], op=ALU.add)
        p.op("dve", "tensor_tensor", reads=[Yt, Gt], writes=[mo], out=mo[:], in0=Yt[:], in1=Gt[:], op=ALU.mult)

    out_proj_phase(c, X, S, make_tile, W["w_o"])
    p.pop()


RW_SC = ["R", "LW", "KP", "V", "A", "B", "G", "BON", "Y", "VF"]


MH = 4
MDK = 256
MDV = 512


def mlstm_layer(c, X, S, W, sc):
    p = c.p
    PS = c.PS
    ident = c.ident
    TG = min(512, S)
    NT = TG // 128
    Win = W["w_in"]
    QK, Vb, SO, MO = sc["QK"], sc["Vb"], sc["SO"], sc["MO"]
    p.push()
    GI = p.sb([4, S], F32, "GI")
    GF = p.sb([4, S], F32, "GF")
    p.push()
    wl = WLoader(c)
    g_bc = p.sb([128, D], F32, "g_bc")
    bcast_row(p, g_bc, W["mix_g"], D)
    xt = p.sb([128, D], F32, "xt")
    hn = p.sb([128, D], F32, "hn")
    ss = p.sb([128, 2], F32, "ss")
    hT = p.sb([128, KT, TG], BF16, "hT")
    qk_sb = [p.sb([128, TG], BF16, "qk_sb") for _ in range(2)]
    vb_sb = [p.sb([128, 512], BF16, "vb_sb") for _ in range(2)]
    so_sb = [p.sb([128, 512], F32, "so_sb") for _ in range(2)]
    for t0 in range(0, S, TG):
        for ti in range(NT):
            p.dma("sp", xt[:], X[t0 + ti * 128:t0 + (ti + 1) * 128, :], writes=[xt])
            rmsnorm_tile(p, xt, xt[:], g_bc, hn, hn[:], hn, ss)
            to_fm(c, hn, lambda kt: hn[:, kt * 128:(kt + 1) * 128], KT, hT, ti * 128)
        for nb in range(4):
            wb = wl.load(Win, 0, D, nb * 512, 512)
            for j in range(4):
                ps = PS[j % 2]
                for kt in range(KT):
                    p.mm(ps, ps[:, 0:TG], wb, wb[:, kt, j * 128:(j + 1) * 128], hT, hT[:, kt, :], start=(kt == 0), stop=(kt == KT - 1))
                o = qk_sb[j % 2]
                p.op("act", "activation", reads=[ps], writes=[o], out=o[:, 0:TG], in_=ps[:, 0:TG], func=AF.Copy, scale=(0.0625 if nb < 2 else 1.0))
                r0 = nb * 512 + j * 128
                p.dma("pool", QK[r0:r0 + 128, t0:t0 + TG], o[:, 0:TG], reads=[o])
        for nb in range(8):
            wb = wl.load(Win, 0, D, 2048 + nb * 512, 512)
            for ti in range(NT):
                ps = PS[2 + ti % 2]
                for kt in range(KT):
                    p.mm(ps, ps[:, :], hT, hT[:, kt, ti * 128:(ti + 1) * 128], wb, wb[:, kt, :], start=(kt == 0), stop=(kt == KT - 1))
                rows = slice(t0 + ti * 128, t0 + (ti + 1) * 128)
                if nb < 4:
                    o = vb_sb[ti % 2]
                    p.op("dve", "tensor_copy", reads=[ps], writes=[o], out=o[:], in_=ps[:])
                    p.dma("pool", Vb[rows, nb * 512:(nb + 1) * 512], o[:], reads=[o])
                else:
                    o = so_sb[ti % 2]
                    p.op("act", "activation", reads=[ps], writes=[o], out=o[:], in_=ps[:], func=AF.Sigmoid)
                    p.dma("pool", SO[rows, (nb - 4) * 512:(nb - 3) * 512], o[:], reads=[o])
        wb = wl.load(Win, 0, D, 6144, 8)
        for gi, (G_, ps) in enumerate([(GI, PS[4]), (GF, PS[5])]):
            for kt in range(KT):
                p.mm(ps, ps[0:4, 0:TG], wb, wb[:, kt, gi * 4:(gi + 1) * 4], hT, hT[:, kt, :], start=(kt == 0), stop=(kt == KT - 1))
            p.op("act", "copy", reads=[ps], writes=[G_], out=G_[:, t0:t0 + TG], in_=ps[0:4, 0:TG])
    p.pop()
    NB = S // 128
    bif = p.sb([4, 2], F32, "bif")
    p.dma("sp", bif[:], W["b_if"].rearrange("a h -> h a"), writes=[bif], allow_slow_non_contiguous=True)
    SCN = min(1024, S)
    onesr = p.sb([4, SCN], F32, "onesr")
    zeror = p.sb([4, SCN], F32, "zeror")
    p.op("pool", "memset", writes=[onesr], ap=onesr[:], constant=1.0)
    p.op("pool", "memset", writes=[zeror], ap=zeror[:], constant=0.0)
    Fr = p.sb([4, S], F32, "Fr")
    for G_, col in ((GI, 0), (GF, 1)):
        p.op("dve", "tensor_scalar", reads=[G_, bif], writes=[G_], out=G_[:], in0=G_[:], scalar1=bif[:, col:col + 1], scalar2=None, op0=ALU.add)
        p.op("act", "activation", reads=[G_], writes=[G_], out=G_[:], in_=G_[:], func=AF.Tanh, scale=1.0 / 15.0)
        p.op("dve", "tensor_scalar", reads=[G_], writes=[G_], out=G_[:], in0=G_[:], scalar1=15.0, scalar2=None, op0=ALU.mult)
    p.op("act", "activation", reads=[GF], writes=[GF], out=GF[:], in_=GF[:], func=AF.Exp, scale=-1.0)
    p.op("dve", "tensor_scalar", reads=[GF], writes=[GF], out=GF[:], in0=GF[:], scalar1=1.0, scalar2=None, op0=ALU.add)
    p.op("act", "activation", reads=[GF], writes=[GF], out=GF[:], in_=GF[:], func=AF.Ln)
    p.op("dve", "tensor_scalar", reads=[GF], writes=[GF], out=GF[:], in0=GF[:], scalar1=-1.0, scalar2=None, op0=ALU.mult)
    for s0 in range(0, S, SCN):
        init = 0.0 if s0 == 0 else Fr[:, s0 - 1:s0]
        p.op("dve", "tensor_tensor_scan", reads=[GF, onesr, Fr], writes=[Fr], out=Fr[:, s0:s0 + SCN], data0=onesr[:, 0:SCN], data1=GF[:, s0:s0 + SCN],
             initial=init, op0=ALU.mult, op1=ALU.add)
    Ur = GI
    p.op("dve", "tensor_tensor", reads=[GI, Fr], writes=[Ur], out=Ur[:], in0=GI[:], in1=Fr[:], op=ALU.subtract)
    Gr = GF
    for s0 in range(0, S, SCN):
        init = 0.0 if s0 == 0 else Gr[:, s0 - 1:s0]
        p.op("dve", "tensor_tensor_scan", reads=[Ur, zeror, Gr], writes=[Gr], out=Gr[:, s0:s0 + SCN], data0=Ur[:, s0:s0 + SCN], data1=zeror[:, 0:SCN],
             initial=init, op0=ALU.max, op1=ALU.add)
    NM = Fr
    p.op("dve", "tensor_tensor", reads=[Fr, Gr], writes=[NM], out=NM[:], in0=Fr[:], in1=Gr[:], op=ALU.add)
    p.op("dve", "tensor_scalar", reads=[NM], writes=[NM], out=NM[:], in0=NM[:], scalar1=-1.0, scalar2=None, op0=ALU.mult)
    p.op("dve", "tensor_scalar", reads=[Gr], writes=[Gr], out=Gr[:], in0=Gr[:], scalar1=-1.0, scalar2=None, op0=ALU.mult)
    Ucol = p.sb([128, NB, 4], F32, "Ucol")
    NMcol = p.sb([128, NB, 4], F32, "NMcol")
    for src, dst in ((Ur, Ucol), (NM, NMcol)):
        for b0 in range(0, NB, 64):
            nb_ = min(64, NB - b0)
            ps = PS[6]
            for b in range(nb_):
                p.op("pe", "transpose", reads=[src, ident], writes=[ps], out=ps[:, b * 4:(b + 1) * 4], in_=src[:, (b0 + b) * 128:(b0 + b + 1) * 128], identity=ident[0:4, 0:4])
            p.op("dve", "tensor_copy", reads=[ps], writes=[dst], out=dst[:, b0:b0 + nb_, :], in_=ps[:, 0:nb_ * 4].rearrange("p (b h) -> p b h", h=4))
    p.op("act", "activation", reads=[NMcol], writes=[NMcol], out=NMcol[:], in_=NMcol[:], func=AF.Exp)
    sel = p.sb([4, 4, 128], F32, "sel")
    p.op("pool", "memset", writes=[sel], ap=sel[:], constant=1.0)
    p.op("pool", "affine_select", reads=[sel], writes=[sel], out=sel[:], in_=sel[:], pattern=[[-1, 4], [0, 128]],
         compare_op=ALU.is_equal, fill=0.0, base=0, channel_multiplier=1)
    TB = min(512, S)
    NTS = TB // 128
    cms = p.sb([128, NTS, TB], F32, "cms")
    p.op("pool", "memset", writes=[cms], ap=cms[:], constant=1.0)
    for r_ in range(NTS):
        p.op("pool", "affine_select", reads=[cms], writes=[cms], out=cms[:, r_, :], in_=cms[:, r_, :], pattern=[[1, TB]], compare_op=ALU.is_ge,
             fill=0.0, base=-128 * r_, channel_multiplier=-1)
    onesb = p.sb([128, 1], BF16, "onesb")
    p.op("pool", "memset", writes=[onesb], ap=onesb[:], constant=1.0)
    qT = p.sb([128, 2, TB], BF16, "qT")
    kT = [p.sb([128, 2, 128], BF16, "kT") for _ in range(2)]
    Vs = [p.sb([128, MDV], BF16, "Vs") for _ in range(2)]
    ngb = p.sb([128, TB], F32, "ngb")
    Ex = [p.sb([128, TB], F32, "Ex") for _ in range(2)]
    Pb = [p.sb([128, TB], BF16, "Pb") for _ in range(2)]
    ng_w = p.sb([128, MDV], F32, "ng_w")
    hr = p.sb([128, MDV], F32, "hr")
    sqh = p.sb([128, MDV], F32, "sqh")
    sot = p.sb([128, MDV], F32, "sot")
    dn = p.sb([128, 8], F32, "dn")
    ssd = p.sb([128, 2], F32, "ssd")
    pnum = [PS[i] for i in range(4)]
    pden = PS[4]
    pdt = PS[7]
    drow = p.sb([1, TB], F32, "drow")
    for h in range(MH):
        bcast_row(p, ng_w, W["norm_g"][h * MDV:(h + 1) * MDV], MDV)
        for tb in range(S // TB):
            tcols = slice(tb * TB, (tb + 1) * TB)
            p.dma("sp", qT[:], QK[h * MDK:(h + 1) * MDK, tcols].rearrange("(kt p) t -> p kt t", p=128), writes=[qT])
            ps = PS[7]
            p.mm(ps, ps[:, 0:TB], sel, sel[:, h, :], Gr, Gr[:, tcols])
            p.op("act", "copy", reads=[ps], writes=[ngb], out=ngb[:], in_=ps[:, 0:TB])
            nsb = (tb + 1) * NTS
            for sb in range(nsb):
                i2 = sb % 2
                p.dma("sp", kT[i2][:], QK[1024 + h * MDK:1024 + (h + 1) * MDK, sb * 128:(sb + 1) * 128].rearrange("(kt p) t -> p kt t", p=128), writes=[kT[i2]])
                p.dma("sp", Vs[i2][:], Vb[sb * 128:(sb + 1) * 128, h * MDV:(h + 1) * MDV], writes=[Vs[i2]])
                pst = PS[5 + i2]
                for kt in range(2):
                    p.mm(pst, pst[:, 0:TB], kT[i2], kT[i2][:, kt, :], qT, qT[:, kt, :], start=(kt == 0), stop=(kt == 1))
                e = Ex[i2]
                p.op("dve", "tensor_scalar", reads=[ngb, Ucol], writes=[e], out=e[:], in0=ngb[:], scalar1=Ucol[:, sb, h:h + 1], scalar2=0.0, op0=ALU.add, op1=ALU.min)
                p.op("act", "activation", reads=[e], writes=[e], out=e[:], in_=e[:], func=AF.Exp)
                r = sb - tb * NTS
                if r >= 0:
                    p.op("pool", "tensor_tensor", reads=[e, cms], writes=[e], out=e[:], in0=e[:], in1=cms[:, r, :], op=ALU.mult)
                pb = Pb[i2]
                p.op("dve", "tensor_tensor", reads=[pst, e], writes=[pb], out=pb[:], in0=pst[:, 0:TB], in1=e[:], op=ALU.mult)
                p.mm(pden, pden[0:1, 0:TB], onesb, onesb[:], pb, pb[:], start=(sb == 0), stop=(sb == nsb - 1))
                for ts in range(max(r, 0), NTS):
                    last = (sb == tb * NTS + ts)
                    p.mm(pnum[ts], pnum[ts][:, :], pb, pb[:, ts * 128:(ts + 1) * 128], Vs[i2], Vs[i2][:], start=(sb == 0), stop=last)
            p.op("act", "activation", reads=[pden], writes=[drow], out=drow[:, 0:TB], in_=pden[0:1, 0:TB], func=AF.Abs)
            for ts in range(NTS):
                p.op("pe", "transpose", reads=[drow, ident], writes=[pdt], out=pdt[:, ts:ts + 1], in_=drow[0:1, ts * 128:(ts + 1) * 128], identity=ident[0:1, 0:1])
            for ts in range(NTS):
                blk = tb * NTS + ts
                rows = slice(blk * 128, (blk + 1) * 128)
                p.op("act", "copy", reads=[pdt], writes=[dn], out=dn[:, 0:1], in_=pdt[:, ts:ts + 1])
                p.op("dve", "tensor_tensor", reads=[dn, NMcol], writes=[dn], out=dn[:, 1:2], in0=dn[:, 0:1], in1=NMcol[:, blk, h:h + 1], op=ALU.max)
                p.op("dve", "reciprocal", reads=[dn], writes=[dn], out=dn[:, 2:3], in_=dn[:, 1:2])
                p.op("dve", "tensor_scalar", reads=[pnum[ts], dn], writes=[hr], out=hr[:], in0=pnum[ts][:], scalar1=dn[:, 2:3], scalar2=None, op0=ALU.mult)
                p.dma("sp", sot[:], SO[rows, h * MDV:(h + 1) * MDV], writes=[sot])
                rmsnorm_tile(p, hr, hr[:], ng_w, hr, hr[:], sqh, ssd, d=MDV)
                p.op("dve", "tensor_tensor", reads=[hr, sot], writes=[sqh], out=sqh[:], in0=hr[:], in1=sot[:], op=ALU.mult)
                p.dma("pool", MO[rows, h * MDV:(h + 1) * MDV], sqh[:], reads=[sqh])
    p.pop()
    p.push()
    def make_tile(r0, mo):
        p.dma("sp", mo[:], MO[r0:r0 + 128, :], writes=[mo])
    out_proj_phase(c, X, S, make_tile, W["w_o"])
    p.pop()


def dn_ss(p, dn):
    return T(dn.ap[:, 4:6], "dnss")


DH = 128
NHD = 16
NEG = -1.0e30


def dsa_layer(c, X, S, W, sc, dbg=None):
    p = c.p
    PS = c.PS
    ident = c.ident
    TG = min(512, S)
    NT = TG // 128
    NB = S // 128
    Win = W["w_in"]
    QT, QIT, MO = sc["QT"], sc["QIT"], sc["MO"]
    p.push()
    kT = p.sb([128, S], BF16, "kT")
    kiT = p.sb([64, S], BF16, "kiT")
    Vs = p.sb([128, NB, 132], BF16, "Vs")
    wis = p.sb([128, NB, 16], F32, "wis")
    p.op("pool", "memset", writes=[Vs], ap=Vs[:, :, 128:129], constant=1.0)
    p.push()
    wl = WLoader(c)
    g_bc = p.sb([128, D], F32, "g_bc")
    bcast_row(p, g_bc, W["mix_g"], D)
    qg = p.sb([128, DH], F32, "qg")
    kg = p.sb([128, DH], F32, "kg")
    bcast_row(p, qg, W["q_norm_g"], DH)
    bcast_row(p, kg, W["k_norm_g"], DH)
    xt = p.sb([128, D], F32, "xt")
    hn = p.sb([128, D], F32, "hn")
    ss = p.sb([128, 2], F32, "ss")
    hT = p.sb([128, KT, TG], BF16, "hT")
    qs = p.sb([128, 512], F32, "qs")
    qsq = p.sb([128, 512], F32, "qsq")
    st4 = p.sb([128, 8], F32, "st4")
    qTg = p.sb([128, NHD, TG], BF16, "qTg")
    qi_sb = [p.sb([128, TG], BF16, "qi_sb") for _ in range(2)]
    for t0 in range(0, S, TG):
        for ti in range(NT):
            p.dma("sp", xt[:], X[t0 + ti * 128:t0 + (ti + 1) * 128, :], writes=[xt])
            rmsnorm_tile(p, xt, xt[:], g_bc, hn, hn[:], hn, ss)
            to_fm(c, hn, lambda kt: hn[:, kt * 128:(kt + 1) * 128], KT, hT, ti * 128)
        for nb in range(4):
            wb = wl.load(Win, 0, D, nb * 512, 512)
            for ti in range(NT):
                ps = PS[ti % 2]
                for kt in range(KT):
                    p.mm(ps, ps[:, :], hT, hT[:, kt, ti * 128:(ti + 1) * 128], wb, wb[:, kt, :], start=(kt == 0), stop=(kt == KT - 1))
                p.op("act", "copy", reads=[ps], writes=[qs], out=qs[:], in_=ps[:])
                p.op("act", "activation", reads=[qs], writes=[qsq], out=qsq[:], in_=qs[:], func=AF.Square)
                p.op("dve", "reduce_sum", reads=[qsq], writes=[st4], out=st4[:, 0:4], in_=qsq[:].rearrange("p (h d) -> p h d", d=DH), axis=AX.X)
                p.op("dve", "tensor_scalar", reads=[st4], writes=[st4], out=st4[:, 0:4], in0=st4[:, 0:4], scalar1=1.0 / DH, scalar2=EPS, op0=ALU.mult, op1=ALU.add)
                p.op("act", "activation", reads=[st4], writes=[st4], out=st4[:, 0:4], in_=st4[:, 0:4], func=AF.Sqrt)
                p.op("dve", "reciprocal", reads=[st4], writes=[st4], out=st4[:, 4:8], in_=st4[:, 0:4])
                p.op("dve", "tensor_scalar", reads=[st4], writes=[st4], out=st4[:, 4:8], in0=st4[:, 4:8], scalar1=DH ** -0.5, scalar2=None, op0=ALU.mult)
                q3 = qs[:].rearrange("p (h d) -> p h d", d=DH)
                p.op("dve", "tensor_tensor", reads=[qs, st4], writes=[qs], out=q3, in0=q3, in1=st4[:, 4:8].unsqueeze(2).to_broadcast([128, 4, DH]), op=ALU.mult)
                p.op("dve", "tensor_tensor", reads=[qs, qg], writes=[qs], out=q3, in0=q3, in1=qg[:].unsqueeze(1).to_broadcast([128, 4, DH]), op=ALU.mult)
                pst = PS[6 + ti % 2]
                for j in range(4):
                    p.op("pe", "transpose", reads=[qs, ident], writes=[pst], out=pst[:, j * 128:(j + 1) * 128], in_=qs[:, j * 128:(j + 1) * 128], identity=ident[:])
                p.op("dve", "tensor_copy", reads=[pst], writes=[qTg], out=qTg[:, nb * 4:(nb + 1) * 4, ti * 128:(ti + 1) * 128],
                     in_=pst[:].rearrange("p (a b) -> p a b", a=4))
        for hh in range(NHD):
            p.dma("pool", QT[hh * 128:(hh + 1) * 128, t0:t0 + TG], qTg[:, hh, :], reads=[qTg])
        wb = wl.load(Win, 0, D, 2048, 256)
        for ti in range(NT):
            blk = (t0 // 128) + ti
            ps = PS[2 + ti % 2]
            for kt in range(KT):
                p.mm(ps, ps[:, 0:256], hT, hT[:, kt, ti * 128:(ti + 1) * 128], wb, wb[:, kt, 0:256], start=(kt == 0), stop=(kt == KT - 1))
            p.op("act", "copy", reads=[ps], writes=[Vs], out=Vs[:, blk, 0:128], in_=ps[:, 128:256])
            p.op("act", "copy", reads=[ps], writes=[qs], out=qs[:, 0:128], in_=ps[:, 0:128])
            rmsnorm_tile(p, qs, qs[:, 0:128], kg, qs, qs[:, 0:128], qsq, ss, d=DH)
            pst = PS[6 + ti % 2]
            p.op("pe", "transpose", reads=[qs, ident], writes=[pst], out=pst[:, 0:128], in_=qs[:, 0:128], identity=ident[:])
            p.op("dve", "tensor_copy", reads=[pst], writes=[kT], out=kT[:, blk * 128:(blk + 1) * 128], in_=pst[:, 0:128])
        for nb in range(2):
            wb = wl.load(Win, 0, D, 2304 + nb * 512, 512)
            for j in range(4):
                ps = PS[j % 2]
                for kt in range(KT):
                    p.mm(ps, ps[:, 0:TG], wb, wb[:, kt, j * 128:(j + 1) * 128], hT, hT[:, kt, :], start=(kt == 0), stop=(kt == KT - 1))
                o = qi_sb[j % 2]
                p.op("act", "copy", reads=[ps], writes=[o], out=o[:, 0:TG], in_=ps[:, 0:TG])
                r0 = nb * 512 + j * 128
                p.dma("pool", QIT[r0:r0 + 128, t0:t0 + TG], o[:, 0:TG], reads=[o])
        wb = wl.load(Win, 0, D, 3328, 80)
        ps = PS[4]
        for kt in range(KT):
            p.mm(ps, ps[0:64, 0:TG], wb, wb[:, kt, 0:64], hT, hT[:, kt, :], start=(kt == 0), stop=(kt == KT - 1))
        p.op("act", "copy", reads=[ps], writes=[kiT], out=kiT[:, t0:t0 + TG], in_=ps[0:64, 0:TG])
        for ti in range(NT):
            blk = (t0 // 128) + ti
            ps = PS[5]
            for kt in range(KT):
                p.mm(ps, ps[:, 0:16], hT, hT[:, kt, ti * 128:(ti + 1) * 128], wb, wb[:, kt, 64:80], start=(kt == 0), stop=(kt == KT - 1))
            p.op("act", "activation", reads=[ps], writes=[wis], out=wis[:, blk, :], in_=ps[:, 0:16], func=AF.Copy, scale=1.0 / 32.0)
    p.pop()
    p.push()
    NBt = p.sb([128, 2, NHD, 128], F32, "NBt")
    CB = p.sb([128, NHD], F32, "CB")
    p.dma("sp", NBt[:], W["t5_bt"], writes=[NBt])
    p.dma("sp", CB[:], W["t5_cb"], writes=[CB])
    for k2 in range(2):
        p.op("dve", "tensor_tensor", reads=[NBt, CB], writes=[NBt], out=NBt[:, k2, :, :], in0=NBt[:, k2, :, :],
             in1=CB[:].unsqueeze(2).to_broadcast([128, NHD, 128]), op=ALU.subtract)
    identb = p.sb([128, 128], BF16, "identb")
    p.op("pool", "tensor_copy", reads=[ident], writes=[identb], out=identb[:], in_=ident[:])
    SC = p.sb([128, S], F32, "SC")
    CM = p.sb([128, 128], F32, "CM")
    p.op("pool", "memset", writes=[CM], ap=CM[:], constant=0.0)
    p.op("pool", "affine_select", reads=[CM], writes=[CM], out=CM[:], in_=CM[:], pattern=[[-1, 128]], compare_op=ALU.is_ge,
         fill=NEG, base=0, channel_multiplier=1)
    Wk = p.sb([128, S], F32, "Wk")
    m8 = p.sb([128, 16], F32, "m8")
    Mk = [p.sb([128, 512], BF16, "Mk") for _ in range(2)]
    MT = p.sb([128, NB, 128], BF16, "MT")
    qTb = p.sb([128, NHD, 128], BF16, "qTb")
    qiTb = p.sb([64, NHD, 128], BF16, "qiTb")
    rl = [p.sb([128, 512], F32, "rl") for _ in range(2)]
    Lf = [p.sb([128, 512], F32, "Lf") for _ in range(2)]
    Pe = [p.sb([128, 512], BF16, "Pe") for _ in range(2)]
    Pm = [p.sb([128, 512], BF16, "Pm") for _ in range(2)]
    mo = p.sb([128, D], F32, "mo")
    rd = p.sb([128, 16], F32, "rd")
    acc = [PS[i] for i in range(6)]
    for qb in range(NB):
        nkb = qb + 1
        ncols = nkb * 128
        tcols = slice(qb * 128, (qb + 1) * 128)
        p.dma("sp", qTb[:], QT[:, tcols].rearrange("(h d) t -> d h t", d=128), writes=[qTb])
        p.dma("sp", qiTb[:], QIT[:, tcols].rearrange("(h d) t -> d h t", d=64), writes=[qiTb])
        for c0 in range(0, ncols, 512):
            cw = min(512, ncols - c0)
            for hh in range(NHD):
                ps = PS[6 + hh % 2]
                p.mm(ps, ps[:, 0:cw], qiTb, qiTb[:, hh, :], kiT, kiT[:, c0:c0 + cw])
                r_ = rl[hh % 2]
                p.op("act", "activation", reads=[ps], writes=[r_], out=r_[:, 0:cw], in_=ps[:, 0:cw], func=AF.Relu)
                if hh == 0:
                    p.op("dve", "tensor_scalar", reads=[r_, wis], writes=[SC], out=SC[:, c0:c0 + cw], in0=r_[:, 0:cw], scalar1=wis[:, qb, 0:1], scalar2=None, op0=ALU.mult)
                else:
                    p.op("dve", "scalar_tensor_tensor", reads=[r_, wis, SC], writes=[SC], out=SC[:, c0:c0 + cw], in0=r_[:, 0:cw], scalar=wis[:, qb, hh:hh + 1],
                         in1=SC[:, c0:c0 + cw], op0=ALU.mult, op1=ALU.add)
        p.op("pool", "tensor_tensor", reads=[SC, CM], writes=[SC], out=SC[:, qb * 128:(qb + 1) * 128], in0=SC[:, qb * 128:(qb + 1) * 128],
             in1=CM[:], op=ALU.add)
        NSEL = min(256, S // 4)
        if ncols > NSEL:
            src = SC
            for r in range(NSEL // 8):
                p.op("dve", "max", reads=[src], writes=[m8], out=m8[:, 0:8], in_=src[:, 0:ncols])
                if r < NSEL // 8 - 1:
                    p.op("dve", "match_replace", reads=[m8, src], writes=[Wk], out=Wk[:, 0:ncols], in_to_replace=m8[:, 0:8], in_values=src[:, 0:ncols], imm_value=NEG)
                    src = Wk
            p.op("dve", "tensor_reduce", reads=[m8], writes=[m8], out=m8[:, 8:9], in_=m8[:, 0:8], axis=AX.X, op=ALU.min)
        else:
            p.op("pool", "memset", writes=[m8], ap=m8[:, 8:9], constant=-1.0e29)
        if dbg is not None and qb == 1:
            p.dma("pool", dbg["SC"][:, :], SC[:, 0:256], reads=[SC])
            p.dma("pool", dbg["m8"][:, :], m8[:], reads=[m8])
        for c0 in range(0, ncols, 512):
            cw = min(512, ncols - c0)
            i2 = (c0 // 512) % 2
            p.op("dve", "tensor_scalar", reads=[SC, m8], writes=[Mk[i2]], out=Mk[i2][:, 0:cw], in0=SC[:, c0:c0 + cw], scalar1=m8[:, 8:9], scalar2=None, op0=ALU.is_ge)
            pst = PS[6 + i2]
            pv = pst[:].bitcast(BF16)
            for j in range(cw // 128):
                p.op("pe", "transpose", reads=[Mk[i2], identb], writes=[pst], out=pv[:, j * 128:(j + 1) * 128], in_=Mk[i2][:, j * 128:(j + 1) * 128], identity=identb[:])
            p.op("act", "copy", reads=[pst], writes=[MT], out=MT[:, c0 // 128:c0 // 128 + cw // 128, :], in_=pv[:, 0:cw].rearrange("p (a b) -> p a b", b=128))
        if dbg is not None and qb == 1:
            p.dma("pool", dbg["MT"][:, :, :], MT[:, 0:2, :], reads=[MT])
        ixc = 0
        for (h0, h1) in ((0, 6), (6, 12), (12, 16)):
            for kb in range(nkb):
                near = (qb - kb) <= 1
                for c0 in range(h0, h1, 4):
                    c1 = min(c0 + 4, h1)
                    nhc = c1 - c0
                    w_ = nhc * 128
                    ix = ixc % 2
                    ixc += 1
                    ps = PS[6 + ix]
                    p.mm(ps, ps[:, 0:w_], kT, kT[:, kb * 128:(kb + 1) * 128], qTb, qTb[:, c0:c1, :].rearrange("p a b -> p (a b)"))
                    pe_ = Pe[ix]
                    if near:
                        p.op("dve", "tensor_tensor", reads=[ps, NBt], writes=[Lf[ix]], out=Lf[ix][:, 0:w_], in0=ps[:, 0:w_],
                             in1=NBt[:, qb - kb, c0:c1, :].rearrange("p a b -> p (a b)"), op=ALU.add)
                        p.op("act", "activation", reads=[Lf[ix]], writes=[pe_], out=pe_[:, 0:w_], in_=Lf[ix][:, 0:w_], func=AF.Exp)
                    else:
                        p.op("act", "activation", reads=[ps], writes=[pe_], out=pe_[:, 0:w_], in_=ps[:, 0:w_], func=AF.Exp)
                    pm_ = Pm[ix]
                    p.op("pool", "tensor_tensor", reads=[pe_, MT], writes=[pm_], out=pm_[:, 0:w_].rearrange("p (a b) -> p a b", a=nhc),
                         in0=pe_[:, 0:w_].rearrange("p (a b) -> p a b", a=nhc), in1=MT[:, kb, :].unsqueeze(1).to_broadcast([128, nhc, 128]), op=ALU.mult)
                    for j in range(nhc):
                        a_ = acc[c0 + j - h0]
                        p.mm(a_, a_[:, 0:129], pm_, pm_[:, j * 128:(j + 1) * 128], Vs, Vs[:, kb, 0:129], start=(kb == 0), stop=(kb == nkb - 1))
            for hh in range(h0, h1):
                a_ = acc[hh - h0]
                p.op("dve", "reciprocal", reads=[a_], writes=[rd], out=rd[:, hh:hh + 1], in_=a_[:, 128:129])
                p.op("dve", "tensor_scalar", reads=[a_, rd], writes=[mo], out=mo[:, hh * 128:(hh + 1) * 128], in0=a_[:, 0:128], scalar1=rd[:, hh:hh + 1], scalar2=None, op0=ALU.mult)
        p.dma("pool", MO[tcols, :], mo[:], reads=[mo])
    p.pop()
    p.pop()
    p.push()
    def make_tile(r0, mo_):
        p.dma("sp", mo_[:], MO[r0:r0 + 128, :], writes=[mo_])
    out_proj_phase(c, X, S, make_tile, W["w_o"])
    p.pop()


def t5_tables(t5_bias):
    n = np.arange(256)
    nf = np.maximum(n, 16).astype(np.float32)
    large = 16 + (np.log(nf / np.float32(16)) / np.float32(np.log(8.0)) * np.float32(16)).astype(np.int32)
    bucket = np.where(n < 16, n, np.minimum(large, 31))
    s = np.arange(128)[:, None]
    t = np.arange(128)[None, :]
    bt = np.empty((128, 2, 16, 128), np.float32)
    for k2 in range(2):
        d = np.clip(t - s + 128 * k2, 0, 255)
        bt[:, k2] = np.transpose(t5_bias[bucket[d]], (0, 2, 1))
    cb = np.ascontiguousarray(np.broadcast_to(t5_bias[31][None, :], (128, 16))).astype(np.float32)
    return bt, cb


SEQ = 8192
BATCH = 4
N_CORES = 4


def build_model(S):
    p = Prog()
    c = setup_consts(p)
    din = p.dram_in
    x = din("x", [S, D])
    A = {}
    shapes = {
        "rwkv_mu_fm": [2, 128, 96], "rwkv_w_rkv": [2, 3, D, D], "rwkv_w0": [2, D], "rwkv_w1": [2, D, 96], "rwkv_w2": [2, 96, D],
        "rwkv_a0": [2, D], "rwkv_a1": [2, D, 96], "rwkv_a2": [2, 96, D], "rwkv_v0": [1, D], "rwkv_v1": [1, D, 64], "rwkv_v2": [1, 64, D],
        "rwkv_g1": [2, D, 256], "rwkv_g2": [2, 256, D], "rwkv_k_k": [2, D], "rwkv_k_a": [2, D], "rwkv_r_k": [2, D],
        "rwkv_lnx_g": [2, D], "rwkv_lnx_b": [2, D], "rwkv_w_o": [2, D, D],
        "mlstm_w_in": [1, D, 6152], "mlstm_b_if": [1, 2, 4], "mlstm_norm_g": [1, D], "mlstm_w_o": [1, D, D],
        "dsa_w_in": [1, D, 3408], "dsa_q_norm_g": [1, 128], "dsa_k_norm_g": [1, 128], "dsa_w_o": [1, D, D],
        "t5_bt": [128, 2, 16, 128], "t5_cb": [128, 16],
        "mix_norm_g": [4, D], "ffn_norm_g": [4, D], "ffn_w_gate": [4, D, DFF], "ffn_w_up": [4, D, DFF], "ffn_w_down": [4, DFF, D],
    }
    for k, v in shapes.items():
        A[k] = din(k, v)
    y = p.dram_out("y", [S, D])
    sc = {k: p.dram_tmp("sc_" + k, [S, D]) for k in RW_SC}
    sc["QK"] = p.dram_tmp("sc_QK", [2048, S], BF16)
    sc["Vb"] = p.dram_tmp("sc_Vb", [S, D], BF16)
    sc["SO"] = sc["R"]
    sc["MO"] = sc["Y"]
    sc["QT"] = sc["QK"]
    sc["QIT"] = p.dram_tmp("sc_QIT", [1024, S], BF16)
    TR = min(S, 1024)
    for r0 in range(0, S, TR):
        p.dma("sp", y[r0:r0 + TR, :], x[r0:r0 + TR, :])
    p.barrier()
    for i in range(4):
        kind, j = i % 3, i // 3
        if kind == 0:
            W = {"mix_g": A["mix_norm_g"][i], "mu_fm": A["rwkv_mu_fm"][j], "w_rkv": A["rwkv_w_rkv"][j]}
            for nm in ["w0", "w1", "w2", "a0", "a1", "a2", "g1", "g2", "k_k", "k_a", "r_k", "lnx_g", "lnx_b", "w_o"]:
                W[nm] = A["rwkv_" + nm][j]
            if j > 0:
                for nm in ["v0", "v1", "v2"]:
                    W[nm] = A["rwkv_" + nm][j - 1]
            rwkv_prep(c, y, S, W, sc, j == 0)
            rwkv_scan(c, S, sc)
            rwkv_post(c, y, S, W, sc)
        elif kind == 1:
            W = {"mix_g": A["mix_norm_g"][i], "w_in": A["mlstm_w_in"][j], "b_if": A["mlstm_b_if"][j], "norm_g": A["mlstm_norm_g"][j], "w_o": A["mlstm_w_o"][j]}
            mlstm_layer(c, y, S, W, sc)
        else:
            W = {"mix_g": A["mix_norm_g"][i], "w_in": A["dsa_w_in"][j], "q_norm_g": A["dsa_q_norm_g"][j], "k_norm_g": A["dsa_k_norm_g"][j],
                 "w_o": A["dsa_w_o"][j], "t5_bt": A["t5_bt"], "t5_cb": A["t5_cb"]}
            dsa_layer(c, y, S, W, sc)
        ffn_phase(c, y, S, A["ffn_norm_g"][i], A["ffn_w_gate"][i], A["ffn_w_up"][i], A["ffn_w_down"][i])
    return p


def host_inputs(inputs):
    f = lambda a: np.ascontiguousarray(np.asarray(a, dtype=np.float32))
    feed = {}
    mu = f(inputs["rwkv_mu"])
    feed["rwkv_mu_fm"] = np.ascontiguousarray(mu.reshape(mu.shape[0], 6, 16, 128).transpose(0, 3, 1, 2).reshape(mu.shape[0], 128, 96))
    for k in ["rwkv_w_rkv", "rwkv_w0", "rwkv_w1", "rwkv_w2", "rwkv_a0", "rwkv_a1", "rwkv_a2", "rwkv_v0", "rwkv_v1", "rwkv_v2", "rwkv_g1", "rwkv_g2",
              "rwkv_k_k", "rwkv_k_a", "rwkv_lnx_g", "rwkv_lnx_b", "rwkv_w_o", "mlstm_w_in", "mlstm_b_if", "mlstm_norm_g", "mlstm_w_o",
              "dsa_w_in", "dsa_q_norm_g", "dsa_k_norm_g", "dsa_w_o", "mix_norm_g", "ffn_norm_g", "ffn_w_gate", "ffn_w_up", "ffn_w_down"]:
        feed[k] = f(inputs[k])
    rk = f(inputs["rwkv_r_k"])
    feed["rwkv_r_k"] = rk.reshape(rk.shape[0], -1)
    bt, cb = t5_tables(f(inputs["t5_bias"]))
    feed["t5_bt"] = bt
    feed["t5_cb"] = cb
    return feed


_CACHE = {}


def kernel(**inputs):
    x = np.asarray(inputs["x"], dtype=np.float32)
    B, S, _ = x.shape
    if S not in _CACHE:
        _CACHE[S] = build_model(S).finish()
    nc = _CACHE[S]
    feed = host_inputs(inputs)
    in_maps = []
    for b in range(B):
        m = dict(feed)
        m["x"] = np.ascontiguousarray(x[b])
        in_maps.append(m)
    res = run_bass_kernel_spmd(nc, in_maps, core_ids=list(range(B)))
    return np.stack([np.asarray(res.results[b]["y"], dtype=np.float32) for b in range(B)], axis=0)
```

```python
import contextlib
import numpy as np
import concourse.bass as bass
import concourse.mybir as mybir
from concourse.bass_utils import run_bass_kernel_spmd

F32 = mybir.dt.float32
BF16 = mybir.dt.bfloat16
ALU = mybir.AluOpType
AF = mybir.ActivationFunctionType
AX = mybir.AxisListType


class T:
    __slots__ = ("ap", "lw", "rd", "name", "root", "excl")

    def __init__(self, ap, name="", root=None):
        self.ap = ap
        self.lw = None
        self.rd = {}
        self.name = name
        self.root = root.root if root is not None else self
        self.excl = False

    def sub(self, rows, a, b):
        return T(self.ap[0:rows, a:b], self.name + "_v", root=self)

    def __getitem__(self, idx):
        return self.ap[idx]


class Prog:
    ENGS = ("pe", "dve", "act", "pool", "sp")

    def __init__(self, name="k", n_dma_sems=12, same_engine_sync=True, nc=None):
        self.nc = nc if nc is not None else bass.Bass("TRN2", target_bir_lowering=False)
        self.es = contextlib.ExitStack()
        self.q = {e: [] for e in self.ENGS}
        self.cnt = {e: 0 for e in self.ENGS}
        self.seen = {e: {} for e in self.ENGS}
        self.sem = {}
        for e in self.ENGS:
            self.sem[e] = self.es.enter_context(self.nc.semaphore("s_" + e))
        self.dma_pool = {}
        self.dma_rr = {}
        self.dma_cnt = {}
        for qe in ("sp", "pool", "act"):
            names = []
            for j in range(n_dma_sems):
                k = "d_%s_%d" % (qe, j)
                self.sem[k] = self.es.enter_context(self.nc.semaphore(k))
                self.dma_cnt[k] = 0
                names.append(k)
            self.dma_pool[qe] = names
            self.dma_rr[qe] = 0
        self.same_engine_sync = same_engine_sync
        self.n_inst = 0
        self._uid = 0
        self.scopes = []

    def uid(self, p):
        self._uid += 1
        return "%s%d" % (p, self._uid)

    def push(self):
        self.scopes.append(contextlib.ExitStack())

    def pop(self):
        self.barrier()
        self.scopes.pop().close()

    def sb(self, shape, dt=F32, name=None):
        es = self.scopes[-1] if self.scopes else self.es
        nm = self.uid(name or "sb")
        h = es.enter_context(self.nc.sbuf_tensor(nm, list(shape), dt))
        return T(h, nm)

    def ps(self, shape, dt=F32, name=None):
        es = self.scopes[-1] if self.scopes else self.es
        nm = self.uid(name or "ps")
        h = es.enter_context(self.nc.psum_tensor(nm, list(shape), dt))
        t = T(h, nm)
        t.excl = True
        return t

    def barrier(self):
        targets = {}
        for e in self.ENGS:
            if self.cnt[e]:
                targets[e] = self.cnt[e]
        for k, c in self.dma_cnt.items():
            if c:
                targets[k] = 16 * c
        for e in self.ENGS:
            for k, v in targets.items():
                if self.seen[e].get(k, 0) < v:
                    self.q[e].append(("w", k, v))
                    self.seen[e][k] = v

    def dram_in(self, name, shape, dt=F32):
        return self.nc.dram_tensor(name, list(shape), dt, kind="ExternalInput").ap()

    def dram_out(self, name, shape, dt=F32):
        return self.nc.dram_tensor(name, list(shape), dt, kind="ExternalOutput").ap()

    def dram_tmp(self, name, shape, dt=F32):
        return self.nc.dram_tensor(name, list(shape), dt, kind="Internal").ap()

    def _deps(self, e, reads, writes):
        deps = {}

        def add(t):
            if t is None:
                return
            k, v = t
            if deps.get(k, 0) < v:
                deps[k] = v

        reads = [r.root for r in reads]
        writes = [w.root for w in writes] + [r for r in reads if r.excl]
        for r in reads:
            add(r.lw)
        for w in writes:
            add(w.lw)
            for t in w.rd.values():
                add(t)
        for k, v in deps.items():
            if k == e and (e == "pe" or not self.same_engine_sync):
                continue
            if self.seen[e].get(k, 0) < v:
                self.q[e].append(("w", k, v))
                self.seen[e][k] = v

    def _mark(self, ticket, reads, writes):
        reads = [r.root for r in reads]
        writes = [w.root for w in writes] + [r for r in reads if r.excl]
        for r in reads:
            r.rd[ticket[0]] = ticket
        for w in writes:
            w.lw = ticket
            w.rd = {}

    def op(self, e, meth, reads=(), writes=(), **kw):
        self._deps(e, reads, writes)
        self.cnt[e] += 1
        ticket = (e, self.cnt[e])
        self.q[e].append(("o", (meth, kw), e, 1))
        self._mark(ticket, reads, writes)
        self.n_inst += 1

    def dma(self, qe, out, in_, reads=(), writes=(), **kw):
        pool = self.dma_pool[qe]
        k = pool[self.dma_rr[qe] % len(pool)]
        self.dma_rr[qe] += 1
        self._deps(qe, reads, writes)
        prev = 16 * self.dma_cnt[k]
        if prev and self.seen[qe].get(k, 0) < prev:
            self.q[qe].append(("w", k, prev))
            self.seen[qe][k] = prev
        self.dma_cnt[k] += 1
        ticket = (k, 16 * self.dma_cnt[k])
        kw = dict(kw); kw["out"] = out; kw["in_"] = in_
        self.q[qe].append(("o", ("dma_start", kw), k, 16))
        self._mark(ticket, reads, writes)
        self.n_inst += 1

    def mm(self, out_t, out_ap, lhsT_t, lhsT_ap, rhs_t, rhs_ap, start=True, stop=True):
        self.op("pe", "matmul", reads=[lhsT_t, rhs_t], writes=[out_t],
                out=out_ap, lhsT=lhsT_ap, rhs=rhs_ap, start=start, stop=stop)

    def finish(self):
        for qe, pool in self.dma_pool.items():
            for k in pool:
                v = 16 * self.dma_cnt[k]
                if v and self.seen[qe].get(k, 0) < v:
                    self.q[qe].append(("w", k, v))
                    self.seen[qe][k] = v
        nc = self.nc
        engmap = {"pe": "tensor", "dve": "vector", "act": "scalar", "pool": "gpsimd", "sp": "sync"}
        with nc.Block() as block:
            for e in self.ENGS:
                items = self.q[e]
                if not items:
                    continue

                def body(eng, items=items):
                    for it in items:
                        if it[0] == "w":
                            eng.wait_ge(self.sem[it[1]], it[2])
                        else:
                            try:
                                ins = getattr(eng, it[1][0])(**it[1][1])
                            except Exception:
                                print("FAILED INSTR", it[1][0], {k: (str(v)[:300]) for k, v in it[1][1].items()})
                                raise
                            ins.then_inc(self.sem[it[2]], it[3])

                getattr(block, engmap[e])(body)
        self.es.close()
        return nc


D = 2048
DFF = 5632
KT = 16
EPS = 1e-6


class Ctx:
    pass


def setup_consts(p):
    c = Ctx()
    c.p = p
    c.ident = p.sb([128, 128], F32, "ident")
    p.op("pool", "memset", writes=[c.ident], ap=c.ident[:], constant=1.0)
    p.op("pool", "affine_select", reads=[c.ident], writes=[c.ident], out=c.ident[:], in_=c.ident[:],
         pattern=[[-1, 128]], compare_op=ALU.is_equal, fill=0.0, base=0, channel_multiplier=1)
    c.PS = [p.ps([128, 512], F32, "bank%d" % i) for i in range(8)]
    c.rr = 0
    return c


def bcast_row(p, dst, src_ap_1d, n, q="sp"):
    p.dma(q, dst[:, 0:n], src_ap_1d.partition_broadcast(128), writes=[dst])


def rmsnorm_tile(p, x_t, x_ap, g_t, out_t, out_ap, tmp_t, ss_t, d=D, eps=EPS):
    p.op("act", "activation", reads=[x_t], writes=[tmp_t], out=tmp_t[:, 0:d], in_=x_ap, func=AF.Square)
    p.op("dve", "reduce_sum", reads=[tmp_t], writes=[ss_t], out=ss_t[:, 0:1], in_=tmp_t[:, 0:d], axis=AX.X)
    p.op("dve", "tensor_scalar", reads=[ss_t], writes=[ss_t], out=ss_t[:, 1:2], in0=ss_t[:, 0:1],
         scalar1=1.0 / d, scalar2=eps, op0=ALU.mult, op1=ALU.add)
    p.op("act", "activation", reads=[ss_t], writes=[ss_t], out=ss_t[:, 1:2], in_=ss_t[:, 1:2], func=AF.Sqrt)
    p.op("dve", "reciprocal", reads=[ss_t], writes=[ss_t], out=ss_t[:, 0:1], in_=ss_t[:, 1:2])
    p.op("dve", "scalar_tensor_tensor", reads=[x_t, ss_t, g_t], writes=[out_t], out=out_ap, in0=x_ap,
         scalar=ss_t[:, 0:1], in1=g_t[:, 0:d], op0=ALU.mult, op1=ALU.mult)


def to_fm(c, src_t, src_ap_fn, nkt, dstT, col0, evac_alt=0):
    p = c.p
    for g in range(0, nkt, 4):
        n = min(4, nkt - g)
        ps = c.PS[6 + (c.rr % 2)]
        c.rr += 1
        for j in range(n):
            p.op("pe", "transpose", reads=[src_t, c.ident], writes=[ps], out=ps[:, j * 128:(j + 1) * 128],
                 in_=src_ap_fn(g + j), identity=c.ident[:])
        eng = "dve" if (c.rr % 2) else "act"
        src = ps[:, 0:n * 128].rearrange("p (a b) -> p a b", a=n)
        dst = dstT[:, g:g + n, col0:col0 + 128]
        if eng == "dve":
            p.op("dve", "tensor_copy", reads=[ps], writes=[dstT], out=dst, in_=src)
        else:
            p.op("act", "copy", reads=[ps], writes=[dstT], out=dst, in_=src)


class WLoader:
    def __init__(self, c, nbuf=4):
        p = c.p
        self.c = c
        self.wb = [p.sb([128, 16, 512], BF16, "wb") for _ in range(nbuf)]
        self.i = 0

    def load(self, W, k0, ksz, n0, nsz):
        p = self.c.p
        wb = self.wb[self.i % len(self.wb)]
        self.i += 1
        nk = ksz // 128
        p.dma("sp", wb[:, 0:nk, 0:nsz], W[k0:k0 + ksz, n0:n0 + nsz].rearrange("(kt p) n -> p kt n", p=128), writes=[wb])
        return wb


def convert_weights(c, pairs):
    p = c.p
    p.push()
    st = [p.sb([128, 4096], F32, "cst") for _ in range(3)]
    cb = [p.sb([128, 4096], BF16, "ccb") for _ in range(3)]
    n = 0
    for src, dst, R, C in pairs:
        for r0 in range(0, R, 128):
            rs = min(128, R - r0)
            for c0 in range(0, C, 4096):
                cs = min(4096, C - c0)
                a = st[n % 3]
                b = cb[n % 3]
                p.dma("sp", a[0:rs, 0:cs], src[r0:r0 + rs, c0:c0 + cs], writes=[a])
                eng = ("act", "dve", "pool")[n % 3]
                if eng == "act":
                    p.op("act", "copy", reads=[a], writes=[b], out=b[0:rs, 0:cs], in_=a[0:rs, 0:cs])
                else:
                    p.op(eng, "tensor_copy", reads=[a], writes=[b], out=b[0:rs, 0:cs], in_=a[0:rs, 0:cs])
                p.dma("pool", dst[r0:r0 + rs, c0:c0 + cs], b[0:rs, 0:cs], reads=[b])
                n += 1
    p.pop()


def ffn_phase(c, X, S, gvec, Wg, Wu, Wd):
    p = c.p
    TG = min(512, S)
    NT = TG // 128
    NF = DFF // 128
    p.push()
    wl = WLoader(c)
    g_bc = p.sb([128, D], F32, "g_bc")
    bcast_row(p, g_bc, gvec, D)
    xres = [p.sb([128, D], F32, "xres") for _ in range(NT)]
    hn = p.sb([128, D], F32, "hn")
    tmp = p.sb([128, D], F32, "tmp")
    ss = p.sb([128, 2], F32, "ss")
    hT = p.sb([128, KT, TG], BF16, "hT")
    hidT = p.sb([128, NF, TG], BF16, "hidT")
    sg = [p.sb([128, TG], F32, "sg") for _ in range(2)]
    ot = [p.sb([128, 512], F32, "ot") for _ in range(2)]
    for t0 in range(0, S, TG):
        for ti in range(NT):
            p.dma("sp", xres[ti][:], X[t0 + ti * 128:t0 + (ti + 1) * 128, :], writes=[xres[ti]])
            rmsnorm_tile(p, xres[ti], xres[ti][:], g_bc, hn, hn[:], tmp, ss)
            to_fm(c, hn, lambda kt: hn[:, kt * 128:(kt + 1) * 128], KT, hT, ti * 128)
        for fb in range(0, NF, 4):
            nf = min(4, NF - fb)
            wg = wl.load(Wg, 0, D, fb * 128, nf * 128)
            wu = wl.load(Wu, 0, D, fb * 128, nf * 128)
            for j in range(nf):
                fc = fb + j
                pg = c.PS[(fc % 2) * 2]
                pu = c.PS[(fc % 2) * 2 + 1]
                for kt in range(KT):
                    p.mm(pg, pg[:, 0:TG], wg, wg[:, kt, j * 128:(j + 1) * 128], hT, hT[:, kt, :], start=(kt == 0), stop=(kt == KT - 1))
                for kt in range(KT):
                    p.mm(pu, pu[:, 0:TG], wu, wu[:, kt, j * 128:(j + 1) * 128], hT, hT[:, kt, :], start=(kt == 0), stop=(kt == KT - 1))
                s_ = sg[fc % 2]
                p.op("act", "activation", reads=[pg], writes=[s_], out=s_[:, 0:TG], in_=pg[:, 0:TG], func=AF.Silu)
                p.op("dve", "tensor_tensor", reads=[s_, pu], writes=[hidT], out=hidT[:, fc, :], in0=s_[:, 0:TG], in1=pu[:, 0:TG], op=ALU.mult)
        for nch in range(D // 512):
            for kb in range(0, NF, 16):
                nk = min(16, NF - kb)
                wd = wl.load(Wd, kb * 128, nk * 128, nch * 512, 512)
                for ti in range(NT):
                    ps = c.PS[ti]
                    for kt in range(nk):
                        p.mm(ps, ps[:, :], hidT, hidT[:, kb + kt, ti * 128:(ti + 1) * 128], wd, wd[:, kt, :],
                             start=(kb + kt == 0), stop=(kb + kt == NF - 1))
            for ti in range(NT):
                o = ot[ti % 2]
                p.op("dve", "tensor_tensor", reads=[c.PS[ti], xres[ti]], writes=[o], out=o[:], in0=c.PS[ti][:],
                     in1=xres[ti][:, nch * 512:(nch + 1) * 512], op=ALU.add)
                p.dma("pool", X[t0 + ti * 128:t0 + (ti + 1) * 128, nch * 512:(nch + 1) * 512], o[:], reads=[o])
    p.pop()


NH = 32
HS = 64
DECAY_SCALE = 0.6065306597126334


def evac(p, eng, ps_t, src, dst_t, dst):
    if eng == "act":
        p.op("act", "copy", reads=[ps_t], writes=[dst_t], out=dst, in_=src)
    else:
        p.op("dve", "tensor_copy", reads=[ps_t], writes=[dst_t], out=dst, in_=src)


def rwkv_prep(c, X, S, W, sc, first):
    p = c.p
    p.push()
    wl = WLoader(c)
    names = ["r", "w", "k", "v", "a", "g"]
    g_bc = p.sb([128, D], F32, "g_bc")
    bcast_row(p, g_bc, W["mix_g"], D)
    mu_fm = p.sb([128, 6 * KT], F32, "mu_fm")
    p.dma("sp", mu_fm[:], W["mu_fm"], writes=[mu_fm])
    vnames = ["w0", "a0", "k_k", "k_a", "r_k"] + ([] if first else ["v0"])
    vch = {nm: p.sb([128, 512], F32, "vc_" + nm) for nm in vnames}
    TG = min(256, S)
    NT = TG // 128
    xT = {nm: p.sb([128, KT, TG], BF16, "xT_" + nm) for nm in names}
    xt = p.sb([128, D], F32, "xt")
    hn = p.sb([128, D], F32, "hn")
    ss = p.sb([128, 2], F32, "ss")
    hT = p.sb([128, KT, 128], F32, "hT")
    xxT = p.sb([128, KT, 128], F32, "xxT")
    last = p.sb([128, KT, 1], F32, "last")
    p.op("pool", "memset", writes=[last], ap=last[:], constant=0.0)
    lor = {"w": 96, "a": 96, "g": 256}
    if not first:
        lor["v"] = 64
    lT = {nm: p.sb([128, 2, TG], BF16, "lT_" + nm) for nm in lor}
    l2 = {nm: p.sb([128, 2, 512], BF16, "l2_" + nm) for nm in lor}
    E = {nm: p.sb([128, 512], F32, "E_" + nm) for nm in ["r", "k", "v", "w", "a", "g", "kk", "t1", "t2", "vf"]}
    st8 = p.sb([128, 16], F32, "st8")
    l1w = {"w": W["w1"], "a": W["a1"], "g": W["g1"]}
    l2w = {"w": W["w2"], "a": W["a2"], "g": W["g2"]}
    if not first:
        l1w["v"] = W["v1"]
        l2w["v"] = W["v2"]
    lfn = {"w": AF.Tanh, "a": AF.Copy, "g": AF.Sigmoid, "v": AF.Copy}
    PS = c.PS
    for t0 in range(0, S, TG):
      for ti in range(NT):
        rows = slice(t0 + ti * 128, t0 + (ti + 1) * 128)
        tsl = slice(ti * 128, (ti + 1) * 128)
        p.dma("sp", xt[:], X[rows, :], writes=[xt])
        rmsnorm_tile(p, xt, xt[:], g_bc, hn, hn[:], hn, ss)
        to_fm(c, hn, lambda kt: hn[:, kt * 128:(kt + 1) * 128], KT, hT, 0)
        p.op("dve", "tensor_tensor", reads=[hT], writes=[xxT], out=xxT[:, :, 1:128], in0=hT[:, :, 0:127], in1=hT[:, :, 1:128], op=ALU.subtract)
        p.op("dve", "tensor_tensor", reads=[hT, last], writes=[xxT], out=xxT[:, :, 0:1], in0=last[:], in1=hT[:, :, 0:1], op=ALU.subtract)
        p.op("pool", "tensor_copy", reads=[hT], writes=[last], out=last[:], in_=hT[:, :, 127:128])
        for i, nm in enumerate(names):
            for kt in range(KT):
                p.op("dve", "scalar_tensor_tensor", reads=[xxT, mu_fm, hT], writes=[xT[nm]], out=xT[nm][:, kt, tsl], in0=xxT[:, kt, :],
                     scalar=mu_fm[:, i * KT + kt:i * KT + kt + 1], in1=hT[:, kt, :], op0=ALU.mult, op1=ALU.add)
      for nm, ld in lor.items():
            wb = wl.load(l1w[nm], 0, D, 0, ld)
            for j in range(0, ld, 128):
                lsz = min(128, ld - j)
                ps = PS[4]
                for kt in range(KT):
                    p.mm(ps, ps[0:lsz, 0:TG], wb, wb[:, kt, j:j + lsz], xT[nm], xT[nm][:, kt, :], start=(kt == 0), stop=(kt == KT - 1))
                p.op("act", "activation", reads=[ps], writes=[lT[nm]], out=lT[nm][0:lsz, j // 128, :], in_=ps[0:lsz, 0:TG], func=lfn[nm])
      for nch in range(4):
        cs = slice(nch * 512, (nch + 1) * 512)
        for nm in vnames:
            p.dma("sp", vch[nm][:], W[nm][cs].partition_broadcast(128), writes=[vch[nm]])
        for nm, ld in lor.items():
            for j in range(0, ld, 128):
                lsz = min(128, ld - j)
                p.dma("sp", l2[nm][0:lsz, j // 128, :], l2w[nm][j:j + lsz, cs], writes=[l2[nm]])
        wbs = {nm: wl.load(W["w_rkv"][i], 0, D, nch * 512, 512) for i, nm in enumerate(["r", "k", "v"])}
        for ti in range(NT):
            rows = slice(t0 + ti * 128, t0 + (ti + 1) * 128)
            tsl = slice(ti * 128, (ti + 1) * 128)
            for i, nm in enumerate(["r", "k", "v"]):
                wb = wbs[nm]
                ps = PS[i]
                for kt in range(KT):
                    p.mm(ps, ps[:, :], xT[nm], xT[nm][:, kt, tsl], wb, wb[:, kt, :], start=(kt == 0), stop=(kt == KT - 1))
                evac(p, "act", ps, ps[:, :], E[nm], E[nm][:])
            lps = {"w": PS[3], "a": PS[4], "g": PS[5], "v": PS[0]}
            for nm, ld in lor.items():
                nj = (ld + 127) // 128
                for jj in range(nj):
                    lsz = min(128, ld - jj * 128)
                    p.mm(lps[nm], lps[nm][:, :], lT[nm], lT[nm][0:lsz, jj, tsl], l2[nm], l2[nm][0:lsz, jj, :], start=(jj == 0), stop=(jj == nj - 1))
            def tt(eng, out_t, a_t, a, b_t, b, op, o=None):
                p.op(eng, "tensor_tensor", reads=[a_t, b_t], writes=[out_t], out=(o if o is not None else out_t[:]), in0=a, in1=b, op=op)
            tt("dve", E["w"], lps["w"], lps["w"][:], vch["w0"], vch["w0"][:], ALU.add)
            p.op("act", "activation", reads=[E["w"]], writes=[E["w"]], out=E["w"][:], in_=E["w"][:], func=AF.Sigmoid)
            p.op("pool", "tensor_scalar", reads=[E["w"]], writes=[E["w"]], out=E["w"][:], in0=E["w"][:], scalar1=-DECAY_SCALE, scalar2=None, op0=ALU.mult)
            p.dma("pool", sc["LW"][rows, cs], E["w"][:], reads=[E["w"]])
            tt("dve", E["a"], lps["a"], lps["a"][:], vch["a0"], vch["a0"][:], ALU.add)
            p.op("act", "activation", reads=[E["a"]], writes=[E["a"]], out=E["a"][:], in_=E["a"][:], func=AF.Sigmoid)
            evac(p, "act", lps["g"], lps["g"][:], E["g"], E["g"][:])
            p.dma("pool", sc["G"][rows, cs], E["g"][:], reads=[E["g"]])
            if first:
                p.dma("pool", sc["VF"][rows, cs], E["v"][:], reads=[E["v"]])
            else:
                p.dma("sp", E["vf"][:], sc["VF"][rows, cs], writes=[E["vf"]])
                tt("dve", E["t1"], lps["v"], lps["v"][:], vch["v0"], vch["v0"][:], ALU.add)
                p.op("act", "activation", reads=[E["t1"]], writes=[E["t1"]], out=E["t1"][:], in_=E["t1"][:], func=AF.Sigmoid)
                tt("pool", E["vf"], E["vf"], E["vf"][:], E["v"], E["v"][:], ALU.subtract)
                tt("dve", E["vf"], E["vf"], E["vf"][:], E["t1"], E["t1"][:], ALU.mult)
                tt("dve", E["v"], E["v"], E["v"][:], E["vf"], E["vf"][:], ALU.add)
            p.dma("pool", sc["V"][rows, cs], E["v"][:], reads=[E["v"]])
            p.dma("pool", sc["R"][rows, cs], E["r"][:], reads=[E["r"]])
            tt("pool", E["kk"], E["k"], E["k"][:], vch["k_k"], vch["k_k"][:], ALU.mult)
            tt("dve", E["t1"], E["kk"], E["kk"][:], E["kk"], E["kk"][:], ALU.mult)
            p.op("dve", "reduce_sum", reads=[E["t1"]], writes=[st8], out=st8[:, 0:8], in_=E["t1"][:].rearrange("p (h k) -> p h k", k=HS), axis=AX.X)
            p.op("act", "activation", reads=[st8], writes=[st8], out=st8[:, 0:8], in_=st8[:, 0:8], func=AF.Sqrt)
            p.op("dve", "tensor_scalar", reads=[st8], writes=[st8], out=st8[:, 0:8], in0=st8[:, 0:8], scalar1=1e-12, scalar2=None, op0=ALU.max)
            p.op("dve", "reciprocal", reads=[st8], writes=[st8], out=st8[:, 8:16], in_=st8[:, 0:8])
            p.op("dve", "tensor_tensor", reads=[E["kk"], st8], writes=[E["kk"]], out=E["kk"][:].rearrange("p (h k) -> p h k", k=HS),
                 in0=E["kk"][:].rearrange("p (h k) -> p h k", k=HS), in1=st8[:, 8:16].unsqueeze(2).to_broadcast([128, 8, HS]), op=ALU.mult)
            p.op("pool", "tensor_scalar", reads=[E["a"]], writes=[E["t1"]], out=E["t1"][:], in0=E["a"][:], scalar1=-1.0, scalar2=None, op0=ALU.add)
            tt("pool", E["t1"], E["t1"], E["t1"][:], vch["k_a"], vch["k_a"][:], ALU.mult)
            p.op("pool", "tensor_scalar", reads=[E["t1"]], writes=[E["t1"]], out=E["t1"][:], in0=E["t1"][:], scalar1=1.0, scalar2=None, op0=ALU.add)
            tt("dve", E["k"], E["k"], E["k"][:], E["t1"], E["t1"][:], ALU.mult)
            p.dma("pool", sc["KP"][rows, cs], E["k"][:], reads=[E["k"]])
            tt("dve", E["t2"], E["kk"], E["kk"][:], E["a"], E["a"][:], ALU.mult)
            p.dma("pool", sc["B"][rows, cs], E["t2"][:], reads=[E["t2"]])
            p.op("pool", "tensor_scalar", reads=[E["kk"]], writes=[E["kk"]], out=E["kk"][:], in0=E["kk"][:], scalar1=-1.0, scalar2=None, op0=ALU.mult)
            p.dma("pool", sc["A"][rows, cs], E["kk"][:], reads=[E["kk"]])
            tt("dve", E["t1"], E["r"], E["r"][:], E["k"], E["k"][:], ALU.mult)
            tt("dve", E["t1"], E["t1"], E["t1"][:], vch["r_k"], vch["r_k"][:], ALU.mult)
            p.op("dve", "reduce_sum", reads=[E["t1"]], writes=[st8], out=st8[:, 0:8], in_=E["t1"][:].rearrange("p (h k) -> p h k", k=HS), axis=AX.X)
            p.op("dve", "tensor_tensor", reads=[E["v"], st8], writes=[E["t1"]], out=E["t1"][:].rearrange("p (h k) -> p h k", k=HS),
                 in0=E["v"][:].rearrange("p (h k) -> p h k", k=HS), in1=st8[:, 0:8].unsqueeze(2).to_broadcast([128, 8, HS]), op=ALU.mult)
            p.dma("pool", sc["BON"][rows, cs], E["t1"][:], reads=[E["t1"]])
    p.pop()


def rwkv_scan(c, S, sc):
    p = c.p
    p.push()
    PS = c.PS
    ident = c.ident
    ule = p.sb([128, 128], F32, "ule")
    p.op("pool", "memset", writes=[ule], ap=ule[:], constant=1.0)
    p.op("pool", "affine_select", reads=[ule], writes=[ule], out=ule[:], in_=ule[:], pattern=[[1, 128]],
         compare_op=ALU.is_ge, fill=0.0, base=0, channel_multiplier=-1)
    su = p.sb([128, 128], F32, "su")
    p.op("pool", "memset", writes=[su], ap=su[:], constant=1.0)
    p.op("pool", "affine_select", reads=[su], writes=[su], out=su[:], in_=su[:], pattern=[[1, 128]],
         compare_op=ALU.is_gt, fill=0.0, base=0, channel_multiplier=-1)
    sl = p.sb([128, 128], F32, "sl")
    p.op("pool", "memset", writes=[sl], ap=sl[:], constant=1.0)
    p.op("pool", "affine_select", reads=[sl], writes=[sl], out=sl[:], in_=sl[:], pattern=[[-1, 128]],
         compare_op=ALU.is_gt, fill=0.0, base=0, channel_multiplier=1)
    mask4 = p.sb([128, 512], F32, "mask4")
    for i, m in enumerate([su, ule, su, ule]):
        p.op("pool", "tensor_copy", reads=[m], writes=[mask4], out=mask4[:, i * 128:(i + 1) * 128], in_=m[:])
    ones_col = p.sb([128, 1], F32, "ones_col")
    p.op("pool", "memset", writes=[ones_col], ap=ones_col[:], constant=1.0)
    raw = {nm: p.sb([128, D], F32, "raw_" + nm) for nm in ["R", "LW", "KP", "V", "A", "B"]}
    Gc = p.sb([128, D], F32, "Gc")
    Et = p.sb([128, D], F32, "Et")
    Rt = p.sb([128, D], F32, "Rt")
    At = p.sb([128, D], F32, "At")
    Bt = p.sb([128, D], F32, "Bt")
    Kt = p.sb([128, D], F32, "Kt")
    GL = p.sb([64, NH], F32, "GL")
    GH = 4
    fm = [p.sb([64, 512], F32, "fm") for _ in range(GH)]
    GM = [p.sb([128, 512], F32, "GM") for _ in range(GH)]
    A0 = [p.sb([128, 128], F32, "A0") for _ in range(GH)]
    AA = [[p.sb([128, 256], F32, "AA") for _ in range(2)] for _ in range(GH)]
    PTb = [[p.sb([128, 128], F32, "PT") for _ in range(2)] for _ in range(GH)]
    NV = p.sb([128, 256], F32, "NV")
    KV = p.sb([64, 256], F32, "KV")
    Xs = p.sb([128, 256], F32, "Xs")
    Us = p.sb([128, 256], F32, "Us")
    S1 = p.sb([64, 256], F32, "S1")
    S2 = p.sb([64, 256], F32, "S2")
    Tst = [p.sb([64, GH * HS], F32, "Tst") for _ in range(NH // GH)]
    for t_ in Tst:
        p.op("pool", "memset", writes=[t_], ap=t_[:], constant=0.0)
    Yt = p.sb([128, D], F32, "Yt")

    TPS = [PS[0], PS[1]]
    pg = PS[2]
    pn = [PS[3].sub(128, 0, 128), PS[3].sub(128, 128, 256), PS[3].sub(128, 256, 384), PS[3].sub(128, 384, 512)]
    sq = [PS[4].sub(128, 0, 256), PS[5].sub(128, 0, 256), PS[4].sub(128, 256, 512), PS[5].sub(128, 256, 512)]
    pp = [PS[6].sub(128, 0, 128), PS[7].sub(128, 0, 128), PS[6].sub(128, 128, 256), PS[7].sub(128, 128, 256)]
    pnv = PS[0].sub(128, 0, 256)
    pY = PS[0].sub(128, 256, 512)
    pkv = PS[1].sub(128, 0, 256)
    p3 = PS[1].sub(128, 256, 512)
    p1 = PS[2].sub(128, 0, 256)
    p2 = PS[2].sub(128, 256, 512)

    for ch in range(S // 128):
        rows = slice(ch * 128, (ch + 1) * 128)
        for nm in raw:
            p.dma("sp", raw[nm][:], sc[nm][rows, :], writes=[raw[nm]])
        LW = raw["LW"]
        V = raw["V"]
        for n4 in range(4):
            ps = TPS[n4 % 2]
            p.mm(ps, ps[:, :], ule, ule[:], LW, LW[:, n4 * 512:(n4 + 1) * 512])
            evac(p, "act" if n4 % 2 else "dve", ps, ps[:, :], Gc, Gc[:, n4 * 512:(n4 + 1) * 512])
        ps = TPS[0]
        for h in range(NH):
            p.mm(ps, ps[0:64, h:h + 1], LW, LW[:, h * 64:(h + 1) * 64], ones_col, ones_col[:])
        p.op("act", "activation", reads=[ps], writes=[GL], out=GL[:], in_=ps[0:64, 0:NH], func=AF.Exp)
        p.op("act", "activation", reads=[Gc], writes=[Et], out=Et[:], in_=Gc[:], func=AF.Exp)
        p.op("dve", "tensor_tensor", reads=[raw["R"], Et], writes=[Rt], out=Rt[:], in0=raw["R"][:], in1=Et[:], op=ALU.mult)
        p.op("act", "activation", reads=[Gc], writes=[Et], out=Et[:], in_=Gc[:], func=AF.Exp, scale=-1.0)
        p.op("dve", "tensor_tensor", reads=[raw["B"], Et], writes=[Bt], out=Bt[:], in0=raw["B"][:], in1=Et[:], op=ALU.mult)
        p.op("pool", "tensor_tensor", reads=[raw["KP"], Et], writes=[Kt], out=Kt[:], in0=raw["KP"][:], in1=Et[:], op=ALU.mult)
        p.op("pool", "tensor_tensor", reads=[Gc, LW], writes=[Gc], out=Gc[:], in0=Gc[:], in1=LW[:], op=ALU.subtract)
        p.op("act", "activation", reads=[Gc], writes=[Et], out=Et[:], in_=Gc[:], func=AF.Exp)
        p.op("dve", "tensor_tensor", reads=[raw["A"], Et], writes=[At], out=At[:], in0=raw["A"][:], in1=Et[:], op=ALU.mult)
        for g in range(NH // GH):
            gcols = slice(g * GH * HS, (g + 1) * GH * HS)
            cur = []
            for j in range(GH):
                h = g * GH + j
                hc = slice(h * HS, (h + 1) * HS)
                pst = TPS[h % 2]
                for qi, Q in enumerate([At, Rt, Bt, Kt]):
                    p.op("pe", "transpose", reads=[Q, ident], writes=[pst], out=pst[0:64, qi * 128:(qi + 1) * 128], in_=Q[:, hc], identity=ident[:])
                evac(p, "act" if h % 2 else "dve", pst, pst[0:64, :], fm[j], fm[j][:])
                f = fm[j]
                p.mm(pg, pg[:, 0:256], f, f[:, 256:384], f, f[:, 0:256])
                p.mm(pg, pg[:, 256:512], f, f[:, 384:512], f, f[:, 0:256])
                p.op("dve", "tensor_tensor", reads=[pg, mask4], writes=[GM[j]], out=GM[j][:], in0=pg[:], in1=mask4[:], op=ALU.mult)
                pn_ = pn[j]
                p.mm(pn_, pn_[:], f, f[:, 0:128], f, f[:, 256:384])
                a0 = A0[j]
                p.op("dve", "tensor_tensor", reads=[pn_, sl], writes=[a0], out=a0[:], in0=pn_[:], in1=sl[:], op=ALU.mult)
                PT = PTb[j][0]
                p.op("pool", "tensor_tensor", reads=[GM[j], ident], writes=[PT], out=PT[:], in0=GM[j][:, 0:128], in1=ident[:], op=ALU.add)
                cur.append([a0, a0[:], GM[j], GM[j][:, 0:128], PT])
            for i in range(1, 7):
                for j in range(GH):
                    A_t, A_ap, AT_t, AT_ap, PT = cur[j]
                    s_ = sq[j]
                    p.mm(s_, s_[:, 0:128], AT_t, AT_ap, A_t, A_ap)
                    if i < 6:
                        p.mm(s_, s_[:, 128:256], A_t, A_ap, AT_t, AT_ap)
                for j in range(GH):
                    aa = AA[j][i % 2]
                    w_ = 256 if i < 6 else 128
                    evac(p, "act" if j % 2 else "dve", sq[j], sq[j][:, 0:w_], aa, aa[:, 0:w_])
                    cur[j][0:4] = [aa, aa[:, 0:128], aa, aa[:, 128:256]]
                for j in range(GH):
                    A_t, A_ap, AT_t, AT_ap, PT = cur[j]
                    p.mm(pp[j], pp[j][:], A_t, A_ap, PT, PT[:])
                for j in range(GH):
                    PT = cur[j][4]
                    PTn = PTb[j][i % 2]
                    p.op("dve", "tensor_tensor", reads=[pp[j], PT], writes=[PTn], out=PTn[:], in0=pp[j][:], in1=PT[:], op=ALU.add)
                    cur[j][4] = PTn
            for j in range(GH):
                h = g * GH + j
                hc = slice(h * HS, (h + 1) * HS)
                jc = slice(j * HS, (j + 1) * HS)
                p.mm(pnv, pnv[:, jc], GM[j], GM[j][:, 256:384], V, V[:, hc])
                p.mm(pkv, pkv[0:64, jc], Kt, Kt[:, hc], V, V[:, hc])
            evac(p, "act", pnv, pnv[:], NV, NV[:])
            evac(p, "act", pkv, pkv[0:64, :], KV, KV[:])
            Tg = Tst[g]
            for j in range(GH):
                jc = slice(j * HS, (j + 1) * HS)
                p.mm(p1, p1[:, jc], fm[j], fm[j][:, 0:128], Tg, Tg[:, jc])
            p.op("dve", "tensor_tensor", reads=[p1, NV], writes=[Xs], out=Xs[:], in0=p1[:], in1=NV[:], op=ALU.add)
            for j in range(GH):
                jc = slice(j * HS, (j + 1) * HS)
                p.mm(p2, p2[:, jc], PTb[j][0], PTb[j][0][:], Xs, Xs[:, jc])
            evac(p, "act", p2, p2[:], Us, Us[:])
            for j in range(GH):
                h = g * GH + j
                hc = slice(h * HS, (h + 1) * HS)
                jc = slice(j * HS, (j + 1) * HS)
                p.mm(pY, pY[:, jc], fm[j], fm[j][:, 128:256], Tg, Tg[:, jc], start=True, stop=False)
                p.mm(pY, pY[:, jc], GM[j], GM[j][:, 128:256], Us, Us[:, jc], start=False, stop=False)
                p.mm(pY, pY[:, jc], GM[j], GM[j][:, 384:512], V, V[:, hc], start=False, stop=True)
            evac(p, "act", pY, pY[:], Yt, Yt[:, gcols])
            for j in range(GH):
                h = g * GH + j
                hc = slice(h * HS, (h + 1) * HS)
                jc = slice(j * HS, (j + 1) * HS)
                p.mm(p3, p3[0:64, jc], Bt, Bt[:, hc], Us, Us[:, jc])
            p.op("pool", "tensor_tensor", reads=[Tg, KV], writes=[S1], out=S1[:], in0=Tg[:], in1=KV[:], op=ALU.add)
            p.op("dve", "tensor_tensor", reads=[p3, S1], writes=[S2], out=S2[:], in0=p3[0:64, :], in1=S1[:], op=ALU.add)
            p.op("dve", "tensor_tensor", reads=[S2, GL], writes=[Tg], out=Tg[:].rearrange("p (h v) -> p h v", v=HS),
                 in0=S2[:].rearrange("p (h v) -> p h v", v=HS),
                 in1=GL[:, g * GH:(g + 1) * GH].unsqueeze(2).to_broadcast([64, GH, HS]), op=ALU.mult)
        p.dma("pool", sc["Y"][rows, :], Yt[:], reads=[Yt])
    p.pop()


def out_proj_phase(c, X, S, make_tile, Wo, extra_alloc=None):
    p = c.p
    TG = min(512, S)
    NT = TG // 128
    wl = WLoader(c)
    oT = p.sb([128, KT, TG], BF16, "oT")
    mo = p.sb([128, D], F32, "mo")
    xres = [p.sb([128, D], F32, "xres") for _ in range(NT)]
    ot = [p.sb([128, 512], F32, "ot") for _ in range(2)]
    for t0 in range(0, S, TG):
        for ti in range(NT):
            r0 = t0 + ti * 128
            make_tile(r0, mo)
            to_fm(c, mo, lambda kt: mo[:, kt * 128:(kt + 1) * 128], KT, oT, ti * 128)
            p.dma("sp", xres[ti][:], X[r0:r0 + 128, :], writes=[xres[ti]])
        for nch in range(4):
            wb = wl.load(Wo, 0, D, nch * 512, 512)
            for ti in range(NT):
                ps = c.PS[ti]
                for kt in range(KT):
                    p.mm(ps, ps[:, :], oT, oT[:, kt, ti * 128:(ti + 1) * 128], wb, wb[:, kt, :], start=(kt == 0), stop=(kt == KT - 1))
                o = ot[ti % 2]
                p.op("dve", "tensor_tensor", reads=[ps, xres[ti]], writes=[o], out=o[:], in0=ps[:], in1=xres[ti][:, nch * 512:(nch + 1) * 512], op=ALU.add)
                p.dma("pool", X[t0 + ti * 128:t0 + (ti + 1) * 128, nch * 512:(nch + 1) * 512], o[:], reads=[o])


def rwkv_post(c, X, S, W, sc):
    p = c.p
    p.push()
    Yt = p.sb([128, D], F32, "Yt")
    Bo = p.sb([128, D], F32, "Bo")
    Gt = p.sb([128, D], F32, "Gt")
    sq = p.sb([128, D], F32, "sq")
    lg = p.sb([128, D], F32, "lg")
    lb = p.sb([128, D], F32, "lb")
    bcast_row(p, lg, W["lnx_g"], D)
    bcast_row(p, lb, W["lnx_b"], D)
    st = p.sb([128, 64], F32, "st")

    def v3(t):
        return t[:].rearrange("p (h k) -> p h k", k=HS)

    def make_tile(r0, mo):
        rows = slice(r0, r0 + 128)
        p.dma("sp", Yt[:], sc["Y"][rows, :], writes=[Yt])
        p.dma("sp", Bo[:], sc["BON"][rows, :], writes=[Bo])
        p.dma("sp", Gt[:], sc["G"][rows, :], writes=[Gt])
        p.op("dve", "reduce_sum", reads=[Yt], writes=[st], out=st[:, 0:32], in_=v3(Yt), axis=AX.X)
        p.op("dve", "tensor_scalar", reads=[st], writes=[st], out=st[:, 0:32], in0=st[:, 0:32], scalar1=1.0 / HS, scalar2=None, op0=ALU.mult)
        p.op("dve", "tensor_tensor", reads=[Yt, st], writes=[Yt], out=v3(Yt), in0=v3(Yt), in1=st[:, 0:32].unsqueeze(2).to_broadcast([128, NH, HS]), op=ALU.subtract)
        p.op("act", "activation", reads=[Yt], writes=[sq], out=sq[:], in_=Yt[:], func=AF.Square)
        p.op("dve", "reduce_sum", reads=[sq], writes=[st], out=st[:, 32:64], in_=v3(sq), axis=AX.X)
        p.op("dve", "tensor_scalar", reads=[st], writes=[st], out=st[:, 32:64], in0=st[:, 32:64], scalar1=1.0 / HS, scalar2=64e-5, op0=ALU.mult, op1=ALU.add)
        p.op("act", "activation", reads=[st], writes=[st], out=st[:, 32:64], in_=st[:, 32:64], func=AF.Sqrt)
        p.op("dve", "reciprocal", reads=[st], writes=[st], out=st[:, 0:32], in_=st[:, 32:64])
        p.op("dve", "tensor_tensor", reads=[Yt, st], writes=[Yt], out=v3(Yt), in0=v3(Yt), in1=st[:, 0:32].unsqueeze(2).to_broadcast([128, NH, HS]), op=ALU.mult)
        p.op("pool", "tensor_tensor", reads=[Yt, lg], writes=[Yt], out=Yt[:], in0=Yt[:], in1=lg[:], op=ALU.mult)
        p.op("pool", "tensor_tensor", reads=[Yt, lb], writes=[Yt], out=Yt[:], in0=Yt[:], in1=lb[:], op=ALU.add)
        p.op("dve", "tensor_tensor", reads=[Yt, Bo], writes=[Yt], out=Yt[:], in0=Yt[:], in1=Bo[:], op=ALU.add)
        p.op("dve", "tensor_tensor", reads=[Yt, Gt], writes=[mo], out=mo[:], in0=Yt[:], in1=Gt[:], op=ALU.mult)

    out_proj_phase(c, X, S, make_tile, W["w_o"])
    p.pop()


RW_SC = ["R", "LW", "KP", "V", "A", "B", "G", "BON", "Y", "VF"]


MH = 4
MDK = 256
MDV = 512


def mlstm_layer(c, X, S, W, sc):
    p = c.p
    PS = c.PS
    ident = c.ident
    TG = min(512, S)
    NT = TG // 128
    Win = W["w_in"]
    QK, Vb, SO, MO = sc["QK"], sc["Vb"], sc["SO"], sc["MO"]
    p.push()
    GI = p.sb([4, S], F32, "GI")
    GF = p.sb([4, S], F32, "GF")
    p.push()
    wl = WLoader(c)
    g_bc = p.sb([128, D], F32, "g_bc")
    bcast_row(p, g_bc, W["mix_g"], D)
    xt = p.sb([128, D], F32, "xt")
    hn = p.sb([128, D], F32, "hn")
    ss = p.sb([128, 2], F32, "ss")
    hT = p.sb([128, KT, TG], BF16, "hT")
    qk_sb = [p.sb([128, TG], BF16, "qk_sb") for _ in range(2)]
    vb_sb = [p.sb([128, 512], BF16, "vb_sb") for _ in range(2)]
    so_sb = [p.sb([128, 512], F32, "so_sb") for _ in range(2)]
    for t0 in range(0, S, TG):
        for ti in range(NT):
            p.dma("sp", xt[:], X[t0 + ti * 128:t0 + (ti + 1) * 128, :], writes=[xt])
            rmsnorm_tile(p, xt, xt[:], g_bc, hn, hn[:], hn, ss)
            to_fm(c, hn, lambda kt: hn[:, kt * 128:(kt + 1) * 128], KT, hT, ti * 128)
        for nb in range(4):
            wb = wl.load(Win, 0, D, nb * 512, 512)
            for j in range(4):
                ps = PS[j % 2]
                for kt in range(KT):
                    p.mm(ps, ps[:, 0:TG], wb, wb[:, kt, j * 128:(j + 1) * 128], hT, hT[:, kt, :], start=(kt == 0), stop=(kt == KT - 1))
                o = qk_sb[j % 2]
                p.op("act", "activation", reads=[ps], writes=[o], out=o[:, 0:TG], in_=ps[:, 0:TG], func=AF.Copy, scale=(0.0625 if nb < 2 else 1.0))
                r0 = nb * 512 + j * 128
                p.dma("pool", QK[r0:r0 + 128, t0:t0 + TG], o[:, 0:TG], reads=[o])
        for nb in range(8):
            wb = wl.load(Win, 0, D, 2048 + nb * 512, 512)
            for ti in range(NT):
                ps = PS[2 + ti % 2]
                for kt in range(KT):
                    p.mm(ps, ps[:, :], hT, hT[:, kt, ti * 128:(ti + 1) * 128], wb, wb[:, kt, :], start=(kt == 0), stop=(kt == KT - 1))
                rows = slice(t0 + ti * 128, t0 + (ti + 1) * 128)
                if nb < 4:
                    o = vb_sb[ti % 2]
                    p.op("dve", "tensor_copy", reads=[ps], writes=[o], out=o[:], in_=ps[:])
                    p.dma("pool", Vb[rows, nb * 512:(nb + 1) * 512], o[:], reads=[o])
                else:
                    o = so_sb[ti % 2]
                    p.op("act", "activation", reads=[ps], writes=[o], out=o[:], in_=ps[:], func=AF.Sigmoid)
                    p.dma("pool", SO[rows, (nb - 4) * 512:(nb - 3) * 512], o[:], reads=[o])
        wb = wl.load(Win, 0, D, 6144, 8)
        for gi, (G_, ps) in enumerate([(GI, PS[4]), (GF, PS[5])]):
            for kt in range(KT):
                p.mm(ps, ps[0:4, 0:TG], wb, wb[:, kt, gi * 4:(gi + 1) * 4], hT, hT[:, kt, :], start=(kt == 0), stop=(kt == KT - 1))
            p.op("act", "copy", reads=[ps], writes=[G_], out=G_[:, t0:t0 + TG], in_=ps[0:4, 0:TG])
    p.pop()
    NB = S // 128
    bif = p.sb([4, 2], F32, "bif")
    p.dma("sp", bif[:], W["b_if"].rearrange("a h -> h a"), writes=[bif], allow_slow_non_contiguous=True)
    SCN = min(1024, S)
    onesr = p.sb([4, SCN], F32, "onesr")
    zeror = p.sb([4, SCN], F32, "zeror")
    p.op("pool", "memset", writes=[onesr], ap=onesr[:], constant=1.0)
    p.op("pool", "memset", writes=[zeror], ap=zeror[:], constant=0.0)
    Fr = p.sb([4, S], F32, "Fr")
    for G_, col in ((GI, 0), (GF, 1)):
        p.op("dve", "tensor_scalar", reads=[G_, bif], writes=[G_], out=G_[:], in0=G_[:], scalar1=bif[:, col:col + 1], scalar2=None, op0=ALU.add)
        p.op("act", "activation", reads=[G_], writes=[G_], out=G_[:], in_=G_[:], func=AF.Tanh, scale=1.0 / 15.0)
        p.op("dve", "tensor_scalar", reads=[G_], writes=[G_], out=G_[:], in0=G_[:], scalar1=15.0, scalar2=None, op0=ALU.mult)
    p.op("act", "activation", reads=[GF], writes=[GF], out=GF[:], in_=GF[:], func=AF.Exp, scale=-1.0)
    p.op("dve", "tensor_scalar", reads=[GF], writes=[GF], out=GF[:], in0=GF[:], scalar1=1.0, scalar2=None, op0=ALU.add)
    p.op("act", "activation", reads=[GF], writes=[GF], out=GF[:], in_=GF[:], func=AF.Ln)
    p.op("dve", "tensor_scalar", reads=[GF], writes=[GF], out=GF[:], in0=GF[:], scalar1=-1.0, scalar2=None, op0=ALU.mult)
    for s0 in range(0, S, SCN):
        init = 0.0 if s0 == 0 else Fr[:, s0 - 1:s0]
        p.op("dve", "tensor_tensor_scan", reads=[GF, onesr, Fr], writes=[Fr], out=Fr[:, s0:s0 + SCN], data0=onesr[:, 0:SCN], data1=GF[:, s0:s0 + SCN],
             initial=init, op0=ALU.mult, op1=ALU.add)
    Ur = GI
    p.op("dve", "tensor_tensor", reads=[GI, Fr], writes=[Ur], out=Ur[:], in0=GI[:], in1=Fr[:], op=ALU.subtract)
    Gr = GF
    for s0 in range(0, S, SCN):
        init = 0.0 if s0 == 0 else Gr[:, s0 - 1:s0]
        p.op("dve", "tensor_tensor_scan", reads=[Ur, zeror, Gr], writes=[Gr], out=Gr[:, s0:s0 + SCN], data0=Ur[:, s0:s0 + SCN], data1=zeror[:, 0:SCN],
             initial=init, op0=ALU.max, op1=ALU.add)
    NM = Fr
    p.op("dve", "tensor_tensor", reads=[Fr, Gr], writes=[NM], out=NM[:], in0=Fr[:], in1=Gr[:], op=ALU.add)
    p.op("dve", "tensor_scalar", reads=[NM], writes=[NM], out=NM[:], in0=NM[:], scalar1=-1.0, scalar2=None, op0=ALU.mult)
    p.op("dve", "tensor_scalar", reads=[Gr], writes=[Gr], out=Gr[:], in0=Gr[:], scalar1=-1.0, scalar2=None, op0=ALU.mult)
    Ucol = p.sb([128, NB, 4], F32, "Ucol")
    NMcol = p.sb([128, NB, 4], F32, "NMcol")
    for src, dst in ((Ur, Ucol), (NM, NMcol)):
        for b0 in range(0, NB, 64):
            nb_ = min(64, NB - b0)
            ps = PS[6]
            for b in range(nb_):
                p.op("pe", "transpose", reads=[src, ident], writes=[ps], out=ps[:, b * 4:(b + 1) * 4], in_=src[:, (b0 + b) * 128:(b0 + b + 1) * 128], identity=ident[0:4, 0:4])
            p.op("dve", "tensor_copy", reads=[ps], writes=[dst], out=dst[:, b0:b0 + nb_, :], in_=ps[:, 0:nb_ * 4].rearrange("p (b h) -> p b h", h=4))
    p.op("act", "activation", reads=[NMcol], writes=[NMcol], out=NMcol[:], in_=NMcol[:], func=AF.Exp)
    sel = p.sb([4, 4, 128], F32, "sel")
    p.op("pool", "memset", writes=[sel], ap=sel[:], constant=1.0)
    p.op("pool", "affine_select", reads=[sel], writes=[sel], out=sel[:], in_=sel[:], pattern=[[-1, 4], [0, 128]],
         compare_op=ALU.is_equal, fill=0.0, base=0, channel_multiplier=1)
    TB = min(512, S)
    NTS = TB // 128
    cms = p.sb([128, NTS, TB], F32, "cms")
    p.op("pool", "memset", writes=[cms], ap=cms[:], constant=1.0)
    for r_ in range(NTS):
        p.op("pool", "affine_select", reads=[cms], writes=[cms], out=cms[:, r_, :], in_=cms[:, r_, :], pattern=[[1, TB]], compare_op=ALU.is_ge,
             fill=0.0, base=-128 * r_, channel_multiplier=-1)
    onesb = p.sb([128, 1], BF16, "onesb")
    p.op("pool", "memset", writes=[onesb], ap=onesb[:], constant=1.0)
    qT = p.sb([128, 2, TB], BF16, "qT")
    kT = [p.sb([128, 2, 128], BF16, "kT") for _ in range(2)]
    Vs = [p.sb([128, MDV], BF16, "Vs") for _ in range(2)]
    ngb = p.sb([128, TB], F32, "ngb")
    Ex = [p.sb([128, TB], F32, "Ex") for _ in range(2)]
    Pb = [p.sb([128, TB], BF16, "Pb") for _ in range(2)]
    ng_w = p.sb([128, MDV], F32, "ng_w")
    hr = p.sb([128, MDV], F32, "hr")
    sqh = p.sb([128, MDV], F32, "sqh")
    sot = p.sb([128, MDV], F32, "sot")
    dn = p.sb([128, 8], F32, "dn")
    ssd = p.sb([128, 2], F32, "ssd")
    pnum = [PS[i] for i in range(4)]
    pden = PS[4]
    pdt = PS[7]
    drow = p.sb([1, TB], F32, "drow")
    for h in range(MH):
        bcast_row(p, ng_w, W["norm_g"][h * MDV:(h + 1) * MDV], MDV)
        for tb in range(S // TB):
            tcols = slice(tb * TB, (tb + 1) * TB)
            p.dma("sp", qT[:], QK[h * MDK:(h + 1) * MDK, tcols].rearrange("(kt p) t -> p kt t", p=128), writes=[qT])
            ps = PS[7]
            p.mm(ps, ps[:, 0:TB], sel, sel[:, h, :], Gr, Gr[:, tcols])
            p.op("act", "copy", reads=[ps], writes=[ngb], out=ngb[:], in_=ps[:, 0:TB])
            nsb = (tb + 1) * NTS
            for sb in range(nsb):
                i2 = sb % 2
                p.dma("sp", kT[i2][:], QK[1024 + h * MDK:1024 + (h + 1) * MDK, sb * 128:(sb + 1) * 128].rearrange("(kt p) t -> p kt t", p=128), writes=[kT[i2]])
                p.dma("sp", Vs[i2][:], Vb[sb * 128:(sb + 1) * 128, h * MDV:(h + 1) * MDV], writes=[Vs[i2]])
                pst = PS[5 + i2]
                for kt in range(2):
                    p.mm(pst, pst[:, 0:TB], kT[i2], kT[i2][:, kt, :], qT, qT[:, kt, :], start=(kt == 0), stop=(kt == 1))
                e = Ex[i2]
                p.op("dve", "tensor_scalar", reads=[ngb, Ucol], writes=[e], out=e[:], in0=ngb[:], scalar1=Ucol[:, sb, h:h + 1], scalar2=0.0, op0=ALU.add, op1=ALU.min)
                p.op("act", "activation", reads=[e], writes=[e], out=e[:], in_=e[:], func=AF.Exp)
                r = sb - tb * NTS
                if r >= 0:
                    p.op("pool", "tensor_tensor", reads=[e, cms], writes=[e], out=e[:], in0=e[:], in1=cms[:, r, :], op=ALU.mult)
                pb = Pb[i2]
                p.op("dve", "tensor_tensor", reads=[pst, e], writes=[pb], out=pb[:], in0=pst[:, 0:TB], in1=e[:], op=ALU.mult)
                p.mm(pden, pden[0:1, 0:TB], onesb, onesb[:], pb, pb[:], start=(sb == 0), stop=(sb == nsb - 1))
                for ts in range(max(r, 0), NTS):
                    last = (sb == tb * NTS + ts)
                    p.mm(pnum[ts], pnum[ts][:, :], pb, pb[:, ts * 128:(ts + 1) * 128], Vs[i2], Vs[i2][:], start=(sb == 0), stop=last)
            p.op("act", "activation", reads=[pden], writes=[drow], out=drow[:, 0:TB], in_=pden[0:1, 0:TB], func=AF.Abs)
            for ts in range(NTS):
                p.op("pe", "transpose", reads=[drow, ident], writes=[pdt], out=pdt[:, ts:ts + 1], in_=drow[0:1, ts * 128:(ts + 1) * 128], identity=ident[0:1, 0:1])
            for ts in range(NTS):
                blk = tb * NTS + ts
                rows = slice(blk * 128, (blk + 1) * 128)
                p.op("act", "copy", reads=[pdt], writes=[dn], out=dn[:, 0:1], in_=pdt[:, ts:ts + 1])
                p.op("dve", "tensor_tensor", reads=[dn, NMcol], writes=[dn], out=dn[:, 1:2], in0=dn[:, 0:1], in1=NMcol[:, blk, h:h + 1], op=ALU.max)
                p.op("dve", "reciprocal", reads=[dn], writes=[dn], out=dn[:, 2:3], in_=dn[:, 1:2])
                p.op("dve", "tensor_scalar", reads=[pnum[ts], dn], writes=[hr], out=hr[:], in0=pnum[ts][:], scalar1=dn[:, 2:3], scalar2=None, op0=ALU.mult)
                p.dma("sp", sot[:], SO[rows, h * MDV:(h + 1) * MDV], writes=[sot])
                rmsnorm_tile(p, hr, hr[:], ng_w, hr, hr[:], sqh, ssd, d=MDV)
                p.op("dve", "tensor_tensor", reads=[hr, sot], writes=[sqh], out=sqh[:], in0=hr[:], in1=sot[:], op=ALU.mult)
                p.dma("pool", MO[rows, h * MDV:(h + 1) * MDV], sqh[:], reads=[sqh])
    p.pop()
    p.push()
    def make_tile(r0, mo):
        p.dma("sp", mo[:], MO[r0:r0 + 128, :], writes=[mo])
    out_proj_phase(c, X, S, make_tile, W["w_o"])
    p.pop()


def dn_ss(p, dn):
    return T(dn.ap[:, 4:6], "dnss")


DH = 128
NHD = 16
NEG = -1.0e30


def dsa_layer(c, X, S, W, sc, dbg=None):
    p = c.p
    PS = c.PS
    ident = c.ident
    TG = min(512, S)
    NT = TG // 128
    NB = S // 128
    Win = W["w_in"]
    QT, QIT, MO = sc["QT"], sc["QIT"], sc["MO"]
    p.push()
    kT = p.sb([128, S], BF16, "kT")
    kiT = p.sb([64, S], BF16, "kiT")
    Vs = p.sb([128, NB, 132], BF16, "Vs")
    wis = p.sb([128, NB, 16], F32, "wis")
    p.op("pool", "memset", writes=[Vs], ap=Vs[:, :, 128:129], constant=1.0)
    p.push()
    wl = WLoader(c)
    g_bc = p.sb([128, D], F32, "g_bc")
    bcast_row(p, g_bc, W["mix_g"], D)
    qg = p.sb([128, DH], F32, "qg")
    kg = p.sb([128, DH], F32, "kg")
    bcast_row(p, qg, W["q_norm_g"], DH)
    bcast_row(p, kg, W["k_norm_g"], DH)
    xt = p.sb([128, D], F32, "xt")
    hn = p.sb([128, D], F32, "hn")
    ss = p.sb([128, 2], F32, "ss")
    hT = p.sb([128, KT, TG], BF16, "hT")
    qs = p.sb([128, 512], F32, "qs")
    qsq = p.sb([128, 512], F32, "qsq")
    st4 = p.sb([128, 8], F32, "st4")
    qTg = p.sb([128, NHD, TG], BF16, "qTg")
    qi_sb = [p.sb([128, TG], BF16, "qi_sb") for _ in range(2)]
    for t0 in range(0, S, TG):
        for ti in range(NT):
            p.dma("sp", xt[:], X[t0 + ti * 128:t0 + (ti + 1) * 128, :], writes=[xt])
            rmsnorm_tile(p, xt, xt[:], g_bc, hn, hn[:], hn, ss)
            to_fm(c, hn, lambda kt: hn[:, kt * 128:(kt + 1) * 128], KT, hT, ti * 128)
        for nb in range(4):
            wb = wl.load(Win, 0, D, nb * 512, 512)
            for ti in range(NT):
                ps = PS[ti % 2]
                for kt in range(KT):
                    p.mm(ps, ps[:, :], hT, hT[:, kt, ti * 128:(ti + 1) * 128], wb, wb[:, kt, :], start=(kt == 0), stop=(kt == KT - 1))
                p.op("act", "copy", reads=[ps], writes=[qs], out=qs[:], in_=ps[:])
                p.op("act", "activation", reads=[qs], writes=[qsq], out=qsq[:], in_=qs[:], func=AF.Square)
                p.op("dve", "reduce_sum", reads=[qsq], writes=[st4], out=st4[:, 0:4], in_=qsq[:].rearrange("p (h d) -> p h d", d=DH), axis=AX.X)
                p.op("dve", "tensor_scalar", reads=[st4], writes=[st4], out=st4[:, 0:4], in0=st4[:, 0:4], scalar1=1.0 / DH, scalar2=EPS, op0=ALU.mult, op1=ALU.add)
                p.op("act", "activation", reads=[st4], writes=[st4], out=st4[:, 0:4], in_=st4[:, 0:4], func=AF.Sqrt)
                p.op("dve", "reciprocal", reads=[st4], writes=[st4], out=st4[:, 4:8], in_=st4[:, 0:4])
                p.op("dve", "tensor_scalar", reads=[st4], writes=[st4], out=st4[:, 4:8], in0=st4[:, 4:8], scalar1=DH ** -0.5, scalar2=None, op0=ALU.mult)
                q3 = qs[:].rearrange("p (h d) -> p h d", d=DH)
                p.op("dve", "tensor_tensor", reads=[qs, st4], writes=[qs], out=q3, in0=q3, in1=st4[:, 4:8].unsqueeze(2).to_broadcast([128, 4, DH]), op=ALU.mult)
                p.op("dve", "tensor_tensor", reads=[qs, qg], writes=[qs], out=q3, in0=q3, in1=qg[:].unsqueeze(1).to_broadcast([128, 4, DH]), op=ALU.mult)
                pst = PS[6 + ti % 2]
                for j in range(4):
                    p.op("pe", "transpose", reads=[qs, ident], writes=[pst], out=pst[:, j * 128:(j + 1) * 128], in_=qs[:, j * 128:(j + 1) * 128], identity=ident[:])
                p.op("dve", "tensor_copy", reads=[pst], writes=[qTg], out=qTg[:, nb * 4:(nb + 1) * 4, ti * 128:(ti + 1) * 128],
                     in_=pst[:].rearrange("p (a b) -> p a b", a=4))
        for hh in range(NHD):
            p.dma("pool", QT[hh * 128:(hh + 1) * 128, t0:t0 + TG], qTg[:, hh, :], reads=[qTg])
        wb = wl.load(Win, 0, D, 2048, 256)
        for ti in range(NT):
            blk = (t0 // 128) + ti
            ps = PS[2 + ti % 2]
            for kt in range(KT):
                p.mm(ps, ps[:, 0:256], hT, hT[:, kt, ti * 128:(ti + 1) * 128], wb, wb[:, kt, 0:256], start=(kt == 0), stop=(kt == KT - 1))
            p.op("act", "copy", reads=[ps], writes=[Vs], out=Vs[:, blk, 0:128], in_=ps[:, 128:256])
            p.op("act", "copy", reads=[ps], writes=[qs], out=qs[:, 0:128], in_=ps[:, 0:128])
            rmsnorm_tile(p, qs, qs[:, 0:128], kg, qs, qs[:, 0:128], qsq, ss, d=DH)
            pst = PS[6 + ti % 2]
            p.op("pe", "transpose", reads=[qs, ident], writes=[pst], out=pst[:, 0:128], in_=qs[:, 0:128], identity=ident[:])
            p.op("dve", "tensor_copy", reads=[pst], writes=[kT], out=kT[:, blk * 128:(blk + 1) * 128], in_=pst[:, 0:128])
        for nb in range(2):
            wb = wl.load(Win, 0, D, 2304 + nb * 512, 512)
            for j in range(4):
                ps = PS[j % 2]
                for kt in range(KT):
                    p.mm(ps, ps[:, 0:TG], wb, wb[:, kt, j * 128:(j + 1) * 128], hT, hT[:, kt, :], start=(kt == 0), stop=(kt == KT - 1))
                o = qi_sb[j % 2]
                p.op("act", "copy", reads=[ps], writes=[o], out=o[:, 0:TG], in_=ps[:, 0:TG])
                r0 = nb * 512 + j * 128
                p.dma("pool", QIT[r0:r0 + 128, t0:t0 + TG], o[:, 0:TG], reads=[o])
        wb = wl.load(Win, 0, D, 3328, 80)
        ps = PS[4]
        for kt in range(KT):
            p.mm(ps, ps[0:64, 0:TG], wb, wb[:, kt, 0:64], hT, hT[:, kt, :], start=(kt == 0), stop=(kt == KT - 1))
        p.op("act", "copy", reads=[ps], writes=[kiT], out=kiT[:, t0:t0 + TG], in_=ps[0:64, 0:TG])
        for ti in range(NT):
            blk = (t0 // 128) + ti
            ps = PS[5]
            for kt in range(KT):
                p.mm(ps, ps[:, 0:16], hT, hT[:, kt, ti * 128:(ti + 1) * 128], wb, wb[:, kt, 64:80], start=(kt == 0), stop=(kt == KT - 1))
            p.op("act", "activation", reads=[ps], writes=[wis], out=wis[:, blk, :], in_=ps[:, 0:16], func=AF.Copy, scale=1.0 / 32.0)
    p.pop()
    p.push()
    NBt = p.sb([128, 2, NHD, 128], F32, "NBt")
    CB = p.sb([128, NHD], F32, "CB")
    p.dma("sp", NBt[:], W["t5_bt"], writes=[NBt])
    p.dma("sp", CB[:], W["t5_cb"], writes=[CB])
    for k2 in range(2):
        p.op("dve", "tensor_tensor", reads=[NBt, CB], writes=[NBt], out=NBt[:, k2, :, :], in0=NBt[:, k2, :, :],
             in1=CB[:].unsqueeze(2).to_broadcast([128, NHD, 128]), op=ALU.subtract)
    identb = p.sb([128, 128], BF16, "identb")
    p.op("pool", "tensor_copy", reads=[ident], writes=[identb], out=identb[:], in_=ident[:])
    SC = p.sb([128, S], F32, "SC")
    CM = p.sb([128, 128], F32, "CM")
    p.op("pool", "memset", writes=[CM], ap=CM[:], constant=0.0)
    p.op("pool", "affine_select", reads=[CM], writes=[CM], out=CM[:], in_=CM[:], pattern=[[-1, 128]], compare_op=ALU.is_ge,
         fill=NEG, base=0, channel_multiplier=1)
    Wk = p.sb([128, S], F32, "Wk")
    m8 = p.sb([128, 16], F32, "m8")
    Mk = [p.sb([128, 512], BF16, "Mk") for _ in range(2)]
    MT = p.sb([128, NB, 128], BF16, "MT")
    qTb = p.sb([128, NHD, 128], BF16, "qTb")
    qiTb = p.sb([64, NHD, 128], BF16, "qiTb")
    rl = [p.sb([128, 512], F32, "rl") for _ in range(2)]
    Lf = [p.sb([128, 512], F32, "Lf") for _ in range(2)]
    Pe = [p.sb([128, 512], BF16, "Pe") for _ in range(2)]
    Pm = [p.sb([128, 512], BF16, "Pm") for _ in range(2)]
    mo = p.sb([128, D], F32, "mo")
    rd = p.sb([128, 16], F32, "rd")
    acc = [PS[i] for i in range(6)]
    for qb in range(NB):
        nkb = qb + 1
        ncols = nkb * 128
        tcols = slice(qb * 128, (qb + 1) * 128)
        p.dma("sp", qTb[:], QT[:, tcols].rearrange("(h d) t -> d h t", d=128), writes=[qTb])
        p.dma("sp", qiTb[:], QIT[:, tcols].rearrange("(h d) t -> d h t", d=64), writes=[qiTb])
        for c0 in range(0, ncols, 512):
            cw = min(512, ncols - c0)
            for hh in range(NHD):
                ps = PS[6 + hh % 2]
                p.mm(ps, ps[:, 0:cw], qiTb, qiTb[:, hh, :], kiT, kiT[:, c0:c0 + cw])
                r_ = rl[hh % 2]
                p.op("act", "activation", reads=[ps], writes=[r_], out=r_[:, 0:cw], in_=ps[:, 0:cw], func=AF.Relu)
                if hh == 0:
                    p.op("dve", "tensor_scalar", reads=[r_, wis], writes=[SC], out=SC[:, c0:c0 + cw], in0=r_[:, 0:cw], scalar1=wis[:, qb, 0:1], scalar2=None, op0=ALU.mult)
                else:
                    p.op("dve", "scalar_tensor_tensor", reads=[r_, wis, SC], writes=[SC], out=SC[:, c0:c0 + cw], in0=r_[:, 0:cw], scalar=wis[:, qb, hh:hh + 1],
                         in1=SC[:, c0:c0 + cw], op0=ALU.mult, op1=ALU.add)
        p.op("pool", "tensor_tensor", reads=[SC, CM], writes=[SC], out=SC[:, qb * 128:(qb + 1) * 128], in0=SC[:, qb * 128:(qb + 1) * 128],
             in1=CM[:], op=ALU.add)
        NSEL = min(256, S // 4)
        if ncols > NSEL:
            src = SC
            for r in range(NSEL // 8):
                p.op("dve", "max", reads=[src], writes=[m8], out=m8[:, 0:8], in_=src[:, 0:ncols])
                if r < NSEL // 8 - 1:
                    p.op("dve", "match_replace", reads=[m8, src], writes=[Wk], out=Wk[:, 0:ncols], in_to_replace=m8[:, 0:8], in_values=src[:, 0:ncols], imm_value=NEG)
                    src = Wk
            p.op("dve", "tensor_reduce", reads=[m8], writes=[m8], out=m8[:, 8:9], in_=m8[:, 0:8], axis=AX.X, op=ALU.min)
        else:
            p.op("pool", "memset", writes=[m8], ap=m8[:, 8:9], constant=-1.0e29)
        if dbg is not None and qb == 1:
            p.dma("pool", dbg["SC"][:, :], SC[:, 0:256], reads=[SC])
            p.dma("pool", dbg["m8"][:, :], m8[:], reads=[m8])
        for c0 in range(0, ncols, 512):
            cw = min(512, ncols - c0)
            i2 = (c0 // 512) % 2
            p.op("dve", "tensor_scalar", reads=[SC, m8], writes=[Mk[i2]], out=Mk[i2][:, 0:cw], in0=SC[:, c0:c0 + cw], scalar1=m8[:, 8:9], scalar2=None, op0=ALU.is_ge)
            pst = PS[6 + i2]
            pv = pst[:].bitcast(BF16)
            for j in range(cw // 128):
                p.op("pe", "transpose", reads=[Mk[i2], identb], writes=[pst], out=pv[:, j * 128:(j + 1) * 128], in_=Mk[i2][:, j * 128:(j + 1) * 128], identity=identb[:])
            p.op("act", "copy", reads=[pst], writes=[MT], out=MT[:, c0 // 128:c0 // 128 + cw // 128, :], in_=pv[:, 0:cw].rearrange("p (a b) -> p a b", b=128))
        if dbg is not None and qb == 1:
            p.dma("pool", dbg["MT"][:, :, :], MT[:, 0:2, :], reads=[MT])
        ixc = 0
        for (h0, h1) in ((0, 6), (6, 12), (12, 16)):
            for kb in range(nkb):
                near = (qb - kb) <= 1
                for c0 in range(h0, h1, 4):
                    c1 = min(c0 + 4, h1)
                    nhc = c1 - c0
                    w_ = nhc * 128
                    ix = ixc % 2
                    ixc += 1
                    ps = PS[6 + ix]
                    p.mm(ps, ps[:, 0:w_], kT, kT[:, kb * 128:(kb + 1) * 128], qTb, qTb[:, c0:c1, :].rearrange("p a b -> p (a b)"))
                    pe_ = Pe[ix]
                    if near:
                        p.op("dve", "tensor_tensor", reads=[ps, NBt], writes=[Lf[ix]], out=Lf[ix][:, 0:w_], in0=ps[:, 0:w_],
                             in1=NBt[:, qb - kb, c0:c1, :].rearrange("p a b -> p (a b)"), op=ALU.add)
                        p.op("act", "activation", reads=[Lf[ix]], writes=[pe_], out=pe_[:, 0:w_], in_=Lf[ix][:, 0:w_], func=AF.Exp)
                    else:
                        p.op("act", "activation", reads=[ps], writes=[pe_], out=pe_[:, 0:w_], in_=ps[:, 0:w_], func=AF.Exp)
                    pm_ = Pm[ix]
                    p.op("pool", "tensor_tensor", reads=[pe_, MT], writes=[pm_], out=pm_[:, 0:w_].rearrange("p (a b) -> p a b", a=nhc),
                         in0=pe_[:, 0:w_].rearrange("p (a b) -> p a b", a=nhc), in1=MT[:, kb, :].unsqueeze(1).to_broadcast([128, nhc, 128]), op=ALU.mult)
                    for j in range(nhc):
                        a_ = acc[c0 + j - h0]
                        p.mm(a_, a_[:, 0:129], pm_, pm_[:, j * 128:(j + 1) * 128], Vs, Vs[:, kb, 0:129], start=(kb == 0), stop=(kb == nkb - 1))
            for hh in range(h0, h1):
                a_ = acc[hh - h0]
                p.op("dve", "reciprocal", reads=[a_], writes=[rd], out=rd[:, hh:hh + 1], in_=a_[:, 128:129])
                p.op("dve", "tensor_scalar", reads=[a_, rd], writes=[mo], out=mo[:, hh * 128:(hh + 1) * 128], in0=a_[:, 0:128], scalar1=rd[:, hh:hh + 1], scalar2=None, op0=ALU.mult)
        p.dma("pool", MO[tcols, :], mo[:], reads=[mo])
    p.pop()
    p.pop()
    p.push()
    def make_tile(r0, mo_):
        p.dma("sp", mo_[:], MO[r0:r0 + 128, :], writes=[mo_])
    out_proj_phase(c, X, S, make_tile, W["w_o"])
    p.pop()


def t5_tables(t5_bias):
    n = np.arange(256)
    nf = np.maximum(n, 16).astype(np.float32)
    large = 16 + (np.log(nf / np.float32(16)) / np.float32(np.log(8.0)) * np.float32(16)).astype(np.int32)
    bucket = np.where(n < 16, n, np.minimum(large, 31))
    s = np.arange(128)[:, None]
    t = np.arange(128)[None, :]
    bt = np.empty((128, 2, 16, 128), np.float32)
    for k2 in range(2):
        d = np.clip(t - s + 128 * k2, 0, 255)
        bt[:, k2] = np.transpose(t5_bias[bucket[d]], (0, 2, 1))
    cb = np.ascontiguousarray(np.broadcast_to(t5_bias[31][None, :], (128, 16))).astype(np.float32)
    return bt, cb


SEQ = 8192
BATCH = 4
N_CORES = 4


def build_model(S):
    p = Prog()
    c = setup_consts(p)
    din = p.dram_in
    x = din("x", [S, D])
    A = {}
    shapes = {
        "rwkv_mu_fm": [2, 128, 96], "rwkv_w_rkv": [2, 3, D, D], "rwkv_w0": [2, D], "rwkv_w1": [2, D, 96], "rwkv_w2": [2, 96, D],
        "rwkv_a0": [2, D], "rwkv_a1": [2, D, 96], "rwkv_a2": [2, 96, D], "rwkv_v0": [1, D], "rwkv_v1": [1, D, 64], "rwkv_v2": [1, 64, D],
        "rwkv_g1": [2, D, 256], "rwkv_g2": [2, 256, D], "rwkv_k_k": [2, D], "rwkv_k_a": [2, D], "rwkv_r_k": [2, D],
        "rwkv_lnx_g": [2, D], "rwkv_lnx_b": [2, D], "rwkv_w_o": [2, D, D],
        "mlstm_w_in": [1, D, 6152], "mlstm_b_if": [1, 2, 4], "mlstm_norm_g": [1, D], "mlstm_w_o": [1, D, D],
        "dsa_w_in": [1, D, 3408], "dsa_q_norm_g": [1, 128], "dsa_k_norm_g": [1, 128], "dsa_w_o": [1, D, D],
        "t5_bt": [128, 2, 16, 128], "t5_cb": [128, 16],
        "mix_norm_g": [4, D], "ffn_norm_g": [4, D], "ffn_w_gate": [4, D, DFF], "ffn_w_up": [4, D, DFF], "ffn_w_down": [4, DFF, D],
    }
    for k, v in shapes.items():
        A[k] = din(k, v)
    y = p.dram_out("y", [S, D])
    sc = {k: p.dram_tmp("sc_" + k, [S, D]) for k in RW_SC}
    sc["QK"] = p.dram_tmp("sc_QK", [2048, S], BF16)
    sc["Vb"] = p.dram_tmp("sc_Vb", [S, D], BF16)
    sc["SO"] = sc["R"]
    sc["MO"] = sc["Y"]
    sc["QT"] = sc["QK"]
    sc["QIT"] = p.dram_tmp("sc_QIT", [1024, S], BF16)
    MATS = ["rwkv_w_rkv", "rwkv_w1", "rwkv_w2", "rwkv_a1", "rwkv_a2", "rwkv_v1", "rwkv_v2", "rwkv_g1", "rwkv_g2", "rwkv_w_o",
            "mlstm_w_in", "mlstm_w_o", "dsa_w_in", "dsa_w_o", "ffn_w_gate", "ffn_w_up", "ffn_w_down"]
    pairs = []
    for k in MATS:
        shp = shapes[k]
        b = p.dram_tmp("bf_" + k, shp, BF16)
        R = int(np.prod(shp[:-1]))
        pat = {3: "a k n -> (a k) n", 4: "a b k n -> (a b k) n"}[len(shp)]
        pairs.append((A[k].rearrange(pat), b.rearrange(pat), R, shp[-1]))
        A[k] = b
    convert_weights(c, pairs)
    TR = min(S, 1024)
    for r0 in range(0, S, TR):
        p.dma("sp", y[r0:r0 + TR, :], x[r0:r0 + TR, :])
    p.barrier()
    for i in range(4):
        kind, j = i % 3, i // 3
        if kind == 0:
            W = {"mix_g": A["mix_norm_g"][i], "mu_fm": A["rwkv_mu_fm"][j], "w_rkv": A["rwkv_w_rkv"][j]}
            for nm in ["w0", "w1", "w2", "a0", "a1", "a2", "g1", "g2", "k_k", "k_a", "r_k", "lnx_g", "lnx_b", "w_o"]:
                W[nm] = A["rwkv_" + nm][j]
            if j > 0:
                for nm in ["v0", "v1", "v2"]:
                    W[nm] = A["rwkv_" + nm][j - 1]
            rwkv_prep(c, y, S, W, sc, j == 0)
            rwkv_scan(c, S, sc)
            rwkv_post(c, y, S, W, sc)
        elif kind == 1:
            W = {"mix_g": A["mix_norm_g"][i], "w_in": A["mlstm_w_in"][j], "b_if": A["mlstm_b_if"][j], "norm_g": A["mlstm_norm_g"][j], "w_o": A["mlstm_w_o"][j]}
            mlstm_layer(c, y, S, W, sc)
        else:
            W = {"mix_g": A["mix_norm_g"][i], "w_in": A["dsa_w_in"][j], "q_norm_g": A["dsa_q_norm_g"][j], "k_norm_g": A["dsa_k_norm_g"][j],
                 "w_o": A["dsa_w_o"][j], "t5_bt": A["t5_bt"], "t5_cb": A["t5_cb"]}
            dsa_layer(c, y, S, W, sc)
        ffn_phase(c, y, S, A["ffn_norm_g"][i], A["ffn_w_gate"][i], A["ffn_w_up"][i], A["ffn_w_down"][i])
    return p


def host_inputs(inputs):
    f = lambda a: np.ascontiguousarray(np.asarray(a, dtype=np.float32))
    feed = {}
    mu = f(inputs["rwkv_mu"])
    feed["rwkv_mu_fm"] = np.ascontiguousarray(mu.reshape(mu.shape[0], 6, 16, 128).transpose(0, 3, 1, 2).reshape(mu.shape[0], 128, 96))
    for k in ["rwkv_w_rkv", "rwkv_w0", "rwkv_w1", "rwkv_w2", "rwkv_a0", "rwkv_a1", "rwkv_a2", "rwkv_v0", "rwkv_v1", "rwkv_v2", "rwkv_g1", "rwkv_g2",
              "rwkv_k_k", "rwkv_k_a", "rwkv_lnx_g", "rwkv_lnx_b", "rwkv_w_o", "mlstm_w_in", "mlstm_b_if", "mlstm_norm_g", "mlstm_w_o",
              "dsa_w_in", "dsa_q_norm_g", "dsa_k_norm_g", "dsa_w_o", "mix_norm_g", "ffn_norm_g", "ffn_w_gate", "ffn_w_up", "ffn_w_down"]:
        feed[k] = f(inputs[k])
    rk = f(inputs["rwkv_r_k"])
    feed["rwkv_r_k"] = rk.reshape(rk.shape[0], -1)
    bt, cb = t5_tables(f(inputs["t5_bias"]))
    feed["t5_bt"] = bt
    feed["t5_cb"] = cb
    return feed


_CACHE = {}


def kernel(**inputs):
    x = np.asarray(inputs["x"], dtype=np.float32)
    B, S, _ = x.shape
    if S not in _CACHE:
        _CACHE[S] = build_model(S).finish()
    nc = _CACHE[S]
    feed = host_inputs(inputs)
    in_maps = []
    for b in range(B):
        m = dict(feed)
        m["x"] = np.ascontiguousarray(x[b])
        in_maps.append(m)
    res = run_bass_kernel_spmd(nc, in_maps, core_ids=list(range(B)))
    return np.stack([np.asarray(res.results[b]["y"], dtype=np.float32) for b in range(B)], axis=0)
```

```python
import contextlib
import numpy as np
import concourse.bass as bass
import concourse.mybir as mybir
from concourse.bass_utils import run_bass_kernel_spmd

F32 = mybir.dt.float32
BF16 = mybir.dt.bfloat16
ALU = mybir.AluOpType
AF = mybir.ActivationFunctionType
AX = mybir.AxisListType


class T:
    __slots__ = ("ap", "lw", "rd", "name", "root", "excl")

    def __init__(self, ap, name="", root=None):
        self.ap = ap
        self.lw = None
        self.rd = {}
        self.name = name
        self.root = root.root if root is not None else self
        self.excl = False

    def sub(self, rows, a, b):
        return T(self.ap[0:rows, a:b], self.name + "_v", root=self)

    def __getitem__(self, idx):
        return self.ap[idx]


class Prog:
    ENGS = ("pe", "dve", "act", "pool", "sp")

    def __init__(self, name="k", n_dma_sems=12, same_engine_sync=True, nc=None):
        self.nc = nc if nc is not None else bass.Bass("TRN2", target_bir_lowering=False)
        self.es = contextlib.ExitStack()
        self.q = {e: [] for e in self.ENGS}
        self.cnt = {e: 0 for e in self.ENGS}
        self.seen = {e: {} for e in self.ENGS}
        self.sem = {}
        for e in self.ENGS:
            self.sem[e] = self.es.enter_context(self.nc.semaphore("s_" + e))
        self.dma_pool = {}
        self.dma_rr = {}
        self.dma_cnt = {}
        for qe in ("sp", "pool", "act"):
            names = []
            for j in range(n_dma_sems):
                k = "d_%s_%d" % (qe, j)
                self.sem[k] = self.es.enter_context(self.nc.semaphore(k))
                self.dma_cnt[k] = 0
                names.append(k)
            self.dma_pool[qe] = names
            self.dma_rr[qe] = 0
        self.same_engine_sync = same_engine_sync
        self.n_inst = 0
        self._uid = 0
        self.scopes = []

    def uid(self, p):
        self._uid += 1
        return "%s%d" % (p, self._uid)

    def push(self):
        self.scopes.append(contextlib.ExitStack())

    def pop(self):
        self.barrier()
        self.scopes.pop().close()

    def sb(self, shape, dt=F32, name=None):
        es = self.scopes[-1] if self.scopes else self.es
        nm = self.uid(name or "sb")
        h = es.enter_context(self.nc.sbuf_tensor(nm, list(shape), dt))
        return T(h, nm)

    def ps(self, shape, dt=F32, name=None):
        es = self.scopes[-1] if self.scopes else self.es
        nm = self.uid(name or "ps")
        h = es.enter_context(self.nc.psum_tensor(nm, list(shape), dt))
        t = T(h, nm)
        t.excl = True
        return t

    def barrier(self):
        targets = {}
        for e in self.ENGS:
            if self.cnt[e]:
                targets[e] = self.cnt[e]
        for k, c in self.dma_cnt.items():
            if c:
                targets[k] = 16 * c
        for e in self.ENGS:
            for k, v in targets.items():
                if self.seen[e].get(k, 0) < v:
                    self.q[e].append(("w", k, v))
                    self.seen[e][k] = v

    def dram_in(self, name, shape, dt=F32):
        return self.nc.dram_tensor(name, list(shape), dt, kind="ExternalInput").ap()

    def dram_out(self, name, shape, dt=F32):
        return self.nc.dram_tensor(name, list(shape), dt, kind="ExternalOutput").ap()

    def dram_tmp(self, name, shape, dt=F32):
        return self.nc.dram_tensor(name, list(shape), dt, kind="Internal").ap()

    def _deps(self, e, reads, writes):
        deps = {}

        def add(t):
            if t is None:
                return
            k, v = t
            if deps.get(k, 0) < v:
                deps[k] = v

        reads = [r.root for r in reads]
        writes = [w.root for w in writes] + [r for r in reads if r.excl]
        for r in reads:
            add(r.lw)
        for w in writes:
            add(w.lw)
            for t in w.rd.values():
                add(t)
        for k, v in deps.items():
            if k == e and (e == "pe" or not self.same_engine_sync):
                continue
            if self.seen[e].get(k, 0) < v:
                self.q[e].append(("w", k, v))
                self.seen[e][k] = v

    def _mark(self, ticket, reads, writes):
        reads = [r.root for r in reads]
        writes = [w.root for w in writes] + [r for r in reads if r.excl]
        for r in reads:
            r.rd[ticket[0]] = ticket
        for w in writes:
            w.lw = ticket
            w.rd = {}

    def op(self, e, meth, reads=(), writes=(), **kw):
        self._deps(e, reads, writes)
        self.cnt[e] += 1
        ticket = (e, self.cnt[e])
        self.q[e].append(("o", (meth, kw), e, 1))
        self._mark(ticket, reads, writes)
        self.n_inst += 1

    def dma(self, qe, out, in_, reads=(), writes=(), **kw):
        pool = self.dma_pool[qe]
        k = pool[self.dma_rr[qe] % len(pool)]
        self.dma_rr[qe] += 1
        self._deps(qe, reads, writes)
        prev = 16 * self.dma_cnt[k]
        if prev and self.seen[qe].get(k, 0) < prev:
            self.q[qe].append(("w", k, prev))
            self.seen[qe][k] = prev
        self.dma_cnt[k] += 1
        ticket = (k, 16 * self.dma_cnt[k])
        kw = dict(kw); kw["out"] = out; kw["in_"] = in_
        self.q[qe].append(("o", ("dma_start", kw), k, 16))
        self._mark(ticket, reads, writes)
        self.n_inst += 1

    def coll(self, kind, alu, ins, outs, groups):
        qe = "pool"
        pool = self.dma_pool[qe]
        k = pool[self.dma_rr[qe] % len(pool)]
        self.dma_rr[qe] += 1
        prev = 16 * self.dma_cnt[k]
        if prev and self.seen[qe].get(k, 0) < prev:
            self.q[qe].append(("w", k, prev))
            self.seen[qe][k] = prev
        self.dma_cnt[k] += 1
        self.q[qe].append(("o", ("collective_compute", dict(kind=kind, op=alu, replica_groups=groups, ins=ins, outs=outs)), k, 16))
        self.n_inst += 1

    def mm(self, out_t, out_ap, lhsT_t, lhsT_ap, rhs_t, rhs_ap, start=True, stop=True):
        self.op("pe", "matmul", reads=[lhsT_t, rhs_t], writes=[out_t],
                out=out_ap, lhsT=lhsT_ap, rhs=rhs_ap, start=start, stop=stop)

    def finish(self):
        for qe, pool in self.dma_pool.items():
            for k in pool:
                v = 16 * self.dma_cnt[k]
                if v and self.seen[qe].get(k, 0) < v:
                    self.q[qe].append(("w", k, v))
                    self.seen[qe][k] = v
        nc = self.nc
        engmap = {"pe": "tensor", "dve": "vector", "act": "scalar", "pool": "gpsimd", "sp": "sync"}
        with nc.Block() as block:
            for e in self.ENGS:
                items = self.q[e]
                if not items:
                    continue

                def body(eng, items=items):
                    for it in items:
                        if it[0] == "w":
                            eng.wait_ge(self.sem[it[1]], it[2])
                        else:
                            try:
                                ins = getattr(eng, it[1][0])(**it[1][1])
                            except Exception:
                                print("FAILED INSTR", it[1][0], {k: (str(v)[:300]) for k, v in it[1][1].items()})
                                raise
                            ins.then_inc(self.sem[it[2]], it[3])

                getattr(block, engmap[e])(body)
        self.es.close()
        return nc


D = 2048
DFF = 5632
KT = 16
EPS = 1e-6


class Ctx:
    pass


def setup_consts(p):
    c = Ctx()
    c.p = p
    c.ident = p.sb([128, 128], F32, "ident")
    p.op("pool", "memset", writes=[c.ident], ap=c.ident[:], constant=1.0)
    p.op("pool", "affine_select", reads=[c.ident], writes=[c.ident], out=c.ident[:], in_=c.ident[:],
         pattern=[[-1, 128]], compare_op=ALU.is_equal, fill=0.0, base=0, channel_multiplier=1)
    c.PS = [p.ps([128, 512], F32, "bank%d" % i) for i in range(8)]
    c.rr = 0
    return c


def bcast_row(p, dst, src_ap_1d, n, q="sp"):
    p.dma(q, dst[:, 0:n], src_ap_1d.partition_broadcast(128), writes=[dst])


def rmsnorm_tile(p, x_t, x_ap, g_t, out_t, out_ap, tmp_t, ss_t, d=D, eps=EPS):
    p.op("act", "activation", reads=[x_t], writes=[tmp_t], out=tmp_t[:, 0:d], in_=x_ap, func=AF.Square)
    p.op("dve", "reduce_sum", reads=[tmp_t], writes=[ss_t], out=ss_t[:, 0:1], in_=tmp_t[:, 0:d], axis=AX.X)
    p.op("dve", "tensor_scalar", reads=[ss_t], writes=[ss_t], out=ss_t[:, 1:2], in0=ss_t[:, 0:1],
         scalar1=1.0 / d, scalar2=eps, op0=ALU.mult, op1=ALU.add)
    p.op("act", "activation", reads=[ss_t], writes=[ss_t], out=ss_t[:, 1:2], in_=ss_t[:, 1:2], func=AF.Sqrt)
    p.op("dve", "reciprocal", reads=[ss_t], writes=[ss_t], out=ss_t[:, 0:1], in_=ss_t[:, 1:2])
    p.op("dve", "scalar_tensor_tensor", reads=[x_t, ss_t, g_t], writes=[out_t], out=out_ap, in0=x_ap,
         scalar=ss_t[:, 0:1], in1=g_t[:, 0:d], op0=ALU.mult, op1=ALU.mult)


def to_fm(c, src_t, src_ap_fn, nkt, dstT, col0, evac_alt=0):
    p = c.p
    for g in range(0, nkt, 4):
        n = min(4, nkt - g)
        ps = c.PS[6 + (c.rr % 2)]
        c.rr += 1
        for j in range(n):
            p.op("pe", "transpose", reads=[src_t, c.ident], writes=[ps], out=ps[:, j * 128:(j + 1) * 128],
                 in_=src_ap_fn(g + j), identity=c.ident[:])
        eng = "dve" if (c.rr % 2) else "act"
        src = ps[:, 0:n * 128].rearrange("p (a b) -> p a b", a=n)
        dst = dstT[:, g:g + n, col0:col0 + 128]
        if eng == "dve":
            p.op("dve", "tensor_copy", reads=[ps], writes=[dstT], out=dst, in_=src)
        else:
            p.op("act", "copy", reads=[ps], writes=[dstT], out=dst, in_=src)


class WLoader:
    def __init__(self, c, nbuf=4):
        p = c.p
        self.c = c
        self.wb = [p.sb([128, 16, 512], BF16, "wb") for _ in range(nbuf)]
        self.i = 0

    def load(self, W, k0, ksz, n0, nsz):
        p = self.c.p
        wb = self.wb[self.i % len(self.wb)]
        self.i += 1
        nk = ksz // 128
        p.dma("sp", wb[:, 0:nk, 0:nsz], W[k0:k0 + ksz, n0:n0 + nsz].rearrange("(kt p) n -> p kt n", p=128), writes=[wb])
        return wb


def convert_weights(c, pairs):
    p = c.p
    p.push()
    st = [p.sb([128, 4096], F32, "cst") for _ in range(3)]
    cb = [p.sb([128, 4096], BF16, "ccb") for _ in range(3)]
    n = 0
    for src, dst, R, C in pairs:
        for r0 in range(0, R, 128):
            rs = min(128, R - r0)
            for c0 in range(0, C, 4096):
                cs = min(4096, C - c0)
                a = st[n % 3]
                b = cb[n % 3]
                p.dma("sp", a[0:rs, 0:cs], src[r0:r0 + rs, c0:c0 + cs], writes=[a])
                eng = ("act", "dve", "pool")[n % 3]
                if eng == "act":
                    p.op("act", "copy", reads=[a], writes=[b], out=b[0:rs, 0:cs], in_=a[0:rs, 0:cs])
                else:
                    p.op(eng, "tensor_copy", reads=[a], writes=[b], out=b[0:rs, 0:cs], in_=a[0:rs, 0:cs])
                p.dma("pool", dst[r0:r0 + rs, c0:c0 + cs], b[0:rs, 0:cs], reads=[b])
                n += 1
    p.pop()


def ffn_phase(c, X, S, gvec, Wg, Wu, Wd):
    p = c.p
    TG = min(512, S)
    NT = TG // 128
    NF = DFF // 128
    p.push()
    wl = WLoader(c)
    g_bc = p.sb([128, D], F32, "g_bc")
    bcast_row(p, g_bc, gvec, D)
    xres = [p.sb([128, D], F32, "xres") for _ in range(NT)]
    hn = p.sb([128, D], F32, "hn")
    tmp = p.sb([128, D], F32, "tmp")
    ss = p.sb([128, 2], F32, "ss")
    hT = p.sb([128, KT, TG], BF16, "hT")
    hidT = p.sb([128, NF, TG], BF16, "hidT")
    sg = [p.sb([128, TG], F32, "sg") for _ in range(2)]
    ot = [p.sb([128, 512], F32, "ot") for _ in range(2)]
    for t0 in range(0, S, TG):
        for ti in range(NT):
            p.dma("sp", xres[ti][:], X[t0 + ti * 128:t0 + (ti + 1) * 128, :], writes=[xres[ti]])
            rmsnorm_tile(p, xres[ti], xres[ti][:], g_bc, hn, hn[:], tmp, ss)
            to_fm(c, hn, lambda kt: hn[:, kt * 128:(kt + 1) * 128], KT, hT, ti * 128)
        for fb in range(0, NF, 4):
            nf = min(4, NF - fb)
            wg = wl.load(Wg, 0, D, fb * 128, nf * 128)
            wu = wl.load(Wu, 0, D, fb * 128, nf * 128)
            for j in range(nf):
                fc = fb + j
                pg = c.PS[(fc % 2) * 2]
                pu = c.PS[(fc % 2) * 2 + 1]
                for kt in range(KT):
                    p.mm(pg, pg[:, 0:TG], wg, wg[:, kt, j * 128:(j + 1) * 128], hT, hT[:, kt, :], start=(kt == 0), stop=(kt == KT - 1))
                for kt in range(KT):
                    p.mm(pu, pu[:, 0:TG], wu, wu[:, kt, j * 128:(j + 1) * 128], hT, hT[:, kt, :], start=(kt == 0), stop=(kt == KT - 1))
                s_ = sg[fc % 2]
                p.op("act", "activation", reads=[pg], writes=[s_], out=s_[:, 0:TG], in_=pg[:, 0:TG], func=AF.Silu)
                p.op("dve", "tensor_tensor", reads=[s_, pu], writes=[hidT], out=hidT[:, fc, :], in0=s_[:, 0:TG], in1=pu[:, 0:TG], op=ALU.mult)
        for nch in range(D // 512):
            for kb in range(0, NF, 16):
                nk = min(16, NF - kb)
                wd = wl.load(Wd, kb * 128, nk * 128, nch * 512, 512)
                for ti in range(NT):
                    ps = c.PS[ti]
                    for kt in range(nk):
                        p.mm(ps, ps[:, :], hidT, hidT[:, kb + kt, ti * 128:(ti + 1) * 128], wd, wd[:, kt, :],
                             start=(kb + kt == 0), stop=(kb + kt == NF - 1))
            for ti in range(NT):
                o = ot[ti % 2]
                p.op("dve", "tensor_tensor", reads=[c.PS[ti], xres[ti]], writes=[o], out=o[:], in0=c.PS[ti][:],
                     in1=xres[ti][:, nch * 512:(nch + 1) * 512], op=ALU.add)
                p.dma("pool", X[t0 + ti * 128:t0 + (ti + 1) * 128, nch * 512:(nch + 1) * 512], o[:], reads=[o])
    p.pop()


NH = 32
HS = 64
DECAY_SCALE = 0.6065306597126334


def evac(p, eng, ps_t, src, dst_t, dst):
    if eng == "act":
        p.op("act", "copy", reads=[ps_t], writes=[dst_t], out=dst, in_=src)
    else:
        p.op("dve", "tensor_copy", reads=[ps_t], writes=[dst_t], out=dst, in_=src)


def rwkv_prep(c, X, S, W, sc, first):
    p = c.p
    p.push()
    wl = WLoader(c)
    names = ["r", "w", "k", "v", "a", "g"]
    g_bc = p.sb([128, D], F32, "g_bc")
    bcast_row(p, g_bc, W["mix_g"], D)
    mu_fm = p.sb([128, 6 * KT], F32, "mu_fm")
    p.dma("sp", mu_fm[:], W["mu_fm"], writes=[mu_fm])
    vnames = ["w0", "a0", "k_k", "k_a", "r_k"] + ([] if first else ["v0"])
    vch = {nm: p.sb([128, 512], F32, "vc_" + nm) for nm in vnames}
    TG = min(256, S)
    NT = TG // 128
    xT = {nm: p.sb([128, KT, TG], BF16, "xT_" + nm) for nm in names}
    xt = p.sb([128, D], F32, "xt")
    hn = p.sb([128, D], F32, "hn")
    ss = p.sb([128, 2], F32, "ss")
    hT = p.sb([128, KT, 128], F32, "hT")
    xxT = p.sb([128, KT, 128], F32, "xxT")
    last = p.sb([128, KT, 1], F32, "last")
    p.op("pool", "memset", writes=[last], ap=last[:], constant=0.0)
    lor = {"w": 96, "a": 96, "g": 256}
    if not first:
        lor["v"] = 64
    lT = {nm: p.sb([128, 2, TG], BF16, "lT_" + nm) for nm in lor}
    l2 = {nm: p.sb([128, 2, 512], BF16, "l2_" + nm) for nm in lor}
    E = {nm: p.sb([128, 512], F32, "E_" + nm) for nm in ["r", "k", "v", "w", "a", "g", "kk", "t1", "t2", "vf"]}
    st8 = p.sb([128, 16], F32, "st8")
    l1w = {"w": W["w1"], "a": W["a1"], "g": W["g1"]}
    l2w = {"w": W["w2"], "a": W["a2"], "g": W["g2"]}
    if not first:
        l1w["v"] = W["v1"]
        l2w["v"] = W["v2"]
    lfn = {"w": AF.Tanh, "a": AF.Copy, "g": AF.Sigmoid, "v": AF.Copy}
    PS = c.PS
    for t0 in range(0, S, TG):
      for ti in range(NT):
        rows = slice(t0 + ti * 128, t0 + (ti + 1) * 128)
        tsl = slice(ti * 128, (ti + 1) * 128)
        p.dma("sp", xt[:], X[rows, :], writes=[xt])
        rmsnorm_tile(p, xt, xt[:], g_bc, hn, hn[:], hn, ss)
        to_fm(c, hn, lambda kt: hn[:, kt * 128:(kt + 1) * 128], KT, hT, 0)
        p.op("dve", "tensor_tensor", reads=[hT], writes=[xxT], out=xxT[:, :, 1:128], in0=hT[:, :, 0:127], in1=hT[:, :, 1:128], op=ALU.subtract)
        p.op("dve", "tensor_tensor", reads=[hT, last], writes=[xxT], out=xxT[:, :, 0:1], in0=last[:], in1=hT[:, :, 0:1], op=ALU.subtract)
        p.op("pool", "tensor_copy", reads=[hT], writes=[last], out=last[:], in_=hT[:, :, 127:128])
        for i, nm in enumerate(names):
            for kt in range(KT):
                p.op("dve", "scalar_tensor_tensor", reads=[xxT, mu_fm, hT], writes=[xT[nm]], out=xT[nm][:, kt, tsl], in0=xxT[:, kt, :],
                     scalar=mu_fm[:, i * KT + kt:i * KT + kt + 1], in1=hT[:, kt, :], op0=ALU.mult, op1=ALU.add)
      for nm, ld in lor.items():
            wb = wl.load(l1w[nm], 0, D, 0, ld)
            for j in range(0, ld, 128):
                lsz = min(128, ld - j)
                ps = PS[4]
                for kt in range(KT):
                    p.mm(ps, ps[0:lsz, 0:TG], wb, wb[:, kt, j:j + lsz], xT[nm], xT[nm][:, kt, :], start=(kt == 0), stop=(kt == KT - 1))
                p.op("act", "activation", reads=[ps], writes=[lT[nm]], out=lT[nm][0:lsz, j // 128, :], in_=ps[0:lsz, 0:TG], func=lfn[nm])
      for nch in range(4):
        cs = slice(nch * 512, (nch + 1) * 512)
        for nm in vnames:
            p.dma("sp", vch[nm][:], W[nm][cs].partition_broadcast(128), writes=[vch[nm]])
        for nm, ld in lor.items():
            for j in range(0, ld, 128):
                lsz = min(128, ld - j)
                p.dma("sp", l2[nm][0:lsz, j // 128, :], l2w[nm][j:j + lsz, cs], writes=[l2[nm]])
        wbs = {nm: wl.load(W["w_rkv"][i], 0, D, nch * 512, 512) for i, nm in enumerate(["r", "k", "v"])}
        for ti in range(NT):
            rows = slice(t0 + ti * 128, t0 + (ti + 1) * 128)
            tsl = slice(ti * 128, (ti + 1) * 128)
            for i, nm in enumerate(["r", "k", "v"]):
                wb = wbs[nm]
                ps = PS[i]
                for kt in range(KT):
                    p.mm(ps, ps[:, :], xT[nm], xT[nm][:, kt, tsl], wb, wb[:, kt, :], start=(kt == 0), stop=(kt == KT - 1))
                evac(p, "act", ps, ps[:, :], E[nm], E[nm][:])
            lps = {"w": PS[3], "a": PS[4], "g": PS[5], "v": PS[0]}
            for nm, ld in lor.items():
                nj = (ld + 127) // 128
                for jj in range(nj):
                    lsz = min(128, ld - jj * 128)
                    p.mm(lps[nm], lps[nm][:, :], lT[nm], lT[nm][0:lsz, jj, tsl], l2[nm], l2[nm][0:lsz, jj, :], start=(jj == 0), stop=(jj == nj - 1))
            def tt(eng, out_t, a_t, a, b_t, b, op, o=None):
                p.op(eng, "tensor_tensor", reads=[a_t, b_t], writes=[out_t], out=(o if o is not None else out_t[:]), in0=a, in1=b, op=op)
            tt("dve", E["w"], lps["w"], lps["w"][:], vch["w0"], vch["w0"][:], ALU.add)
            p.op("act", "activation", reads=[E["w"]], writes=[E["w"]], out=E["w"][:], in_=E["w"][:], func=AF.Sigmoid)
            p.op("pool", "tensor_scalar", reads=[E["w"]], writes=[E["w"]], out=E["w"][:], in0=E["w"][:], scalar1=-DECAY_SCALE, scalar2=None, op0=ALU.mult)
            p.dma("pool", sc["LW"][rows, cs], E["w"][:], reads=[E["w"]])
            tt("dve", E["a"], lps["a"], lps["a"][:], vch["a0"], vch["a0"][:], ALU.add)
            p.op("act", "activation", reads=[E["a"]], writes=[E["a"]], out=E["a"][:], in_=E["a"][:], func=AF.Sigmoid)
            evac(p, "act", lps["g"], lps["g"][:], E["g"], E["g"][:])
            p.dma("pool", sc["G"][rows, cs], E["g"][:], reads=[E["g"]])
            if first:
                p.dma("pool", sc["VF"][rows, cs], E["v"][:], reads=[E["v"]])
            else:
                p.dma("sp", E["vf"][:], sc["VF"][rows, cs], writes=[E["vf"]])
                tt("dve", E["t1"], lps["v"], lps["v"][:], vch["v0"], vch["v0"][:], ALU.add)
                p.op("act", "activation", reads=[E["t1"]], writes=[E["t1"]], out=E["t1"][:], in_=E["t1"][:], func=AF.Sigmoid)
                tt("pool", E["vf"], E["vf"], E["vf"][:], E["v"], E["v"][:], ALU.subtract)
                tt("dve", E["vf"], E["vf"], E["vf"][:], E["t1"], E["t1"][:], ALU.mult)
                tt("dve", E["v"], E["v"], E["v"][:], E["vf"], E["vf"][:], ALU.add)
            p.dma("pool", sc["V"][rows, cs], E["v"][:], reads=[E["v"]])
            p.dma("pool", sc["R"][rows, cs], E["r"][:], reads=[E["r"]])
            tt("pool", E["kk"], E["k"], E["k"][:], vch["k_k"], vch["k_k"][:], ALU.mult)
            tt("dve", E["t1"], E["kk"], E["kk"][:], E["kk"], E["kk"][:], ALU.mult)
            p.op("dve", "reduce_sum", reads=[E["t1"]], writes=[st8], out=st8[:, 0:8], in_=E["t1"][:].rearrange("p (h k) -> p h k", k=HS), axis=AX.X)
            p.op("act", "activation", reads=[st8], writes=[st8], out=st8[:, 0:8], in_=st8[:, 0:8], func=AF.Sqrt)
            p.op("dve", "tensor_scalar", reads=[st8], writes=[st8], out=st8[:, 0:8], in0=st8[:, 0:8], scalar1=1e-12, scalar2=None, op0=ALU.max)
            p.op("dve", "reciprocal", reads=[st8], writes=[st8], out=st8[:, 8:16], in_=st8[:, 0:8])
            p.op("dve", "tensor_tensor", reads=[E["kk"], st8], writes=[E["kk"]], out=E["kk"][:].rearrange("p (h k) -> p h k", k=HS),
                 in0=E["kk"][:].rearrange("p (h k) -> p h k", k=HS), in1=st8[:, 8:16].unsqueeze(2).to_broadcast([128, 8, HS]), op=ALU.mult)
            p.op("pool", "tensor_scalar", reads=[E["a"]], writes=[E["t1"]], out=E["t1"][:], in0=E["a"][:], scalar1=-1.0, scalar2=None, op0=ALU.add)
            tt("pool", E["t1"], E["t1"], E["t1"][:], vch["k_a"], vch["k_a"][:], ALU.mult)
            p.op("pool", "tensor_scalar", reads=[E["t1"]], writes=[E["t1"]], out=E["t1"][:], in0=E["t1"][:], scalar1=1.0, scalar2=None, op0=ALU.add)
            tt("dve", E["k"], E["k"], E["k"][:], E["t1"], E["t1"][:], ALU.mult)
            p.dma("pool", sc["KP"][rows, cs], E["k"][:], reads=[E["k"]])
            tt("dve", E["t2"], E["kk"], E["kk"][:], E["a"], E["a"][:], ALU.mult)
            p.dma("pool", sc["B"][rows, cs], E["t2"][:], reads=[E["t2"]])
            p.op("pool", "tensor_scalar", reads=[E["kk"]], writes=[E["kk"]], out=E["kk"][:], in0=E["kk"][:], scalar1=-1.0, scalar2=None, op0=ALU.mult)
            p.dma("pool", sc["A"][rows, cs], E["kk"][:], reads=[E["kk"]])
            tt("dve", E["t1"], E["r"], E["r"][:], E["k"], E["k"][:], ALU.mult)
            tt("dve", E["t1"], E["t1"], E["t1"][:], vch["r_k"], vch["r_k"][:], ALU.mult)
            p.op("dve", "reduce_sum", reads=[E["t1"]], writes=[st8], out=st8[:, 0:8], in_=E["t1"][:].rearrange("p (h k) -> p h k", k=HS), axis=AX.X)
            p.op("dve", "tensor_tensor", reads=[E["v"], st8], writes=[E["t1"]], out=E["t1"][:].rearrange("p (h k) -> p h k", k=HS),
                 in0=E["v"][:].rearrange("p (h k) -> p h k", k=HS), in1=st8[:, 0:8].unsqueeze(2).to_broadcast([128, 8, HS]), op=ALU.mult)
            p.dma("pool", sc["BON"][rows, cs], E["t1"][:], reads=[E["t1"]])
    p.pop()


def rwkv_scan(c, S, sc):
    p = c.p
    p.push()
    PS = c.PS
    ident = c.ident
    ule = p.sb([128, 128], F32, "ule")
    p.op("pool", "memset", writes=[ule], ap=ule[:], constant=1.0)
    p.op("pool", "affine_select", reads=[ule], writes=[ule], out=ule[:], in_=ule[:], pattern=[[1, 128]],
         compare_op=ALU.is_ge, fill=0.0, base=0, channel_multiplier=-1)
    su = p.sb([128, 128], F32, "su")
    p.op("pool", "memset", writes=[su], ap=su[:], constant=1.0)
    p.op("pool", "affine_select", reads=[su], writes=[su], out=su[:], in_=su[:], pattern=[[1, 128]],
         compare_op=ALU.is_gt, fill=0.0, base=0, channel_multiplier=-1)
    sl = p.sb([128, 128], F32, "sl")
    p.op("pool", "memset", writes=[sl], ap=sl[:], constant=1.0)
    p.op("pool", "affine_select", reads=[sl], writes=[sl], out=sl[:], in_=sl[:], pattern=[[-1, 128]],
         compare_op=ALU.is_gt, fill=0.0, base=0, channel_multiplier=1)
    mask4 = p.sb([128, 512], F32, "mask4")
    for i, m in enumerate([su, ule, su, ule]):
        p.op("pool", "tensor_copy", reads=[m], writes=[mask4], out=mask4[:, i * 128:(i + 1) * 128], in_=m[:])
    ones_col = p.sb([128, 1], F32, "ones_col")
    p.op("pool", "memset", writes=[ones_col], ap=ones_col[:], constant=1.0)
    raw = {nm: p.sb([128, D], F32, "raw_" + nm) for nm in ["R", "LW", "KP", "V", "A", "B"]}
    Gc = p.sb([128, D], F32, "Gc")
    Et = p.sb([128, D], F32, "Et")
    Rt = p.sb([128, D], F32, "Rt")
    At = p.sb([128, D], F32, "At")
    Bt = p.sb([128, D], F32, "Bt")
    Kt = p.sb([128, D], F32, "Kt")
    GL = p.sb([64, NH], F32, "GL")
    GH = 4
    fm = [p.sb([64, 512], F32, "fm") for _ in range(GH)]
    GM = [p.sb([128, 512], F32, "GM") for _ in range(GH)]
    A0 = [p.sb([128, 128], BF16, "A0") for _ in range(GH)]
    NTb = [p.sb([128, 128], BF16, "NTb") for _ in range(GH)]
    AA = [[p.sb([128, 256], BF16, "AA") for _ in range(2)] for _ in range(GH)]
    PTb = [[p.sb([128, 128], BF16, "PT") for _ in range(2)] for _ in range(GH)]
    PTf = [p.sb([128, 128], F32, "PTf") for _ in range(GH)]
    NV = p.sb([128, 256], F32, "NV")
    KV = p.sb([64, 256], F32, "KV")
    Xs = p.sb([128, 256], F32, "Xs")
    Us = p.sb([128, 256], F32, "Us")
    S1 = p.sb([64, 256], F32, "S1")
    S2 = p.sb([64, 256], F32, "S2")
    Tst = [p.sb([64, GH * HS], F32, "Tst") for _ in range(NH // GH)]
    for t_ in Tst:
        p.op("pool", "memset", writes=[t_], ap=t_[:], constant=0.0)
    Yt = p.sb([128, D], F32, "Yt")

    TPS = [PS[0], PS[1]]
    pg = PS[2]
    pn = [PS[3].sub(128, 0, 128), PS[3].sub(128, 128, 256), PS[3].sub(128, 256, 384), PS[3].sub(128, 384, 512)]
    sq = [PS[4].sub(128, 0, 256), PS[5].sub(128, 0, 256), PS[4].sub(128, 256, 512), PS[5].sub(128, 256, 512)]
    pp = [PS[6].sub(128, 0, 128), PS[7].sub(128, 0, 128), PS[6].sub(128, 128, 256), PS[7].sub(128, 128, 256)]
    pnv = PS[0].sub(128, 0, 256)
    pY = PS[0].sub(128, 256, 512)
    pkv = PS[1].sub(128, 0, 256)
    p3 = PS[1].sub(128, 256, 512)
    p1 = PS[2].sub(128, 0, 256)
    p2 = PS[2].sub(128, 256, 512)

    for ch in range(S // 128):
        rows = slice(ch * 128, (ch + 1) * 128)
        for nm in raw:
            p.dma("sp", raw[nm][:], sc[nm][rows, :], writes=[raw[nm]])
        LW = raw["LW"]
        V = raw["V"]
        for n4 in range(4):
            ps = TPS[n4 % 2]
            p.mm(ps, ps[:, :], ule, ule[:], LW, LW[:, n4 * 512:(n4 + 1) * 512])
            evac(p, "act" if n4 % 2 else "dve", ps, ps[:, :], Gc, Gc[:, n4 * 512:(n4 + 1) * 512])
        ps = TPS[0]
        for h in range(NH):
            p.mm(ps, ps[0:64, h:h + 1], LW, LW[:, h * 64:(h + 1) * 64], ones_col, ones_col[:])
        p.op("act", "activation", reads=[ps], writes=[GL], out=GL[:], in_=ps[0:64, 0:NH], func=AF.Exp)
        p.op("act", "activation", reads=[Gc], writes=[Et], out=Et[:], in_=Gc[:], func=AF.Exp)
        p.op("dve", "tensor_tensor", reads=[raw["R"], Et], writes=[Rt], out=Rt[:], in0=raw["R"][:], in1=Et[:], op=ALU.mult)
        p.op("act", "activation", reads=[Gc], writes=[Et], out=Et[:], in_=Gc[:], func=AF.Exp, scale=-1.0)
        p.op("dve", "tensor_tensor", reads=[raw["B"], Et], writes=[Bt], out=Bt[:], in0=raw["B"][:], in1=Et[:], op=ALU.mult)
        p.op("pool", "tensor_tensor", reads=[raw["KP"], Et], writes=[Kt], out=Kt[:], in0=raw["KP"][:], in1=Et[:], op=ALU.mult)
        p.op("pool", "tensor_tensor", reads=[Gc, LW], writes=[Gc], out=Gc[:], in0=Gc[:], in1=LW[:], op=ALU.subtract)
        p.op("act", "activation", reads=[Gc], writes=[Et], out=Et[:], in_=Gc[:], func=AF.Exp)
        p.op("dve", "tensor_tensor", reads=[raw["A"], Et], writes=[At], out=At[:], in0=raw["A"][:], in1=Et[:], op=ALU.mult)
        for g in range(NH // GH):
            gcols = slice(g * GH * HS, (g + 1) * GH * HS)
            cur = []
            for j in range(GH):
                h = g * GH + j
                hc = slice(h * HS, (h + 1) * HS)
                pst = TPS[h % 2]
                for qi, Q in enumerate([At, Rt, Bt, Kt]):
                    p.op("pe", "transpose", reads=[Q, ident], writes=[pst], out=pst[0:64, qi * 128:(qi + 1) * 128], in_=Q[:, hc], identity=ident[:])
                evac(p, "act" if h % 2 else "dve", pst, pst[0:64, :], fm[j], fm[j][:])
                f = fm[j]
                p.mm(pg, pg[:, 0:256], f, f[:, 256:384], f, f[:, 0:256])
                p.mm(pg, pg[:, 256:512], f, f[:, 384:512], f, f[:, 0:256])
                p.op("dve", "tensor_tensor", reads=[pg, mask4], writes=[GM[j]], out=GM[j][:], in0=pg[:], in1=mask4[:], op=ALU.mult)
                pn_ = pn[j]
                p.mm(pn_, pn_[:], f, f[:, 0:128], f, f[:, 256:384])
                a0 = A0[j]
                p.op("dve", "tensor_tensor", reads=[pn_, sl], writes=[a0], out=a0[:], in0=pn_[:], in1=sl[:], op=ALU.mult)
                PT = PTb[j][0]
                p.op("pool", "tensor_tensor", reads=[GM[j], ident], writes=[PT], out=PT[:], in0=GM[j][:, 0:128], in1=ident[:], op=ALU.add)
                p.op("pool", "tensor_copy", reads=[GM[j]], writes=[NTb[j]], out=NTb[j][:], in_=GM[j][:, 0:128])
                cur.append([a0, a0[:], NTb[j], NTb[j][:], PT])
            for i in range(1, 7):
                for j in range(GH):
                    A_t, A_ap, AT_t, AT_ap, PT = cur[j]
                    s_ = sq[j]
                    p.mm(s_, s_[:, 0:128], AT_t, AT_ap, A_t, A_ap)
                    if i < 6:
                        p.mm(s_, s_[:, 128:256], A_t, A_ap, AT_t, AT_ap)
                for j in range(GH):
                    aa = AA[j][i % 2]
                    w_ = 256 if i < 6 else 128
                    evac(p, "act" if j % 2 else "dve", sq[j], sq[j][:, 0:w_], aa, aa[:, 0:w_])
                    cur[j][0:4] = [aa, aa[:, 0:128], aa, aa[:, 128:256]]
                for j in range(GH):
                    A_t, A_ap, AT_t, AT_ap, PT = cur[j]
                    p.mm(pp[j], pp[j][:], A_t, A_ap, PT, PT[:])
                for j in range(GH):
                    PT = cur[j][4]
                    PTn = PTb[j][i % 2] if i < 6 else PTf[j]
                    p.op("dve", "tensor_tensor", reads=[pp[j], PT], writes=[PTn], out=PTn[:], in0=pp[j][:], in1=PT[:], op=ALU.add)
                    cur[j][4] = PTn
            for j in range(GH):
                h = g * GH + j
                hc = slice(h * HS, (h + 1) * HS)
                jc = slice(j * HS, (j + 1) * HS)
                p.mm(pnv, pnv[:, jc], GM[j], GM[j][:, 256:384], V, V[:, hc])
                p.mm(pkv, pkv[0:64, jc], Kt, Kt[:, hc], V, V[:, hc])
            evac(p, "act", pnv, pnv[:], NV, NV[:])
            evac(p, "act", pkv, pkv[0:64, :], KV, KV[:])
            Tg = Tst[g]
            for j in range(GH):
                jc = slice(j * HS, (j + 1) * HS)
                p.mm(p1, p1[:, jc], fm[j], fm[j][:, 0:128], Tg, Tg[:, jc])
            p.op("dve", "tensor_tensor", reads=[p1, NV], writes=[Xs], out=Xs[:], in0=p1[:], in1=NV[:], op=ALU.add)
            for j in range(GH):
                jc = slice(j * HS, (j + 1) * HS)
                p.mm(p2, p2[:, jc], PTf[j], PTf[j][:], Xs, Xs[:, jc])
            evac(p, "act", p2, p2[:], Us, Us[:])
            for j in range(GH):
                h = g * GH + j
                hc = slice(h * HS, (h + 1) * HS)
                jc = slice(j * HS, (j + 1) * HS)
                p.mm(pY, pY[:, jc], fm[j], fm[j][:, 128:256], Tg, Tg[:, jc], start=True, stop=False)
                p.mm(pY, pY[:, jc], GM[j], GM[j][:, 128:256], Us, Us[:, jc], start=False, stop=False)
                p.mm(pY, pY[:, jc], GM[j], GM[j][:, 384:512], V, V[:, hc], start=False, stop=True)
            evac(p, "act", pY, pY[:], Yt, Yt[:, gcols])
            for j in range(GH):
                h = g * GH + j
                hc = slice(h * HS, (h + 1) * HS)
                jc = slice(j * HS, (j + 1) * HS)
                p.mm(p3, p3[0:64, jc], Bt, Bt[:, hc], Us, Us[:, jc])
            p.op("pool", "tensor_tensor", reads=[Tg, KV], writes=[S1], out=S1[:], in0=Tg[:], in1=KV[:], op=ALU.add)
            p.op("dve", "tensor_tensor", reads=[p3, S1], writes=[S2], out=S2[:], in0=p3[0:64, :], in1=S1[:], op=ALU.add)
            p.op("dve", "tensor_tensor", reads=[S2, GL], writes=[Tg], out=Tg[:].rearrange("p (h v) -> p h v", v=HS),
                 in0=S2[:].rearrange("p (h v) -> p h v", v=HS),
                 in1=GL[:, g * GH:(g + 1) * GH].unsqueeze(2).to_broadcast([64, GH, HS]), op=ALU.mult)
        p.dma("pool", sc["Y"][rows, :], Yt[:], reads=[Yt])
    p.pop()


def out_proj_phase(c, X, S, make_tile, Wo, extra_alloc=None):
    p = c.p
    TG = min(512, S)
    NT = TG // 128
    wl = WLoader(c)
    oT = p.sb([128, KT, TG], BF16, "oT")
    mo = p.sb([128, D], F32, "mo")
    xres = [p.sb([128, D], F32, "xres") for _ in range(NT)]
    ot = [p.sb([128, 512], F32, "ot") for _ in range(2)]
    for t0 in range(0, S, TG):
        for ti in range(NT):
            r0 = t0 + ti * 128
            make_tile(r0, mo)
            to_fm(c, mo, lambda kt: mo[:, kt * 128:(kt + 1) * 128], KT, oT, ti * 128)
            p.dma("sp", xres[ti][:], X[r0:r0 + 128, :], writes=[xres[ti]])
        for nch in range(4):
            wb = wl.load(Wo, 0, D, nch * 512, 512)
            for ti in range(NT):
                ps = c.PS[ti]
                for kt in range(KT):
                    p.mm(ps, ps[:, :], oT, oT[:, kt, ti * 128:(ti + 1) * 128], wb, wb[:, kt, :], start=(kt == 0), stop=(kt == KT - 1))
                o = ot[ti % 2]
                p.op("dve", "tensor_tensor", reads=[ps, xres[ti]], writes=[o], out=o[:], in0=ps[:], in1=xres[ti][:, nch * 512:(nch + 1) * 512], op=ALU.add)
                p.dma("pool", X[t0 + ti * 128:t0 + (ti + 1) * 128, nch * 512:(nch + 1) * 512], o[:], reads=[o])


def rwkv_post(c, X, S, W, sc):
    p = c.p
    p.push()
    Yt = p.sb([128, D], F32, "Yt")
    Bo = p.sb([128, D], F32, "Bo")
    Gt = p.sb([128, D], F32, "Gt")
    sq = p.sb([128, D], F32, "sq")
    lg = p.sb([128, D], F32, "lg")
    lb = p.sb([128, D], F32, "lb")
    bcast_row(p, lg, W["lnx_g"], D)
    bcast_row(p, lb, W["lnx_b"], D)
    st = p.sb([128, 64], F32, "st")

    def v3(t):
        return t[:].rearrange("p (h k) -> p h k", k=HS)

    def make_tile(r0, mo):
        rows = slice(r0, r0 + 128)
        p.dma("sp", Yt[:], sc["Y"][rows, :], writes=[Yt])
        p.dma("sp", Bo[:], sc["BON"][rows, :], writes=[Bo])
        p.dma("sp", Gt[:], sc["G"][rows, :], writes=[Gt])
        p.op("dve", "reduce_sum", reads=[Yt], writes=[st], out=st[:, 0:32], in_=v3(Yt), axis=AX.X)
        p.op("dve", "tensor_scalar", reads=[st], writes=[st], out=st[:, 0:32], in0=st[:, 0:32], scalar1=1.0 / HS, scalar2=None, op0=ALU.mult)
        p.op("dve", "tensor_tensor", reads=[Yt, st], writes=[Yt], out=v3(Yt), in0=v3(Yt), in1=st[:, 0:32].unsqueeze(2).to_broadcast([128, NH, HS]), op=ALU.subtract)
        p.op("act", "activation", reads=[Yt], writes=[sq], out=sq[:], in_=Yt[:], func=AF.Square)
        p.op("dve", "reduce_sum", reads=[sq], writes=[st], out=st[:, 32:64], in_=v3(sq), axis=AX.X)
        p.op("dve", "tensor_scalar", reads=[st], writes=[st], out=st[:, 32:64], in0=st[:, 32:64], scalar1=1.0 / HS, scalar2=64e-5, op0=ALU.mult, op1=ALU.add)
        p.op("act", "activation", reads=[st], writes=[st], out=st[:, 32:64], in_=st[:, 32:64], func=AF.Sqrt)
        p.op("dve", "reciprocal", reads=[st], writes=[st], out=st[:, 0:32], in_=st[:, 32:64])
        p.op("dve", "tensor_tensor", reads=[Yt, st], writes=[Yt], out=v3(Yt), in0=v3(Yt), in1=st[:, 0:32].unsqueeze(2).to_broadcast([128, NH, HS]), op=ALU.mult)
        p.op("pool", "tensor_tensor", reads=[Yt, lg], writes=[Yt], out=Yt[:], in0=Yt[:], in1=lg[:], op=ALU.mult)
        p.op("pool", "tensor_tensor", reads=[Yt, lb], writes=[Yt], out=Yt[:], in0=Yt[:], in1=lb[:], op=ALU.add)
        p.op("dve", "tensor_tensor", reads=[Yt, Bo], writes=[Yt], out=Yt[:], in0=Yt[:], in1=Bo[:], op=ALU.add)
        p.op("dve", "tensor_tensor", reads=[Yt, Gt], writes=[mo], out=mo[:], in0=Yt[:], in1=Gt[:], op=ALU.mult)

    out_proj_phase(c, X, S, make_tile, W["w_o"])
    p.pop()


RW_SC = ["R", "LW", "KP", "V", "A", "B", "G", "BON", "Y", "VF"]


MH = 4
MDK = 256
MDV = 512


def mlstm_layer(c, X, S, W, sc):
    p = c.p
    PS = c.PS
    ident = c.ident
    TG = min(512, S)
    NT = TG // 128
    Win = W["w_in"]
    QK, Vb, SO, MO = sc["QK"], sc["Vb"], sc["SO"], sc["MO"]
    p.push()
    GI = p.sb([4, S], F32, "GI")
    GF = p.sb([4, S], F32, "GF")
    p.push()
    wl = WLoader(c)
    g_bc = p.sb([128, D], F32, "g_bc")
    bcast_row(p, g_bc, W["mix_g"], D)
    xt = p.sb([128, D], F32, "xt")
    hn = p.sb([128, D], F32, "hn")
    ss = p.sb([128, 2], F32, "ss")
    hT = p.sb([128, KT, TG], BF16, "hT")
    qk_sb = [p.sb([128, TG], BF16, "qk_sb") for _ in range(2)]
    vb_sb = [p.sb([128, 512], BF16, "vb_sb") for _ in range(2)]
    so_sb = [p.sb([128, 512], F32, "so_sb") for _ in range(2)]
    for t0 in range(0, S, TG):
        for ti in range(NT):
            p.dma("sp", xt[:], X[t0 + ti * 128:t0 + (ti + 1) * 128, :], writes=[xt])
            rmsnorm_tile(p, xt, xt[:], g_bc, hn, hn[:], hn, ss)
            to_fm(c, hn, lambda kt: hn[:, kt * 128:(kt + 1) * 128], KT, hT, ti * 128)
        for nb in range(4):
            wb = wl.load(Win, 0, D, nb * 512, 512)
            for j in range(4):
                ps = PS[j % 2]
                for kt in range(KT):
                    p.mm(ps, ps[:, 0:TG], wb, wb[:, kt, j * 128:(j + 1) * 128], hT, hT[:, kt, :], start=(kt == 0), stop=(kt == KT - 1))
                o = qk_sb[j % 2]
                p.op("act", "activation", reads=[ps], writes=[o], out=o[:, 0:TG], in_=ps[:, 0:TG], func=AF.Copy, scale=(0.0625 if nb < 2 else 1.0))
                r0 = nb * 512 + j * 128
                p.dma("pool", QK[r0:r0 + 128, t0:t0 + TG], o[:, 0:TG], reads=[o])
        for nb in range(8):
            wb = wl.load(Win, 0, D, 2048 + nb * 512, 512)
            for ti in range(NT):
                ps = PS[2 + ti % 2]
                for kt in range(KT):
                    p.mm(ps, ps[:, :], hT, hT[:, kt, ti * 128:(ti + 1) * 128], wb, wb[:, kt, :], start=(kt == 0), stop=(kt == KT - 1))
                rows = slice(t0 + ti * 128, t0 + (ti + 1) * 128)
                if nb < 4:
                    o = vb_sb[ti % 2]
                    p.op("dve", "tensor_copy", reads=[ps], writes=[o], out=o[:], in_=ps[:])
                    p.dma("pool", Vb[rows, nb * 512:(nb + 1) * 512], o[:], reads=[o])
                else:
                    o = so_sb[ti % 2]
                    p.op("act", "activation", reads=[ps], writes=[o], out=o[:], in_=ps[:], func=AF.Sigmoid)
                    p.dma("pool", SO[rows, (nb - 4) * 512:(nb - 3) * 512], o[:], reads=[o])
        wb = wl.load(Win, 0, D, 6144, 8)
        for gi, (G_, ps) in enumerate([(GI, PS[4]), (GF, PS[5])]):
            for kt in range(KT):
                p.mm(ps, ps[0:4, 0:TG], wb, wb[:, kt, gi * 4:(gi + 1) * 4], hT, hT[:, kt, :], start=(kt == 0), stop=(kt == KT - 1))
            p.op("act", "copy", reads=[ps], writes=[G_], out=G_[:, t0:t0 + TG], in_=ps[0:4, 0:TG])
    p.pop()
    NB = S // 128
    bif = p.sb([4, 2], F32, "bif")
    p.dma("sp", bif[:], W["b_if"].rearrange("a h -> h a"), writes=[bif], allow_slow_non_contiguous=True)
    SCN = min(1024, S)
    onesr = p.sb([4, SCN], F32, "onesr")
    zeror = p.sb([4, SCN], F32, "zeror")
    p.op("pool", "memset", writes=[onesr], ap=onesr[:], constant=1.0)
    p.op("pool", "memset", writes=[zeror], ap=zeror[:], constant=0.0)
    Fr = p.sb([4, S], F32, "Fr")
    for G_, col in ((GI, 0), (GF, 1)):
        p.op("dve", "tensor_scalar", reads=[G_, bif], writes=[G_], out=G_[:], in0=G_[:], scalar1=bif[:, col:col + 1], scalar2=None, op0=ALU.add)
        p.op("act", "activation", reads=[G_], writes=[G_], out=G_[:], in_=G_[:], func=AF.Tanh, scale=1.0 / 15.0)
        p.op("dve", "tensor_scalar", reads=[G_], writes=[G_], out=G_[:], in0=G_[:], scalar1=15.0, scalar2=None, op0=ALU.mult)
    p.op("act", "activation", reads=[GF], writes=[GF], out=GF[:], in_=GF[:], func=AF.Exp, scale=-1.0)
    p.op("dve", "tensor_scalar", reads=[GF], writes=[GF], out=GF[:], in0=GF[:], scalar1=1.0, scalar2=None, op0=ALU.add)
    p.op("act", "activation", reads=[GF], writes=[GF], out=GF[:], in_=GF[:], func=AF.Ln)
    p.op("dve", "tensor_scalar", reads=[GF], writes=[GF], out=GF[:], in0=GF[:], scalar1=-1.0, scalar2=None, op0=ALU.mult)
    for s0 in range(0, S, SCN):
        init = 0.0 if s0 == 0 else Fr[:, s0 - 1:s0]
        p.op("dve", "tensor_tensor_scan", reads=[GF, onesr, Fr], writes=[Fr], out=Fr[:, s0:s0 + SCN], data0=onesr[:, 0:SCN], data1=GF[:, s0:s0 + SCN],
             initial=init, op0=ALU.mult, op1=ALU.add)
    Ur = GI
    p.op("dve", "tensor_tensor", reads=[GI, Fr], writes=[Ur], out=Ur[:], in0=GI[:], in1=Fr[:], op=ALU.subtract)
    Gr = GF
    for s0 in range(0, S, SCN):
        init = 0.0 if s0 == 0 else Gr[:, s0 - 1:s0]
        p.op("dve", "tensor_tensor_scan", reads=[Ur, zeror, Gr], writes=[Gr], out=Gr[:, s0:s0 + SCN], data0=Ur[:, s0:s0 + SCN], data1=zeror[:, 0:SCN],
             initial=init, op0=ALU.max, op1=ALU.add)
    NM = Fr
    p.op("dve", "tensor_tensor", reads=[Fr, Gr], writes=[NM], out=NM[:], in0=Fr[:], in1=Gr[:], op=ALU.add)
    p.op("dve", "tensor_scalar", reads=[NM], writes=[NM], out=NM[:], in0=NM[:], scalar1=-1.0, scalar2=None, op0=ALU.mult)
    p.op("dve", "tensor_scalar", reads=[Gr], writes=[Gr], out=Gr[:], in0=Gr[:], scalar1=-1.0, scalar2=None, op0=ALU.mult)
    Ucol = p.sb([128, NB, 4], F32, "Ucol")
    NMcol = p.sb([128, NB, 4], F32, "NMcol")
    for src, dst in ((Ur, Ucol), (NM, NMcol)):
        for b0 in range(0, NB, 64):
            nb_ = min(64, NB - b0)
            ps = PS[6]
            for b in range(nb_):
                p.op("pe", "transpose", reads=[src, ident], writes=[ps], out=ps[:, b * 4:(b + 1) * 4], in_=src[:, (b0 + b) * 128:(b0 + b + 1) * 128], identity=ident[0:4, 0:4])
            p.op("dve", "tensor_copy", reads=[ps], writes=[dst], out=dst[:, b0:b0 + nb_, :], in_=ps[:, 0:nb_ * 4].rearrange("p (b h) -> p b h", h=4))
    p.op("act", "activation", reads=[NMcol], writes=[NMcol], out=NMcol[:], in_=NMcol[:], func=AF.Exp)
    sel = p.sb([4, 4, 128], F32, "sel")
    p.op("pool", "memset", writes=[sel], ap=sel[:], constant=1.0)
    p.op("pool", "affine_select", reads=[sel], writes=[sel], out=sel[:], in_=sel[:], pattern=[[-1, 4], [0, 128]],
         compare_op=ALU.is_equal, fill=0.0, base=0, channel_multiplier=1)
    TB = min(512, S)
    NTS = TB // 128
    cms = p.sb([128, NTS, TB], F32, "cms")
    p.op("pool", "memset", writes=[cms], ap=cms[:], constant=1.0)
    for r_ in range(NTS):
        p.op("pool", "affine_select", reads=[cms], writes=[cms], out=cms[:, r_, :], in_=cms[:, r_, :], pattern=[[1, TB]], compare_op=ALU.is_ge,
             fill=0.0, base=-128 * r_, channel_multiplier=-1)
    onesb = p.sb([128, 1], BF16, "onesb")
    p.op("pool", "memset", writes=[onesb], ap=onesb[:], constant=1.0)
    qT = p.sb([128, 2, TB], BF16, "qT")
    kT = [p.sb([128, 2, 128], BF16, "kT") for _ in range(3)]
    Vs = [p.sb([128, MDV], BF16, "Vs") for _ in range(3)]
    ngb = p.sb([128, TB], F32, "ngb")
    Ex = [p.sb([128, TB], F32, "Ex") for _ in range(3)]
    Pb = [p.sb([128, TB], BF16, "Pb") for _ in range(3)]
    ng_w = p.sb([128, MDV], F32, "ng_w")
    hr = p.sb([128, MDV], F32, "hr")
    sqh = p.sb([128, MDV], F32, "sqh")
    sot = p.sb([128, MDV], F32, "sot")
    dn = p.sb([128, 8], F32, "dn")
    ssd = p.sb([128, 2], F32, "ssd")
    pnum = [PS[i] for i in range(4)]
    pden = PS[4]
    pdt = PS[7]
    drow = p.sb([1, TB], F32, "drow")
    for h in range(MH):
        bcast_row(p, ng_w, W["norm_g"][h * MDV:(h + 1) * MDV], MDV)
        for tb in range(S // TB):
            tcols = slice(tb * TB, (tb + 1) * TB)
            p.dma("sp", qT[:], QK[h * MDK:(h + 1) * MDK, tcols].rearrange("(kt p) t -> p kt t", p=128), writes=[qT])
            ps = PS[7]
            p.mm(ps, ps[:, 0:TB], sel, sel[:, h, :], Gr, Gr[:, tcols])
            p.op("act", "copy", reads=[ps], writes=[ngb], out=ngb[:], in_=ps[:, 0:TB])
            nsb = (tb + 1) * NTS

            def front(sb):
                i2 = sb % 3
                p.dma("sp", kT[i2][:], QK[1024 + h * MDK:1024 + (h + 1) * MDK, sb * 128:(sb + 1) * 128].rearrange("(kt p) t -> p kt t", p=128), writes=[kT[i2]])
                p.dma("sp", Vs[i2][:], Vb[sb * 128:(sb + 1) * 128, h * MDV:(h + 1) * MDV], writes=[Vs[i2]])
                pst = PS[5 + sb % 2]
                for kt in range(2):
                    p.mm(pst, pst[:, 0:TB], kT[i2], kT[i2][:, kt, :], qT, qT[:, kt, :], start=(kt == 0), stop=(kt == 1))
                e = Ex[i2]
                p.op("dve", "tensor_scalar", reads=[ngb, Ucol], writes=[e], out=e[:], in0=ngb[:], scalar1=Ucol[:, sb, h:h + 1], scalar2=0.0, op0=ALU.add, op1=ALU.min)
                p.op("act", "activation", reads=[e], writes=[e], out=e[:], in_=e[:], func=AF.Exp)
                r = sb - tb * NTS
                if r >= 0:
                    p.op("pool", "tensor_tensor", reads=[e, cms], writes=[e], out=e[:], in0=e[:], in1=cms[:, r, :], op=ALU.mult)
                pb = Pb[i2]
                p.op("dve", "tensor_tensor", reads=[pst, e], writes=[pb], out=pb[:], in0=pst[:, 0:TB], in1=e[:], op=ALU.mult)

            def back(sb):
                i2 = sb % 3
                r = sb - tb * NTS
                pb = Pb[i2]
                p.mm(pden, pden[0:1, 0:TB], onesb, onesb[:], pb, pb[:], start=(sb == 0), stop=(sb == nsb - 1))
                for ts in range(max(r, 0), NTS):
                    last = (sb == tb * NTS + ts)
                    p.mm(pnum[ts], pnum[ts][:, :], pb, pb[:, ts * 128:(ts + 1) * 128], Vs[i2], Vs[i2][:], start=(sb == 0), stop=last)

            LA = 2
            for i in range(nsb + LA):
                if i < nsb:
                    front(i)
                if i - LA >= 0:
                    back(i - LA)
            p.op("act", "activation", reads=[pden], writes=[drow], out=drow[:, 0:TB], in_=pden[0:1, 0:TB], func=AF.Abs)
            for ts in range(NTS):
                p.op("pe", "transpose", reads=[drow, ident], writes=[pdt], out=pdt[:, ts:ts + 1], in_=drow[0:1, ts * 128:(ts + 1) * 128], identity=ident[0:1, 0:1])
            for ts in range(NTS):
                blk = tb * NTS + ts
                rows = slice(blk * 128, (blk + 1) * 128)
                p.op("act", "copy", reads=[pdt], writes=[dn], out=dn[:, 0:1], in_=pdt[:, ts:ts + 1])
                p.op("dve", "tensor_tensor", reads=[dn, NMcol], writes=[dn], out=dn[:, 1:2], in0=dn[:, 0:1], in1=NMcol[:, blk, h:h + 1], op=ALU.max)
                p.op("dve", "reciprocal", reads=[dn], writes=[dn], out=dn[:, 2:3], in_=dn[:, 1:2])
                p.op("dve", "tensor_scalar", reads=[pnum[ts], dn], writes=[hr], out=hr[:], in0=pnum[ts][:], scalar1=dn[:, 2:3], scalar2=None, op0=ALU.mult)
                p.dma("sp", sot[:], SO[rows, h * MDV:(h + 1) * MDV], writes=[sot])
                rmsnorm_tile(p, hr, hr[:], ng_w, hr, hr[:], sqh, ssd, d=MDV)
                p.op("dve", "tensor_tensor", reads=[hr, sot], writes=[sqh], out=sqh[:], in0=hr[:], in1=sot[:], op=ALU.mult)
                p.dma("pool", MO[rows, h * MDV:(h + 1) * MDV], sqh[:], reads=[sqh])
    p.pop()
    p.push()
    def make_tile(r0, mo):
        p.dma("sp", mo[:], MO[r0:r0 + 128, :], writes=[mo])
    out_proj_phase(c, X, S, make_tile, W["w_o"])
    p.pop()


def dn_ss(p, dn):
    return T(dn.ap[:, 4:6], "dnss")


DH = 128
NHD = 16
NEG = -1.0e30


def dsa_layer(c, X, S, W, sc, dbg=None, skip=()):
    p = c.p
    PS = c.PS
    ident = c.ident
    TG = min(512, S)
    NT = TG // 128
    NB = S // 128
    Win = W["w_in"]
    QT, QIT, MO = sc["QT"], sc["QIT"], sc["MO"]
    p.push()
    kT = p.sb([128, S], BF16, "kT")
    kiT = p.sb([64, S], BF16, "kiT")
    Vs = p.sb([128, NB, 132], BF16, "Vs")
    wis = p.sb([128, NB, 16], F32, "wis")
    p.op("pool", "memset", writes=[Vs], ap=Vs[:, :, 128:129], constant=1.0)
    p.push()
    wl = WLoader(c)
    g_bc = p.sb([128, D], F32, "g_bc")
    bcast_row(p, g_bc, W["mix_g"], D)
    qg = p.sb([128, DH], F32, "qg")
    kg = p.sb([128, DH], F32, "kg")
    bcast_row(p, qg, W["q_norm_g"], DH)
    bcast_row(p, kg, W["k_norm_g"], DH)
    xt = p.sb([128, D], F32, "xt")
    hn = p.sb([128, D], F32, "hn")
    ss = p.sb([128, 2], F32, "ss")
    hT = p.sb([128, KT, TG], BF16, "hT")
    qs = p.sb([128, 512], F32, "qs")
    qsq = p.sb([128, 512], F32, "qsq")
    st4 = p.sb([128, 8], F32, "st4")
    qTg = p.sb([128, NHD, TG], BF16, "qTg")
    qi_sb = [p.sb([128, TG], BF16, "qi_sb") for _ in range(2)]
    for t0 in range(0, S, TG):
        for ti in range(NT):
            p.dma("sp", xt[:], X[t0 + ti * 128:t0 + (ti + 1) * 128, :], writes=[xt])
            rmsnorm_tile(p, xt, xt[:], g_bc, hn, hn[:], hn, ss)
            to_fm(c, hn, lambda kt: hn[:, kt * 128:(kt + 1) * 128], KT, hT, ti * 128)
        for nb in range(4):
            wb = wl.load(Win, 0, D, nb * 512, 512)
            for ti in range(NT):
                ps = PS[ti % 2]
                for kt in range(KT):
                    p.mm(ps, ps[:, :], hT, hT[:, kt, ti * 128:(ti + 1) * 128], wb, wb[:, kt, :], start=(kt == 0), stop=(kt == KT - 1))
                p.op("act", "copy", reads=[ps], writes=[qs], out=qs[:], in_=ps[:])
                p.op("act", "activation", reads=[qs], writes=[qsq], out=qsq[:], in_=qs[:], func=AF.Square)
                p.op("dve", "reduce_sum", reads=[qsq], writes=[st4], out=st4[:, 0:4], in_=qsq[:].rearrange("p (h d) -> p h d", d=DH), axis=AX.X)
                p.op("dve", "tensor_scalar", reads=[st4], writes=[st4], out=st4[:, 0:4], in0=st4[:, 0:4], scalar1=1.0 / DH, scalar2=EPS, op0=ALU.mult, op1=ALU.add)
                p.op("act", "activation", reads=[st4], writes=[st4], out=st4[:, 0:4], in_=st4[:, 0:4], func=AF.Sqrt)
                p.op("dve", "reciprocal", reads=[st4], writes=[st4], out=st4[:, 4:8], in_=st4[:, 0:4])
                p.op("dve", "tensor_scalar", reads=[st4], writes=[st4], out=st4[:, 4:8], in0=st4[:, 4:8], scalar1=DH ** -0.5, scalar2=None, op0=ALU.mult)
                q3 = qs[:].rearrange("p (h d) -> p h d", d=DH)
                p.op("dve", "tensor_tensor", reads=[qs, st4], writes=[qs], out=q3, in0=q3, in1=st4[:, 4:8].unsqueeze(2).to_broadcast([128, 4, DH]), op=ALU.mult)
                p.op("dve", "tensor_tensor", reads=[qs, qg], writes=[qs], out=q3, in0=q3, in1=qg[:].unsqueeze(1).to_broadcast([128, 4, DH]), op=ALU.mult)
                pst = PS[6 + ti % 2]
                for j in range(4):
                    p.op("pe", "transpose", reads=[qs, ident], writes=[pst], out=pst[:, j * 128:(j + 1) * 128], in_=qs[:, j * 128:(j + 1) * 128], identity=ident[:])
                p.op("dve", "tensor_copy", reads=[pst], writes=[qTg], out=qTg[:, nb * 4:(nb + 1) * 4, ti * 128:(ti + 1) * 128],
                     in_=pst[:].rearrange("p (a b) -> p a b", a=4))
        for hh in range(NHD):
            p.dma("pool", QT[hh * 128:(hh + 1) * 128, t0:t0 + TG], qTg[:, hh, :], reads=[qTg])
        wb = wl.load(Win, 0, D, 2048, 256)
        for ti in range(NT):
            blk = (t0 // 128) + ti
            ps = PS[2 + ti % 2]
            for kt in range(KT):
                p.mm(ps, ps[:, 0:256], hT, hT[:, kt, ti * 128:(ti + 1) * 128], wb, wb[:, kt, 0:256], start=(kt == 0), stop=(kt == KT - 1))
            p.op("act", "copy", reads=[ps], writes=[Vs], out=Vs[:, blk, 0:128], in_=ps[:, 128:256])
            p.op("act", "copy", reads=[ps], writes=[qs], out=qs[:, 0:128], in_=ps[:, 0:128])
            rmsnorm_tile(p, qs, qs[:, 0:128], kg, qs, qs[:, 0:128], qsq, ss, d=DH)
            pst = PS[6 + ti % 2]
            p.op("pe", "transpose", reads=[qs, ident], writes=[pst], out=pst[:, 0:128], in_=qs[:, 0:128], identity=ident[:])
            p.op("dve", "tensor_copy", reads=[pst], writes=[kT], out=kT[:, blk * 128:(blk + 1) * 128], in_=pst[:, 0:128])
        for nb in range(2):
            wb = wl.load(Win, 0, D, 2304 + nb * 512, 512)
            for j in range(4):
                ps = PS[j % 2]
                for kt in range(KT):
                    p.mm(ps, ps[:, 0:TG], wb, wb[:, kt, j * 128:(j + 1) * 128], hT, hT[:, kt, :], start=(kt == 0), stop=(kt == KT - 1))
                o = qi_sb[j % 2]
                p.op("act", "copy", reads=[ps], writes=[o], out=o[:, 0:TG], in_=ps[:, 0:TG])
                r0 = nb * 512 + j * 128
                p.dma("pool", QIT[r0:r0 + 128, t0:t0 + TG], o[:, 0:TG], reads=[o])
        wb = wl.load(Win, 0, D, 3328, 80)
        ps = PS[4]
        for kt in range(KT):
            p.mm(ps, ps[0:64, 0:TG], wb, wb[:, kt, 0:64], hT, hT[:, kt, :], start=(kt == 0), stop=(kt == KT - 1))
        p.op("act", "copy", reads=[ps], writes=[kiT], out=kiT[:, t0:t0 + TG], in_=ps[0:64, 0:TG])
        for ti in range(NT):
            blk = (t0 // 128) + ti
            ps = PS[5]
            for kt in range(KT):
                p.mm(ps, ps[:, 0:16], hT, hT[:, kt, ti * 128:(ti + 1) * 128], wb, wb[:, kt, 64:80], start=(kt == 0), stop=(kt == KT - 1))
            p.op("act", "activation", reads=[ps], writes=[wis], out=wis[:, blk, :], in_=ps[:, 0:16], func=AF.Copy, scale=1.0 / 32.0)
    p.pop()
    p.push()
    NBt = p.sb([128, 2, NHD, 128], F32, "NBt")
    CB = p.sb([128, NHD], F32, "CB")
    p.dma("sp", NBt[:], W["t5_bt"], writes=[NBt])
    p.dma("sp", CB[:], W["t5_cb"], writes=[CB])
    for k2 in range(2):
        p.op("dve", "tensor_tensor", reads=[NBt, CB], writes=[NBt], out=NBt[:, k2, :, :], in0=NBt[:, k2, :, :],
             in1=CB[:].unsqueeze(2).to_broadcast([128, NHD, 128]), op=ALU.subtract)
    identb = p.sb([128, 128], BF16, "identb")
    p.op("pool", "tensor_copy", reads=[ident], writes=[identb], out=identb[:], in_=ident[:])
    SC = p.sb([128, S], F32, "SC")
    CM = p.sb([128, 128], F32, "CM")
    p.op("pool", "memset", writes=[CM], ap=CM[:], constant=0.0)
    p.op("pool", "affine_select", reads=[CM], writes=[CM], out=CM[:], in_=CM[:], pattern=[[-1, 128]], compare_op=ALU.is_ge,
         fill=NEG, base=0, channel_multiplier=1)
    Wk = p.sb([128, S], F32, "Wk")
    m8 = p.sb([128, 16], F32, "m8")
    Mk = [p.sb([128, 512], BF16, "Mk") for _ in range(2)]
    MT = p.sb([128, NB, 128], BF16, "MT")
    qTb = p.sb([128, NHD, 128], BF16, "qTb")
    qiTb = p.sb([64, NHD, 128], BF16, "qiTb")
    rl = [p.sb([128, 512], F32, "rl") for _ in range(2)]
    Pe = [p.sb([128, 512], BF16, "Pe") for _ in range(4)]
    Pm = [p.sb([128, 512], BF16, "Pm") for _ in range(4)]
    numsb = p.sb([128, 4, 132], F32, "numsb")
    NBb = p.sb([128, 2, NHD, 128], BF16, "NBb")
    p.op("pool", "tensor_copy", reads=[NBt], writes=[NBb], out=NBb[:], in_=NBt[:])
    mo = p.sb([128, D], F32, "mo")
    rd = p.sb([128, 16], F32, "rd")
    acc = [PS[i] for i in range(4)]
    for qb in range(NB):
        nkb = qb + 1
        ncols = nkb * 128
        tcols = slice(qb * 128, (qb + 1) * 128)
        p.dma("sp", qTb[:], QT[:, tcols].rearrange("(h d) t -> d h t", d=128), writes=[qTb])
        p.dma("sp", qiTb[:], QIT[:, tcols].rearrange("(h d) t -> d h t", d=64), writes=[qiTb])
        for c0 in (() if 'scores' in skip else range(0, ncols, 512)):
            cw = min(512, ncols - c0)
            for hh in range(NHD):
                ps = PS[6 + hh % 2]
                p.mm(ps, ps[:, 0:cw], qiTb, qiTb[:, hh, :], kiT, kiT[:, c0:c0 + cw])
                r_ = rl[hh % 2]
                p.op("act", "activation", reads=[ps], writes=[r_], out=r_[:, 0:cw], in_=ps[:, 0:cw], func=AF.Relu)
                if hh == 0:
                    p.op("dve", "tensor_scalar", reads=[r_, wis], writes=[SC], out=SC[:, c0:c0 + cw], in0=r_[:, 0:cw], scalar1=wis[:, qb, 0:1], scalar2=None, op0=ALU.mult)
                else:
                    p.op("dve", "scalar_tensor_tensor", reads=[r_, wis, SC], writes=[SC], out=SC[:, c0:c0 + cw], in0=r_[:, 0:cw], scalar=wis[:, qb, hh:hh + 1],
                         in1=SC[:, c0:c0 + cw], op0=ALU.mult, op1=ALU.add)
        p.op("pool", "tensor_tensor", reads=[SC, CM], writes=[SC], out=SC[:, qb * 128:(qb + 1) * 128], in0=SC[:, qb * 128:(qb + 1) * 128],
             in1=CM[:], op=ALU.add)
        NSEL = min(256, S // 4)
        if ncols > NSEL and 'topk' not in skip:
            src = SC
            for r in range(NSEL // 8):
                p.op("dve", "max", reads=[src], writes=[m8], out=m8[:, 0:8], in_=src[:, 0:ncols])
                if r < NSEL // 8 - 1:
                    p.op("dve", "match_replace", reads=[m8, src], writes=[Wk], out=Wk[:, 0:ncols], in_to_replace=m8[:, 0:8], in_values=src[:, 0:ncols], imm_value=NEG)
                    src = Wk
            p.op("dve", "tensor_reduce", reads=[m8], writes=[m8], out=m8[:, 8:9], in_=m8[:, 0:8], axis=AX.X, op=ALU.min)
        else:
            p.op("pool", "memset", writes=[m8], ap=m8[:, 8:9], constant=-1.0e29)
        if dbg is not None and qb == 1:
            p.dma("pool", dbg["SC"][:, :], SC[:, 0:256], reads=[SC])
            p.dma("pool", dbg["m8"][:, :], m8[:], reads=[m8])
        for c0 in range(0, ncols, 512):
            cw = min(512, ncols - c0)
            i2 = (c0 // 512) % 2
            p.op("dve", "tensor_scalar", reads=[SC, m8], writes=[Mk[i2]], out=Mk[i2][:, 0:cw], in0=SC[:, c0:c0 + cw], scalar1=m8[:, 8:9], scalar2=None, op0=ALU.is_ge)
            pst = PS[6 + i2]
            pv = pst[:].bitcast(BF16)
            for j in range(cw // 128):
                p.op("pe", "transpose", reads=[Mk[i2], identb], writes=[pst], out=pv[:, j * 128:(j + 1) * 128], in_=Mk[i2][:, j * 128:(j + 1) * 128], identity=identb[:])
            p.op("act", "copy", reads=[pst], writes=[MT], out=MT[:, c0 // 128:c0 // 128 + cw // 128, :], in_=pv[:, 0:cw].rearrange("p (a b) -> p a b", b=128))
        if dbg is not None and qb == 1:
            p.dma("pool", dbg["MT"][:, :, :], MT[:, 0:2, :], reads=[MT])
        ixc = 0
        chunks = [] if 'attn' in skip else [(hp, kb) for hp in range(4) for kb in range(nkb)]

        def front(i):
            hp, kb = chunks[i]
            hs = slice(hp * 4, (hp + 1) * 4)
            near = (qb - kb) <= 1
            ix = i % 4
            ps = PS[4 + ix]
            p.mm(ps, ps[:, :], kT, kT[:, kb * 128:(kb + 1) * 128], qTb, qTb[:, hs, :].rearrange("p a b -> p (a b)"), start=True, stop=(not near))
            if near:
                p.mm(ps, ps[:, :], identb, identb[:], NBb, NBb[:, qb - kb, hs, :].rearrange("p a b -> p (a b)"), start=False, stop=True)
            pe_ = Pe[ix]
            p.op("act", "activation", reads=[ps], writes=[pe_], out=pe_[:], in_=ps[:], func=AF.Exp)
            pm_ = Pm[ix]
            p.op("pool" if ix % 2 == 0 else "dve", "tensor_tensor", reads=[pe_, MT], writes=[pm_], out=pm_[:].rearrange("p (a b) -> p a b", a=4),
                 in0=pe_[:].rearrange("p (a b) -> p a b", a=4), in1=MT[:, kb, :].unsqueeze(1).to_broadcast([128, 4, 128]), op=ALU.mult)

        def back(i):
            hp, kb = chunks[i]
            pm_ = Pm[i % 4]
            for j in range(4):
                a_ = acc[j]
                p.mm(a_, a_[:, 0:129], pm_, pm_[:, j * 128:(j + 1) * 128], Vs, Vs[:, kb, 0:129], start=(kb == 0), stop=(kb == nkb - 1))
            if kb == nkb - 1:
                for j in range(4):
                    p.op("act", "copy", reads=[acc[j]], writes=[numsb], out=numsb[:, j, 0:129], in_=acc[j][:, 0:129])
                p.op("dve", "reciprocal", reads=[numsb], writes=[rd], out=rd[:, 0:4], in_=numsb[:, :, 128:129].rearrange("p a b -> p (a b)"))
                p.op("dve", "tensor_tensor", reads=[numsb, rd], writes=[mo], out=mo[:, hp * 512:(hp + 1) * 512].rearrange("p (a b) -> p a b", a=4),
                     in0=numsb[:, :, 0:128], in1=rd[:, 0:4].unsqueeze(2).to_broadcast([128, 4, 128]), op=ALU.mult)

        LA = 3
        for i in range(len(chunks) + LA):
            if i < len(chunks):
                front(i)
            if i - LA >= 0:
                back(i - LA)
        p.dma("pool", MO[tcols, :], mo[:], reads=[mo])
    p.pop()
    p.pop()
    p.push()
    def make_tile(r0, mo_):
        p.dma("sp", mo_[:], MO[r0:r0 + 128, :], writes=[mo_])
    out_proj_phase(c, X, S, make_tile, W["w_o"])
    p.pop()


def t5_tables(t5_bias):
    n = np.arange(256)
    nf = np.maximum(n, 16).astype(np.float32)
    large = 16 + (np.log(nf / np.float32(16)) / np.float32(np.log(8.0)) * np.float32(16)).astype(np.int32)
    bucket = np.where(n < 16, n, np.minimum(large, 31))
    s = np.arange(128)[:, None]
    t = np.arange(128)[None, :]
    bt = np.empty((128, 2, 16, 128), np.float32)
    for k2 in range(2):
        d = np.clip(t - s + 128 * k2, 0, 255)
        bt[:, k2] = np.transpose(t5_bias[bucket[d]], (0, 2, 1))
    cb = np.ascontiguousarray(np.broadcast_to(t5_bias[31][None, :], (128, 16))).astype(np.float32)
    return bt, cb


SEQ = 8192
BATCH = 4
N_CORES = 4


def build_model(S):
    p = Prog()
    c = setup_consts(p)
    din = p.dram_in
    x = din("x", [S, D])
    A = {}
    shapes = {
        "rwkv_mu_fm": [2, 128, 96], "rwkv_w_rkv": [2, 3, D, D], "rwkv_w0": [2, D], "rwkv_w1": [2, D, 96], "rwkv_w2": [2, 96, D],
        "rwkv_a0": [2, D], "rwkv_a1": [2, D, 96], "rwkv_a2": [2, 96, D], "rwkv_v0": [1, D], "rwkv_v1": [1, D, 64], "rwkv_v2": [1, 64, D],
        "rwkv_g1": [2, D, 256], "rwkv_g2": [2, 256, D], "rwkv_k_k": [2, D], "rwkv_k_a": [2, D], "rwkv_r_k": [2, D],
        "rwkv_lnx_g": [2, D], "rwkv_lnx_b": [2, D], "rwkv_w_o": [2, D, D],
        "mlstm_w_in": [1, D, 6152], "mlstm_b_if": [1, 2, 4], "mlstm_norm_g": [1, D], "mlstm_w_o": [1, D, D],
        "dsa_w_in": [1, D, 3408], "dsa_q_norm_g": [1, 128], "dsa_k_norm_g": [1, 128], "dsa_w_o": [1, D, D],
        "t5_bt": [128, 2, 16, 128], "t5_cb": [128, 16],
        "mix_norm_g": [4, D], "ffn_norm_g": [4, D], "ffn_w_gate": [4, D, DFF], "ffn_w_up": [4, D, DFF], "ffn_w_down": [4, DFF, D],
    }
    for k, v in shapes.items():
        A[k] = din(k, v)
    y = p.dram_out("y", [S, D])
    sc = {k: p.dram_tmp("sc_" + k, [S, D]) for k in RW_SC}
    sc["QK"] = p.dram_tmp("sc_QK", [2048, S], BF16)
    sc["Vb"] = p.dram_tmp("sc_Vb", [S, D], BF16)
    sc["SO"] = sc["R"]
    sc["MO"] = sc["Y"]
    sc["QT"] = sc["QK"]
    sc["QIT"] = p.dram_tmp("sc_QIT", [1024, S], BF16)
    MATS = ["rwkv_w_rkv", "rwkv_w1", "rwkv_w2", "rwkv_a1", "rwkv_a2", "rwkv_v1", "rwkv_v2", "rwkv_g1", "rwkv_g2", "rwkv_w_o",
            "mlstm_w_in", "mlstm_w_o", "dsa_w_in", "dsa_w_o", "ffn_w_gate", "ffn_w_up", "ffn_w_down"]
    pairs = []
    for k in MATS:
        shp = shapes[k]
        b = p.dram_tmp("bf_" + k, shp, BF16)
        R = int(np.prod(shp[:-1]))
        pat = {3: "a k n -> (a k) n", 4: "a b k n -> (a b k) n"}[len(shp)]
        pairs.append((A[k].rearrange(pat), b.rearrange(pat), R, shp[-1]))
        A[k] = b
    convert_weights(c, pairs)
    TR = min(S, 1024)
    for r0 in range(0, S, TR):
        p.dma("sp", y[r0:r0 + TR, :], x[r0:r0 + TR, :])
    p.barrier()
    for i in range(4):
        kind, j = i % 3, i // 3
        if kind == 0:
            W = {"mix_g": A["mix_norm_g"][i], "mu_fm": A["rwkv_mu_fm"][j], "w_rkv": A["rwkv_w_rkv"][j]}
            for nm in ["w0", "w1", "w2", "a0", "a1", "a2", "g1", "g2", "k_k", "k_a", "r_k", "lnx_g", "lnx_b", "w_o"]:
                W[nm] = A["rwkv_" + nm][j]
            if j > 0:
                for nm in ["v0", "v1", "v2"]:
                    W[nm] = A["rwkv_" + nm][j - 1]
            rwkv_prep(c, y, S, W, sc, j == 0)
            rwkv_scan(c, S, sc)
            rwkv_post(c, y, S, W, sc)
        elif kind == 1:
            W = {"mix_g": A["mix_norm_g"][i], "w_in": A["mlstm_w_in"][j], "b_if": A["mlstm_b_if"][j], "norm_g": A["mlstm_norm_g"][j], "w_o": A["mlstm_w_o"][j]}
            mlstm_layer(c, y, S, W, sc)
        else:
            W = {"mix_g": A["mix_norm_g"][i], "w_in": A["dsa_w_in"][j], "q_norm_g": A["dsa_q_norm_g"][j], "k_norm_g": A["dsa_k_norm_g"][j],
                 "w_o": A["dsa_w_o"][j], "t5_bt": A["t5_bt"], "t5_cb": A["t5_cb"]}
            dsa_layer(c, y, S, W, sc)
        ffn_phase(c, y, S, A["ffn_norm_g"][i], A["ffn_w_gate"][i], A["ffn_w_up"][i], A["ffn_w_down"][i])
    return p


def host_inputs(inputs):
    f = lambda a: np.ascontiguousarray(np.asarray(a, dtype=np.float32))
    feed = {}
    mu = f(inputs["rwkv_mu"])
    feed["rwkv_mu_fm"] = np.ascontiguousarray(mu.reshape(mu.shape[0], 6, 16, 128).transpose(0, 3, 1, 2).reshape(mu.shape[0], 128, 96))
    for k in ["rwkv_w_rkv", "rwkv_w0", "rwkv_w1", "rwkv_w2", "rwkv_a0", "rwkv_a1", "rwkv_a2", "rwkv_v0", "rwkv_v1", "rwkv_v2", "rwkv_g1", "rwkv_g2",
              "rwkv_k_k", "rwkv_k_a", "rwkv_lnx_g", "rwkv_lnx_b", "rwkv_w_o", "mlstm_w_in", "mlstm_b_if", "mlstm_norm_g", "mlstm_w_o",
              "dsa_w_in", "dsa_q_norm_g", "dsa_k_norm_g", "dsa_w_o", "mix_norm_g", "ffn_norm_g", "ffn_w_gate", "ffn_w_up", "ffn_w_down"]:
        feed[k] = f(inputs[k])
    rk = f(inputs["rwkv_r_k"])
    feed["rwkv_r_k"] = rk.reshape(rk.shape[0], -1)
    bt, cb = t5_tables(f(inputs["t5_bias"]))
    feed["t5_bt"] = bt
    feed["t5_cb"] = cb
    return feed


_CACHE = {}


def kernel(**inputs):
    x = np.asarray(inputs["x"], dtype=np.float32)
    B, S, _ = x.shape
    if S not in _CACHE:
        _CACHE[S] = build_model(S).finish()
    nc = _CACHE[S]
    feed = host_inputs(inputs)
    in_maps = []
    for b in range(B):
        m = dict(feed)
        m["x"] = np.ascontiguousarray(x[b])
        in_maps.append(m)
    res = run_bass_kernel_spmd(nc, in_maps, core_ids=list(range(B)))
    return np.stack([np.asarray(res.results[b]["y"], dtype=np.float32) for b in range(B)], axis=0)
```

```python
import contextlib
import numpy as np
import concourse.bass as bass
import concourse.mybir as mybir
from concourse.bass_utils import run_bass_kernel_spmd

F32 = mybir.dt.float32
BF16 = mybir.dt.bfloat16
ALU = mybir.AluOpType
AF = mybir.ActivationFunctionType
AX = mybir.AxisListType


class T:
    __slots__ = ("ap", "lw", "rd", "name", "root", "excl")

    def __init__(self, ap, name="", root=None):
        self.ap = ap
        self.lw = None
        self.rd = {}
        self.name = name
        self.root = root.root if root is not None else self
        self.excl = False

    def sub(self, rows, a, b):
        return T(self.ap[0:rows, a:b], self.name + "_v", root=self)

    def __getitem__(self, idx):
        return self.ap[idx]


class Prog:
    ENGS = ("pe", "dve", "act", "pool", "sp")

    def __init__(self, name="k", n_dma_sems=12, same_engine_sync=True, nc=None):
        self.nc = nc if nc is not None else bass.Bass("TRN2", target_bir_lowering=False)
        self.es = contextlib.ExitStack()
        self.q = {e: [] for e in self.ENGS}
        self.cnt = {e: 0 for e in self.ENGS}
        self.seen = {e: {} for e in self.ENGS}
        self.sem = {}
        for e in self.ENGS:
            self.sem[e] = self.es.enter_context(self.nc.semaphore("s_" + e))
        self.dma_pool = {}
        self.dma_rr = {}
        self.dma_cnt = {}
        for qe in ("sp", "pool", "act"):
            names = []
            for j in range(n_dma_sems):
                k = "d_%s_%d" % (qe, j)
                self.sem[k] = self.es.enter_context(self.nc.semaphore(k))
                self.dma_cnt[k] = 0
                names.append(k)
            self.dma_pool[qe] = names
            self.dma_rr[qe] = 0
        self.same_engine_sync = same_engine_sync
        self.n_inst = 0
        self._uid = 0
        self.scopes = []

    def uid(self, p):
        self._uid += 1
        return "%s%d" % (p, self._uid)

    def push(self):
        self.scopes.append(contextlib.ExitStack())

    def pop(self):
        self.barrier()
        self.scopes.pop().close()

    def sb(self, shape, dt=F32, name=None):
        es = self.scopes[-1] if self.scopes else self.es
        nm = self.uid(name or "sb")
        h = es.enter_context(self.nc.sbuf_tensor(nm, list(shape), dt))
        return T(h, nm)

    def ps(self, shape, dt=F32, name=None):
        es = self.scopes[-1] if self.scopes else self.es
        nm = self.uid(name or "ps")
        h = es.enter_context(self.nc.psum_tensor(nm, list(shape), dt))
        t = T(h, nm)
        t.excl = True
        return t

    def barrier(self):
        targets = {}
        for e in self.ENGS:
            if self.cnt[e]:
                targets[e] = self.cnt[e]
        for k, c in self.dma_cnt.items():
            if c:
                targets[k] = 16 * c
        for e in self.ENGS:
            for k, v in targets.items():
                if self.seen[e].get(k, 0) < v:
                    self.q[e].append(("w", k, v))
                    self.seen[e][k] = v

    def dram_in(self, name, shape, dt=F32):
        return self.nc.dram_tensor(name, list(shape), dt, kind="ExternalInput").ap()

    def dram_out(self, name, shape, dt=F32):
        return self.nc.dram_tensor(name, list(shape), dt, kind="ExternalOutput").ap()

    def dram_tmp(self, name, shape, dt=F32):
        return self.nc.dram_tensor(name, list(shape), dt, kind="Internal").ap()

    def _deps(self, e, reads, writes):
        deps = {}

        def add(t):
            if t is None:
                return
            k, v = t
            if deps.get(k, 0) < v:
                deps[k] = v

        reads = [r.root for r in reads]
        writes = [w.root for w in writes] + [r for r in reads if r.excl]
        for r in reads:
            add(r.lw)
        for w in writes:
            add(w.lw)
            for t in w.rd.values():
                add(t)
        for k, v in deps.items():
            if k == e and (e == "pe" or not self.same_engine_sync):
                continue
            if self.seen[e].get(k, 0) < v:
                self.q[e].append(("w", k, v))
                self.seen[e][k] = v

    def _mark(self, ticket, reads, writes):
        reads = [r.root for r in reads]
        writes = [w.root for w in writes] + [r for r in reads if r.excl]
        for r in reads:
            r.rd[ticket[0]] = ticket
        for w in writes:
            w.lw = ticket
            w.rd = {}

    def op(self, e, meth, reads=(), writes=(), **kw):
        self._deps(e, reads, writes)
        self.cnt[e] += 1
        ticket = (e, self.cnt[e])
        self.q[e].append(("o", (meth, kw), e, 1))
        self._mark(ticket, reads, writes)
        self.n_inst += 1

    def dma(self, qe, out, in_, reads=(), writes=(), **kw):
        pool = self.dma_pool[qe]
        k = pool[self.dma_rr[qe] % len(pool)]
        self.dma_rr[qe] += 1
        self._deps(qe, reads, writes)
        prev = 16 * self.dma_cnt[k]
        if prev and self.seen[qe].get(k, 0) < prev:
            self.q[qe].append(("w", k, prev))
            self.seen[qe][k] = prev
        self.dma_cnt[k] += 1
        ticket = (k, 16 * self.dma_cnt[k])
        kw = dict(kw); kw["out"] = out; kw["in_"] = in_
        self.q[qe].append(("o", ("dma_start", kw), k, 16))
        self._mark(ticket, reads, writes)
        self.n_inst += 1

    def coll(self, kind, alu, ins, outs, groups):
        qe = "pool"
        pool = self.dma_pool[qe]
        k = pool[self.dma_rr[qe] % len(pool)]
        self.dma_rr[qe] += 1
        prev = 16 * self.dma_cnt[k]
        if prev and self.seen[qe].get(k, 0) < prev:
            self.q[qe].append(("w", k, prev))
            self.seen[qe][k] = prev
        self.dma_cnt[k] += 1
        self.q[qe].append(("o", ("collective_compute", dict(kind=kind, op=alu, replica_groups=groups, ins=ins, outs=outs)), k, 16))
        self.n_inst += 1

    def mm(self, out_t, out_ap, lhsT_t, lhsT_ap, rhs_t, rhs_ap, start=True, stop=True):
        self.op("pe", "matmul", reads=[lhsT_t, rhs_t], writes=[out_t],
                out=out_ap, lhsT=lhsT_ap, rhs=rhs_ap, start=start, stop=stop)

    def finish(self):
        for qe, pool in self.dma_pool.items():
            for k in pool:
                v = 16 * self.dma_cnt[k]
                if v and self.seen[qe].get(k, 0) < v:
                    self.q[qe].append(("w", k, v))
                    self.seen[qe][k] = v
        nc = self.nc
        engmap = {"pe": "tensor", "dve": "vector", "act": "scalar", "pool": "gpsimd", "sp": "sync"}
        with nc.Block() as block:
            for e in self.ENGS:
                items = self.q[e]
                if not items:
                    continue

                def body(eng, items=items):
                    for it in items:
                        if it[0] == "w":
                            eng.wait_ge(self.sem[it[1]], it[2])
                        else:
                            try:
                                ins = getattr(eng, it[1][0])(**it[1][1])
                            except Exception:
                                print("FAILED INSTR", it[1][0], {k: (str(v)[:300]) for k, v in it[1][1].items()})
                                raise
                            ins.then_inc(self.sem[it[2]], it[3])

                getattr(block, engmap[e])(body)
        self.es.close()
        return nc


D = 2048
DFF = 5632
KT = 16
EPS = 1e-6


class Ctx:
    pass


def setup_consts(p):
    c = Ctx()
    c.p = p
    c.ident = p.sb([128, 128], F32, "ident")
    p.op("pool", "memset", writes=[c.ident], ap=c.ident[:], constant=1.0)
    p.op("pool", "affine_select", reads=[c.ident], writes=[c.ident], out=c.ident[:], in_=c.ident[:],
         pattern=[[-1, 128]], compare_op=ALU.is_equal, fill=0.0, base=0, channel_multiplier=1)
    c.PS = [p.ps([128, 512], F32, "bank%d" % i) for i in range(8)]
    c.rr = 0
    return c


def bcast_row(p, dst, src_ap_1d, n, q="sp"):
    p.dma(q, dst[:, 0:n], src_ap_1d.partition_broadcast(128), writes=[dst])


def rmsnorm_tile(p, x_t, x_ap, g_t, out_t, out_ap, tmp_t, ss_t, d=D, eps=EPS):
    p.op("act", "activation", reads=[x_t], writes=[tmp_t], out=tmp_t[:, 0:d], in_=x_ap, func=AF.Square)
    p.op("dve", "reduce_sum", reads=[tmp_t], writes=[ss_t], out=ss_t[:, 0:1], in_=tmp_t[:, 0:d], axis=AX.X)
    p.op("dve", "tensor_scalar", reads=[ss_t], writes=[ss_t], out=ss_t[:, 1:2], in0=ss_t[:, 0:1],
         scalar1=1.0 / d, scalar2=eps, op0=ALU.mult, op1=ALU.add)
    p.op("act", "activation", reads=[ss_t], writes=[ss_t], out=ss_t[:, 1:2], in_=ss_t[:, 1:2], func=AF.Sqrt)
    p.op("dve", "reciprocal", reads=[ss_t], writes=[ss_t], out=ss_t[:, 0:1], in_=ss_t[:, 1:2])
    p.op("dve", "scalar_tensor_tensor", reads=[x_t, ss_t, g_t], writes=[out_t], out=out_ap, in0=x_ap,
         scalar=ss_t[:, 0:1], in1=g_t[:, 0:d], op0=ALU.mult, op1=ALU.mult)


def to_fm(c, src_t, src_ap_fn, nkt, dstT, col0, evac_alt=0):
    p = c.p
    for g in range(0, nkt, 4):
        n = min(4, nkt - g)
        ps = c.PS[6 + (c.rr % 2)]
        c.rr += 1
        for j in range(n):
            p.op("pe", "transpose", reads=[src_t, c.ident], writes=[ps], out=ps[:, j * 128:(j + 1) * 128],
                 in_=src_ap_fn(g + j), identity=c.ident[:])
        eng = "dve" if (c.rr % 2) else "act"
        src = ps[:, 0:n * 128].rearrange("p (a b) -> p a b", a=n)
        dst = dstT[:, g:g + n, col0:col0 + 128]
        if eng == "dve":
            p.op("dve", "tensor_copy", reads=[ps], writes=[dstT], out=dst, in_=src)
        else:
            p.op("act", "copy", reads=[ps], writes=[dstT], out=dst, in_=src)


class WLoader:
    def __init__(self, c, nbuf=4):
        p = c.p
        self.c = c
        self.wb = [p.sb([128, 16, 512], BF16, "wb") for _ in range(nbuf)]
        self.i = 0

    def load(self, W, k0, ksz, n0, nsz):
        p = self.c.p
        wb = self.wb[self.i % len(self.wb)]
        self.i += 1
        nk = ksz // 128
        p.dma("sp", wb[:, 0:nk, 0:nsz], W[k0:k0 + ksz, n0:n0 + nsz].rearrange("(kt p) n -> p kt n", p=128), writes=[wb])
        return wb


def convert_weights(c, pairs):
    p = c.p
    p.push()
    st = [p.sb([128, 4096], F32, "cst") for _ in range(3)]
    cb = [p.sb([128, 4096], BF16, "ccb") for _ in range(3)]
    n = 0
    for src, dst, R, C in pairs:
        for r0 in range(0, R, 128):
            rs = min(128, R - r0)
            for c0 in range(0, C, 4096):
                cs = min(4096, C - c0)
                a = st[n % 3]
                b = cb[n % 3]
                p.dma("sp", a[0:rs, 0:cs], src[r0:r0 + rs, c0:c0 + cs], writes=[a])
                eng = ("act", "act", "dve")[n % 3]
                if eng == "act":
                    p.op("act", "copy", reads=[a], writes=[b], out=b[0:rs, 0:cs], in_=a[0:rs, 0:cs])
                else:
                    p.op(eng, "tensor_copy", reads=[a], writes=[b], out=b[0:rs, 0:cs], in_=a[0:rs, 0:cs])
                p.dma("pool", dst[r0:r0 + rs, c0:c0 + cs], b[0:rs, 0:cs], reads=[b])
                n += 1
    p.pop()


def ffn_phase(c, X, S, gvec, Wg, Wu, Wd):
    p = c.p
    TG = min(512, S)
    NT = TG // 128
    NF = DFF // 128
    p.push()
    wl = WLoader(c)
    g_bc = p.sb([128, D], F32, "g_bc")
    bcast_row(p, g_bc, gvec, D)
    xres = [p.sb([128, D], F32, "xres") for _ in range(NT)]
    hn = p.sb([128, D], F32, "hn")
    tmp = p.sb([128, D], F32, "tmp")
    ss = p.sb([128, 2], F32, "ss")
    hT = p.sb([128, KT, TG], BF16, "hT")
    hidT = p.sb([128, NF, TG], BF16, "hidT")
    sg = [p.sb([128, TG], F32, "sg") for _ in range(2)]
    ot = [p.sb([128, 512], F32, "ot") for _ in range(2)]
    for t0 in range(0, S, TG):
        for ti in range(NT):
            p.dma("sp", xres[ti][:], X[t0 + ti * 128:t0 + (ti + 1) * 128, :], writes=[xres[ti]])
            rmsnorm_tile(p, xres[ti], xres[ti][:], g_bc, hn, hn[:], tmp, ss)
            to_fm(c, hn, lambda kt: hn[:, kt * 128:(kt + 1) * 128], KT, hT, ti * 128)
        for fb in range(0, NF, 4):
            nf = min(4, NF - fb)
            wg = wl.load(Wg, 0, D, fb * 128, nf * 128)
            wu = wl.load(Wu, 0, D, fb * 128, nf * 128)
            for j in range(nf):
                fc = fb + j
                pg = c.PS[(fc % 2) * 2]
                pu = c.PS[(fc % 2) * 2 + 1]
                for kt in range(KT):
                    p.mm(pg, pg[:, 0:TG], wg, wg[:, kt, j * 128:(j + 1) * 128], hT, hT[:, kt, :], start=(kt == 0), stop=(kt == KT - 1))
                for kt in range(KT):
                    p.mm(pu, pu[:, 0:TG], wu, wu[:, kt, j * 128:(j + 1) * 128], hT, hT[:, kt, :], start=(kt == 0), stop=(kt == KT - 1))
                s_ = sg[fc % 2]
                p.op("act", "activation", reads=[pg], writes=[s_], out=s_[:, 0:TG], in_=pg[:, 0:TG], func=AF.Silu)
                p.op("dve", "tensor_tensor", reads=[s_, pu], writes=[hidT], out=hidT[:, fc, :], in0=s_[:, 0:TG], in1=pu[:, 0:TG], op=ALU.mult)
        for nch in range(D // 512):
            for kb in range(0, NF, 16):
                nk = min(16, NF - kb)
                wd = wl.load(Wd, kb * 128, nk * 128, nch * 512, 512)
                for ti in range(NT):
                    ps = c.PS[ti]
                    for kt in range(nk):
                        p.mm(ps, ps[:, :], hidT, hidT[:, kb + kt, ti * 128:(ti + 1) * 128], wd, wd[:, kt, :],
                             start=(kb + kt == 0), stop=(kb + kt == NF - 1))
            for ti in range(NT):
                o = ot[ti % 2]
                p.op("dve", "tensor_tensor", reads=[c.PS[ti], xres[ti]], writes=[o], out=o[:], in0=c.PS[ti][:],
                     in1=xres[ti][:, nch * 512:(nch + 1) * 512], op=ALU.add)
                p.dma("pool", X[t0 + ti * 128:t0 + (ti + 1) * 128, nch * 512:(nch + 1) * 512], o[:], reads=[o])
    p.pop()


NH = 32
HS = 64
DECAY_SCALE = 0.6065306597126334


def evac(p, eng, ps_t, src, dst_t, dst):
    if eng == "act":
        p.op("act", "copy", reads=[ps_t], writes=[dst_t], out=dst, in_=src)
    else:
        p.op("dve", "tensor_copy", reads=[ps_t], writes=[dst_t], out=dst, in_=src)


def rwkv_prep(c, X, S, W, sc, first):
    p = c.p
    p.push()
    wl = WLoader(c, nbuf=3)
    names = ["r", "w", "k", "v", "a", "g"]
    g_bc = p.sb([128, D], F32, "g_bc")
    bcast_row(p, g_bc, W["mix_g"], D)
    mu_fm = p.sb([128, 6 * KT], F32, "mu_fm")
    p.dma("sp", mu_fm[:], W["mu_fm"], writes=[mu_fm])
    vnames = ["w0", "a0", "k_k", "k_a", "r_k"] + ([] if first else ["v0"])
    vch = {nm: p.sb([128, 512], F32, "vc_" + nm) for nm in vnames}
    TG = min(256, S)
    NT = TG // 128
    xT = {nm: p.sb([128, KT, TG], BF16, "xT_" + nm) for nm in names}
    xt = p.sb([128, D], F32, "xt")
    hn = p.sb([128, D], F32, "hn")
    ss = p.sb([128, 2], F32, "ss")
    hT = p.sb([128, KT, 128], F32, "hT")
    xxT = p.sb([128, KT, 128], F32, "xxT")
    last = p.sb([128, KT, 1], F32, "last")
    p.op("pool", "memset", writes=[last], ap=last[:], constant=0.0)
    lor = {"w": 96, "a": 96, "g": 256}
    if not first:
        lor["v"] = 64
    lT = {nm: p.sb([128, 2, TG], BF16, "lT_" + nm) for nm in lor}
    l2 = {nm: p.sb([128, 2, 512], BF16, "l2_" + nm) for nm in lor}
    Es = [{nm: p.sb([128, 512], F32, "E_" + nm) for nm in ["r", "k", "v", "w", "a", "g", "kk", "t1", "t2", "vf"]} for _ in range(2)]
    st8s = [p.sb([128, 16], F32, "st8") for _ in range(2)]
    itc = 0
    l1w = {"w": W["w1"], "a": W["a1"], "g": W["g1"]}
    l2w = {"w": W["w2"], "a": W["a2"], "g": W["g2"]}
    if not first:
        l1w["v"] = W["v1"]
        l2w["v"] = W["v2"]
    lfn = {"w": AF.Tanh, "a": AF.Copy, "g": AF.Sigmoid, "v": AF.Copy}
    PS = c.PS
    for t0 in range(0, S, TG):
      for ti in range(NT):
        rows = slice(t0 + ti * 128, t0 + (ti + 1) * 128)
        tsl = slice(ti * 128, (ti + 1) * 128)
        p.dma("sp", xt[:], X[rows, :], writes=[xt])
        rmsnorm_tile(p, xt, xt[:], g_bc, hn, hn[:], hn, ss)
        to_fm(c, hn, lambda kt: hn[:, kt * 128:(kt + 1) * 128], KT, hT, 0)
        p.op("dve", "tensor_tensor", reads=[hT], writes=[xxT], out=xxT[:, :, 1:128], in0=hT[:, :, 0:127], in1=hT[:, :, 1:128], op=ALU.subtract)
        p.op("dve", "tensor_tensor", reads=[hT, last], writes=[xxT], out=xxT[:, :, 0:1], in0=last[:], in1=hT[:, :, 0:1], op=ALU.subtract)
        p.op("act", "copy", reads=[hT], writes=[last], out=last[:], in_=hT[:, :, 127:128])
        for i, nm in enumerate(names):
            for kt in range(KT):
                p.op("dve", "scalar_tensor_tensor", reads=[xxT, mu_fm, hT], writes=[xT[nm]], out=xT[nm][:, kt, tsl], in0=xxT[:, kt, :],
                     scalar=mu_fm[:, i * KT + kt:i * KT + kt + 1], in1=hT[:, kt, :], op0=ALU.mult, op1=ALU.add)
      for nm, ld in lor.items():
            wb = wl.load(l1w[nm], 0, D, 0, ld)
            for j in range(0, ld, 128):
                lsz = min(128, ld - j)
                ps = PS[4]
                for kt in range(KT):
                    p.mm(ps, ps[0:lsz, 0:TG], wb, wb[:, kt, j:j + lsz], xT[nm], xT[nm][:, kt, :], start=(kt == 0), stop=(kt == KT - 1))
                p.op("act", "activation", reads=[ps], writes=[lT[nm]], out=lT[nm][0:lsz, j // 128, :], in_=ps[0:lsz, 0:TG], func=lfn[nm])
      for nch in range(4):
        cs = slice(nch * 512, (nch + 1) * 512)
        for nm in vnames:
            p.dma("sp", vch[nm][:], W[nm][cs].partition_broadcast(128), writes=[vch[nm]])
        for nm, ld in lor.items():
            for j in range(0, ld, 128):
                lsz = min(128, ld - j)
                p.dma("sp", l2[nm][0:lsz, j // 128, :], l2w[nm][j:j + lsz, cs], writes=[l2[nm]])
        wbs = {nm: wl.load(W["w_rkv"][i], 0, D, nch * 512, 512) for i, nm in enumerate(["r", "k", "v"])}
        for ti in range(NT):
            rows = slice(t0 + ti * 128, t0 + (ti + 1) * 128)
            tsl = slice(ti * 128, (ti + 1) * 128)
            E = Es[itc % 2]
            st8 = st8s[itc % 2]
            itc += 1
            for i, nm in enumerate(["r", "k", "v"]):
                wb = wbs[nm]
                ps = PS[i]
                for kt in range(KT):
                    p.mm(ps, ps[:, :], xT[nm], xT[nm][:, kt, tsl], wb, wb[:, kt, :], start=(kt == 0), stop=(kt == KT - 1))
                evac(p, "act", ps, ps[:, :], E[nm], E[nm][:])
            lps = {"w": PS[3], "a": PS[4], "g": PS[5], "v": PS[0]}
            for nm, ld in lor.items():
                nj = (ld + 127) // 128
                for jj in range(nj):
                    lsz = min(128, ld - jj * 128)
                    p.mm(lps[nm], lps[nm][:, :], lT[nm], lT[nm][0:lsz, jj, tsl], l2[nm], l2[nm][0:lsz, jj, :], start=(jj == 0), stop=(jj == nj - 1))
            def tt(eng, out_t, a_t, a, b_t, b, op, o=None):
                p.op(eng, "tensor_tensor", reads=[a_t, b_t], writes=[out_t], out=(o if o is not None else out_t[:]), in0=a, in1=b, op=op)
            tt("dve", E["w"], lps["w"], lps["w"][:], vch["w0"], vch["w0"][:], ALU.add)
            p.op("act", "activation", reads=[E["w"]], writes=[E["w"]], out=E["w"][:], in_=E["w"][:], func=AF.Sigmoid)
            p.op("dve", "tensor_scalar", reads=[E["w"]], writes=[E["w"]], out=E["w"][:], in0=E["w"][:], scalar1=-DECAY_SCALE, scalar2=None, op0=ALU.mult)
            p.dma("pool", sc["LW"][rows, cs], E["w"][:], reads=[E["w"]])
            tt("dve", E["a"], lps["a"], lps["a"][:], vch["a0"], vch["a0"][:], ALU.add)
            p.op("act", "activation", reads=[E["a"]], writes=[E["a"]], out=E["a"][:], in_=E["a"][:], func=AF.Sigmoid)
            evac(p, "act", lps["g"], lps["g"][:], E["g"], E["g"][:])
            p.dma("pool", sc["G"][rows, cs], E["g"][:], reads=[E["g"]])
            if first:
                p.dma("pool", sc["VF"][rows, cs], E["v"][:], reads=[E["v"]])
            else:
                p.dma("sp", E["vf"][:], sc["VF"][rows, cs], writes=[E["vf"]])
                tt("dve", E["t1"], lps["v"], lps["v"][:], vch["v0"], vch["v0"][:], ALU.add)
                p.op("act", "activation", reads=[E["t1"]], writes=[E["t1"]], out=E["t1"][:], in_=E["t1"][:], func=AF.Sigmoid)
                tt("dve", E["vf"], E["vf"], E["vf"][:], E["v"], E["v"][:], ALU.subtract)
                tt("dve", E["vf"], E["vf"], E["vf"][:], E["t1"], E["t1"][:], ALU.mult)
                tt("dve", E["v"], E["v"], E["v"][:], E["vf"], E["vf"][:], ALU.add)
            p.dma("pool", sc["V"][rows, cs], E["v"][:], reads=[E["v"]])
            p.dma("pool", sc["R"][rows, cs], E["r"][:], reads=[E["r"]])
            tt("dve", E["kk"], E["k"], E["k"][:], vch["k_k"], vch["k_k"][:], ALU.mult)
            tt("dve", E["t1"], E["kk"], E["kk"][:], E["kk"], E["kk"][:], ALU.mult)
            p.op("dve", "reduce_sum", reads=[E["t1"]], writes=[st8], out=st8[:, 0:8], in_=E["t1"][:].rearrange("p (h k) -> p h k", k=HS), axis=AX.X)
            p.op("act", "activation", reads=[st8], writes=[st8], out=st8[:, 0:8], in_=st8[:, 0:8], func=AF.Sqrt)
            p.op("dve", "tensor_scalar", reads=[st8], writes=[st8], out=st8[:, 0:8], in0=st8[:, 0:8], scalar1=1e-12, scalar2=None, op0=ALU.max)
            p.op("dve", "reciprocal", reads=[st8], writes=[st8], out=st8[:, 8:16], in_=st8[:, 0:8])
            p.op("dve", "tensor_tensor", reads=[E["kk"], st8], writes=[E["kk"]], out=E["kk"][:].rearrange("p (h k) -> p h k", k=HS),
                 in0=E["kk"][:].rearrange("p (h k) -> p h k", k=HS), in1=st8[:, 8:16].unsqueeze(2).to_broadcast([128, 8, HS]), op=ALU.mult)
            p.op("dve", "scalar_tensor_tensor", reads=[E["a"], vch["k_a"]], writes=[E["t1"]], out=E["t1"][:], in0=E["a"][:], scalar=-1.0, in1=vch["k_a"][:], op0=ALU.add, op1=ALU.mult)
            p.op("dve", "scalar_tensor_tensor", reads=[E["t1"], E["k"]], writes=[E["k"]], out=E["k"][:], in0=E["t1"][:], scalar=1.0, in1=E["k"][:], op0=ALU.add, op1=ALU.mult)
            p.dma("pool", sc["KP"][rows, cs], E["k"][:], reads=[E["k"]])
            tt("dve", E["t2"], E["kk"], E["kk"][:], E["a"], E["a"][:], ALU.mult)
            p.dma("pool", sc["B"][rows, cs], E["t2"][:], reads=[E["t2"]])
            p.op("dve", "tensor_scalar", reads=[E["kk"]], writes=[E["kk"]], out=E["kk"][:], in0=E["kk"][:], scalar1=-1.0, scalar2=None, op0=ALU.mult)
            p.dma("pool", sc["A"][rows, cs], E["kk"][:], reads=[E["kk"]])
            tt("dve", E["t1"], E["r"], E["r"][:], E["k"], E["k"][:], ALU.mult)
            tt("dve", E["t1"], E["t1"], E["t1"][:], vch["r_k"], vch["r_k"][:], ALU.mult)
            p.op("dve", "reduce_sum", reads=[E["t1"]], writes=[st8], out=st8[:, 0:8], in_=E["t1"][:].rearrange("p (h k) -> p h k", k=HS), axis=AX.X)
            p.op("dve", "tensor_tensor", reads=[E["v"], st8], writes=[E["t1"]], out=E["t1"][:].rearrange("p (h k) -> p h k", k=HS),
                 in0=E["v"][:].rearrange("p (h k) -> p h k", k=HS), in1=st8[:, 0:8].unsqueeze(2).to_broadcast([128, 8, HS]), op=ALU.mult)
            p.dma("pool", sc["BON"][rows, cs], E["t1"][:], reads=[E["t1"]])
    p.pop()


def rwkv_scan(c, S, sc):
    p = c.p
    p.push()
    PS = c.PS
    ident = c.ident
    ule = p.sb([128, 128], F32, "ule")
    p.op("pool", "memset", writes=[ule], ap=ule[:], constant=1.0)
    p.op("pool", "affine_select", reads=[ule], writes=[ule], out=ule[:], in_=ule[:], pattern=[[1, 128]],
         compare_op=ALU.is_ge, fill=0.0, base=0, channel_multiplier=-1)
    su = p.sb([128, 128], F32, "su")
    p.op("pool", "memset", writes=[su], ap=su[:], constant=1.0)
    p.op("pool", "affine_select", reads=[su], writes=[su], out=su[:], in_=su[:], pattern=[[1, 128]],
         compare_op=ALU.is_gt, fill=0.0, base=0, channel_multiplier=-1)
    sl = p.sb([128, 128], F32, "sl")
    p.op("pool", "memset", writes=[sl], ap=sl[:], constant=1.0)
    p.op("pool", "affine_select", reads=[sl], writes=[sl], out=sl[:], in_=sl[:], pattern=[[-1, 128]],
         compare_op=ALU.is_gt, fill=0.0, base=0, channel_multiplier=1)
    mask4 = p.sb([128, 512], F32, "mask4")
    for i, m in enumerate([su, ule, su, ule]):
        p.op("pool", "tensor_copy", reads=[m], writes=[mask4], out=mask4[:, i * 128:(i + 1) * 128], in_=m[:])
    ones_col = p.sb([128, 1], F32, "ones_col")
    p.op("pool", "memset", writes=[ones_col], ap=ones_col[:], constant=1.0)
    raws = [{nm: p.sb([128, D], F32, "raw_" + nm) for nm in ["R", "LW", "KP", "V", "A", "B"]} for _ in range(2)]

    def prefetch(ch_):
        rows_ = slice(ch_ * 128, (ch_ + 1) * 128)
        for nm in raws[ch_ % 2]:
            p.dma("sp", raws[ch_ % 2][nm][:], sc[nm][rows_, :], writes=[raws[ch_ % 2][nm]])
    prefetch(0)
    Gc = p.sb([128, D], F32, "Gc")
    Et = p.sb([128, D], F32, "Et")
    Rt = p.sb([128, D], BF16, "Rt")
    At = p.sb([128, D], BF16, "At")
    Bt = p.sb([128, D], BF16, "Bt")
    Kt = p.sb([128, D], BF16, "Kt")
    Vb = p.sb([128, D], BF16, "Vb")
    identb = p.sb([128, 128], BF16, "identb")
    p.op("pool", "tensor_copy", reads=[ident], writes=[identb], out=identb[:], in_=ident[:])
    GL = p.sb([64, NH], F32, "GL")
    GH = 4
    fm = [p.sb([64, 512], BF16, "fm") for _ in range(GH)]
    GM = [p.sb([128, 512], BF16, "GM") for _ in range(GH)]
    A0 = [p.sb([128, 128], BF16, "A0") for _ in range(GH)]
    NTb = [p.sb([128, 128], BF16, "NTb") for _ in range(GH)]
    AA = [[p.sb([128, 256], BF16, "AA") for _ in range(2)] for _ in range(GH)]
    PTb = [[p.sb([128, 128], BF16, "PT") for _ in range(2)] for _ in range(GH)]
    NV = p.sb([128, 256], F32, "NV")
    KV = p.sb([64, 256], F32, "KV")
    Xs = p.sb([128, 256], BF16, "Xs")
    Us = p.sb([128, 256], BF16, "Us")
    S1 = p.sb([64, 256], F32, "S1")
    S2 = p.sb([64, 256], F32, "S2")
    Tst = [p.sb([64, GH * HS], F32, "Tst") for _ in range(NH // GH)]
    Tstb = [p.sb([64, GH * HS], BF16, "Tstb") for _ in range(NH // GH)]
    for t_ in Tst + Tstb:
        p.op("pool", "memset", writes=[t_], ap=t_[:], constant=0.0)
    Yt = p.sb([128, D], F32, "Yt")

    TPS = [PS[0], PS[1]]
    pg = PS[2]
    pn = [PS[3].sub(128, 0, 128), PS[3].sub(128, 128, 256), PS[3].sub(128, 256, 384), PS[3].sub(128, 384, 512)]
    sq = [PS[4].sub(128, 0, 256), PS[5].sub(128, 0, 256), PS[4].sub(128, 256, 512), PS[5].sub(128, 256, 512)]
    pp = [PS[6].sub(128, 0, 128), PS[7].sub(128, 0, 128), PS[6].sub(128, 128, 256), PS[7].sub(128, 128, 256)]
    pnv = PS[0].sub(128, 0, 256)
    pY = PS[0].sub(128, 256, 512)
    pkv = PS[1].sub(128, 0, 256)
    p3 = PS[1].sub(128, 256, 512)
    p1 = PS[2].sub(128, 0, 256)
    p2 = PS[2].sub(128, 256, 512)

    for ch in range(S // 128):
        rows = slice(ch * 128, (ch + 1) * 128)
        raw = raws[ch % 2]
        if ch + 1 < S // 128:
            prefetch(ch + 1)
        LW = raw["LW"]
        V = Vb
        p.op("act", "copy", reads=[raw["V"]], writes=[Vb], out=Vb[:], in_=raw["V"][:])
        for n4 in range(4):
            ps = TPS[n4 % 2]
            p.mm(ps, ps[:, :], ule, ule[:], LW, LW[:, n4 * 512:(n4 + 1) * 512])
            evac(p, "act" if n4 % 2 else "dve", ps, ps[:, :], Gc, Gc[:, n4 * 512:(n4 + 1) * 512])
        ps = TPS[0]
        for h in range(NH):
            p.mm(ps, ps[0:64, h:h + 1], LW, LW[:, h * 64:(h + 1) * 64], ones_col, ones_col[:])
        p.op("act", "activation", reads=[ps], writes=[GL], out=GL[:], in_=ps[0:64, 0:NH], func=AF.Exp)
        p.op("act", "activation", reads=[Gc], writes=[Et], out=Et[:], in_=Gc[:], func=AF.Exp)
        p.op("dve", "tensor_tensor", reads=[raw["R"], Et], writes=[Rt], out=Rt[:], in0=raw["R"][:], in1=Et[:], op=ALU.mult)
        p.op("act", "activation", reads=[Gc], writes=[Et], out=Et[:], in_=Gc[:], func=AF.Exp, scale=-1.0)
        p.op("dve", "tensor_tensor", reads=[raw["B"], Et], writes=[Bt], out=Bt[:], in0=raw["B"][:], in1=Et[:], op=ALU.mult)
        p.op("dve", "tensor_tensor", reads=[raw["KP"], Et], writes=[Kt], out=Kt[:], in0=raw["KP"][:], in1=Et[:], op=ALU.mult)
        p.op("dve", "tensor_tensor", reads=[Gc, LW], writes=[Gc], out=Gc[:], in0=Gc[:], in1=LW[:], op=ALU.subtract)
        p.op("act", "activation", reads=[Gc], writes=[Et], out=Et[:], in_=Gc[:], func=AF.Exp)
        p.op("dve", "tensor_tensor", reads=[raw["A"], Et], writes=[At], out=At[:], in0=raw["A"][:], in1=Et[:], op=ALU.mult)
        for g in range(NH // GH):
            gcols = slice(g * GH * HS, (g + 1) * GH * HS)
            cur = []
            for j in range(GH):
                h = g * GH + j
                hc = slice(h * HS, (h + 1) * HS)
                pst = TPS[h % 2]
                pv = pst[:].bitcast(BF16)
                for qi, Q in enumerate([At, Rt, Bt, Kt]):
                    p.op("pe", "transpose", reads=[Q, identb], writes=[pst], out=pv[0:64, qi * 128:(qi + 1) * 128], in_=Q[:, hc], identity=identb[:])
                evac(p, "act" if h % 2 else "dve", pst, pv[0:64, 0:512], fm[j], fm[j][:])
                f = fm[j]
                p.mm(pg, pg[:, 0:256], f, f[:, 256:384], f, f[:, 0:256])
                p.mm(pg, pg[:, 256:512], f, f[:, 384:512], f, f[:, 0:256])
                p.op("dve", "tensor_tensor", reads=[pg, mask4], writes=[GM[j]], out=GM[j][:], in0=pg[:], in1=mask4[:], op=ALU.mult)
                pn_ = pn[j]
                p.mm(pn_, pn_[:], f, f[:, 0:128], f, f[:, 256:384])
                a0 = A0[j]
                p.op("dve", "tensor_tensor", reads=[pn_, sl], writes=[a0], out=a0[:], in0=pn_[:], in1=sl[:], op=ALU.mult)
                PT = PTb[j][0]
                p.op("dve", "tensor_tensor", reads=[GM[j], ident], writes=[PT], out=PT[:], in0=GM[j][:, 0:128], in1=ident[:], op=ALU.add)
                cur.append([a0, a0[:], GM[j], GM[j][:, 0:128], PT])
            for i in range(1, 7):
                for j in range(GH):
                    A_t, A_ap, AT_t, AT_ap, PT = cur[j]
                    s_ = sq[j]
                    p.mm(s_, s_[:, 0:128], AT_t, AT_ap, A_t, A_ap)
                    if i < 6:
                        p.mm(s_, s_[:, 128:256], A_t, A_ap, AT_t, AT_ap)
                for j in range(GH):
                    aa = AA[j][i % 2]
                    w_ = 256 if i < 6 else 128
                    evac(p, "act" if j % 2 else "dve", sq[j], sq[j][:, 0:w_], aa, aa[:, 0:w_])
                    cur[j][0:4] = [aa, aa[:, 0:128], aa, aa[:, 128:256]]
                for j in range(GH):
                    A_t, A_ap, AT_t, AT_ap, PT = cur[j]
                    p.mm(pp[j], pp[j][:], A_t, A_ap, PT, PT[:])
                for j in range(GH):
                    PT = cur[j][4]
                    PTn = PTb[j][i % 2]
                    p.op("dve", "tensor_tensor", reads=[pp[j], PT], writes=[PTn], out=PTn[:], in0=pp[j][:], in1=PT[:], op=ALU.add)
                    cur[j][4] = PTn
            for j in range(GH):
                h = g * GH + j
                hc = slice(h * HS, (h + 1) * HS)
                jc = slice(j * HS, (j + 1) * HS)
                p.mm(pnv, pnv[:, jc], GM[j], GM[j][:, 256:384], V, V[:, hc])
                p.mm(pkv, pkv[0:64, jc], Kt, Kt[:, hc], V, V[:, hc])
            evac(p, "act", pnv, pnv[:], NV, NV[:])
            evac(p, "act", pkv, pkv[0:64, :], KV, KV[:])
            Tg = Tst[g]
            Tgb = Tstb[g]
            for j in range(GH):
                jc = slice(j * HS, (j + 1) * HS)
                p.mm(p1, p1[:, jc], fm[j], fm[j][:, 0:128], Tgb, Tgb[:, jc])
            p.op("dve", "tensor_tensor", reads=[p1, NV], writes=[Xs], out=Xs[:], in0=p1[:], in1=NV[:], op=ALU.add)
            for j in range(GH):
                jc = slice(j * HS, (j + 1) * HS)
                p.mm(p2, p2[:, jc], PTb[j][0], PTb[j][0][:], Xs, Xs[:, jc])
            evac(p, "act", p2, p2[:], Us, Us[:])
            for j in range(GH):
                h = g * GH + j
                hc = slice(h * HS, (h + 1) * HS)
                jc = slice(j * HS, (j + 1) * HS)
                p.mm(pY, pY[:, jc], fm[j], fm[j][:, 128:256], Tgb, Tgb[:, jc], start=True, stop=False)
                p.mm(pY, pY[:, jc], GM[j], GM[j][:, 128:256], Us, Us[:, jc], start=False, stop=False)
                p.mm(pY, pY[:, jc], GM[j], GM[j][:, 384:512], V, V[:, hc], start=False, stop=True)
            evac(p, "act", pY, pY[:], Yt, Yt[:, gcols])
            for j in range(GH):
                h = g * GH + j
                hc = slice(h * HS, (h + 1) * HS)
                jc = slice(j * HS, (j + 1) * HS)
                p.mm(p3, p3[0:64, jc], Bt, Bt[:, hc], Us, Us[:, jc])
            p.op("dve", "tensor_tensor", reads=[Tg, KV], writes=[S1], out=S1[:], in0=Tg[:], in1=KV[:], op=ALU.add)
            p.op("dve", "tensor_tensor", reads=[p3, S1], writes=[S2], out=S2[:], in0=p3[0:64, :], in1=S1[:], op=ALU.add)
            p.op("dve", "tensor_tensor", reads=[S2, GL], writes=[Tg], out=Tg[:].rearrange("p (h v) -> p h v", v=HS),
                 in0=S2[:].rearrange("p (h v) -> p h v", v=HS),
                 in1=GL[:, g * GH:(g + 1) * GH].unsqueeze(2).to_broadcast([64, GH, HS]), op=ALU.mult)
            p.op("act", "copy", reads=[Tg], writes=[Tgb], out=Tgb[:], in_=Tg[:])
        p.dma("pool", sc["Y"][rows, :], Yt[:], reads=[Yt])
    p.pop()


def out_proj_phase(c, X, S, make_tile, Wo, extra_alloc=None):
    p = c.p
    TG = min(512, S)
    NT = TG // 128
    wl = WLoader(c)
    oT = p.sb([128, KT, TG], BF16, "oT")
    mo = p.sb([128, D], F32, "mo")
    xres = [p.sb([128, D], F32, "xres") for _ in range(NT)]
    ot = [p.sb([128, 512], F32, "ot") for _ in range(2)]
    for t0 in range(0, S, TG):
        for ti in range(NT):
            r0 = t0 + ti * 128
            make_tile(r0, mo)
            to_fm(c, mo, lambda kt: mo[:, kt * 128:(kt + 1) * 128], KT, oT, ti * 128)
            p.dma("sp", xres[ti][:], X[r0:r0 + 128, :], writes=[xres[ti]])
        for nch in range(4):
            wb = wl.load(Wo, 0, D, nch * 512, 512)
            for ti in range(NT):
                ps = c.PS[ti]
                for kt in range(KT):
                    p.mm(ps, ps[:, :], oT, oT[:, kt, ti * 128:(ti + 1) * 128], wb, wb[:, kt, :], start=(kt == 0), stop=(kt == KT - 1))
                o = ot[ti % 2]
                p.op("dve", "tensor_tensor", reads=[ps, xres[ti]], writes=[o], out=o[:], in0=ps[:], in1=xres[ti][:, nch * 512:(nch + 1) * 512], op=ALU.add)
                p.dma("pool", X[t0 + ti * 128:t0 + (ti + 1) * 128, nch * 512:(nch + 1) * 512], o[:], reads=[o])


def rwkv_post(c, X, S, W, sc):
    p = c.p
    p.push()
    Yt = p.sb([128, D], F32, "Yt")
    Bo = p.sb([128, D], F32, "Bo")
    Gt = p.sb([128, D], F32, "Gt")
    sq = p.sb([128, D], F32, "sq")
    lg = p.sb([128, D], F32, "lg")
    lb = p.sb([128, D], F32, "lb")
    bcast_row(p, lg, W["lnx_g"], D)
    bcast_row(p, lb, W["lnx_b"], D)
    st = p.sb([128, 64], F32, "st")

    def v3(t):
        return t[:].rearrange("p (h k) -> p h k", k=HS)

    def make_tile(r0, mo):
        rows = slice(r0, r0 + 128)
        p.dma("sp", Yt[:], sc["Y"][rows, :], writes=[Yt])
        p.dma("sp", Bo[:], sc["BON"][rows, :], writes=[Bo])
        p.dma("sp", Gt[:], sc["G"][rows, :], writes=[Gt])
        p.op("dve", "reduce_sum", reads=[Yt], writes=[st], out=st[:, 0:32], in_=v3(Yt), axis=AX.X)
        p.op("dve", "tensor_scalar", reads=[st], writes=[st], out=st[:, 0:32], in0=st[:, 0:32], scalar1=1.0 / HS, scalar2=None, op0=ALU.mult)
        p.op("dve", "tensor_tensor", reads=[Yt, st], writes=[Yt], out=v3(Yt), in0=v3(Yt), in1=st[:, 0:32].unsqueeze(2).to_broadcast([128, NH, HS]), op=ALU.subtract)
        p.op("act", "activation", reads=[Yt], writes=[sq], out=sq[:], in_=Yt[:], func=AF.Square)
        p.op("dve", "reduce_sum", reads=[sq], writes=[st], out=st[:, 32:64], in_=v3(sq), axis=AX.X)
        p.op("dve", "tensor_scalar", reads=[st], writes=[st], out=st[:, 32:64], in0=st[:, 32:64], scalar1=1.0 / HS, scalar2=64e-5, op0=ALU.mult, op1=ALU.add)
        p.op("act", "activation", reads=[st], writes=[st], out=st[:, 32:64], in_=st[:, 32:64], func=AF.Sqrt)
        p.op("dve", "reciprocal", reads=[st], writes=[st], out=st[:, 0:32], in_=st[:, 32:64])
        p.op("dve", "tensor_tensor", reads=[Yt, st], writes=[Yt], out=v3(Yt), in0=v3(Yt), in1=st[:, 0:32].unsqueeze(2).to_broadcast([128, NH, HS]), op=ALU.mult)
        p.op("dve", "tensor_tensor", reads=[Yt, lg], writes=[Yt], out=Yt[:], in0=Yt[:], in1=lg[:], op=ALU.mult)
        p.op("dve", "tensor_tensor", reads=[Yt, lb], writes=[Yt], out=Yt[:], in0=Yt[:], in1=lb[:], op=ALU.add)
        p.op("dve", "tensor_tensor", reads=[Yt, Bo], writes=[Yt], out=Yt[:], in0=Yt[:], in1=Bo[:], op=ALU.add)
        p.op("dve", "tensor_tensor", reads=[Yt, Gt], writes=[mo], out=mo[:], in0=Yt[:], in1=Gt[:], op=ALU.mult)

    out_proj_phase(c, X, S, make_tile, W["w_o"])
    p.pop()


RW_SC = ["R", "LW", "KP", "V", "A", "B", "G", "BON", "Y", "VF"]


MH = 4
MDK = 256
MDV = 512


def mlstm_layer(c, X, S, W, sc):
    p = c.p
    PS = c.PS
    ident = c.ident
    TG = min(512, S)
    NT = TG // 128
    Win = W["w_in"]
    QK, Vb, SO, MO = sc["QK"], sc["Vb"], sc["SO"], sc["MO"]
    p.push()
    GI = p.sb([4, S], F32, "GI")
    GF = p.sb([4, S], F32, "GF")
    p.push()
    wl = WLoader(c)
    g_bc = p.sb([128, D], F32, "g_bc")
    bcast_row(p, g_bc, W["mix_g"], D)
    xt = p.sb([128, D], F32, "xt")
    hn = p.sb([128, D], F32, "hn")
    ss = p.sb([128, 2], F32, "ss")
    hT = p.sb([128, KT, TG], BF16, "hT")
    qk_sb = [p.sb([128, TG], BF16, "qk_sb") for _ in range(2)]
    vb_sb = [p.sb([128, 512], BF16, "vb_sb") for _ in range(2)]
    so_sb = [p.sb([128, 512], F32, "so_sb") for _ in range(2)]
    for t0 in range(0, S, TG):
        for ti in range(NT):
            p.dma("sp", xt[:], X[t0 + ti * 128:t0 + (ti + 1) * 128, :], writes=[xt])
            rmsnorm_tile(p, xt, xt[:], g_bc, hn, hn[:], hn, ss)
            to_fm(c, hn, lambda kt: hn[:, kt * 128:(kt + 1) * 128], KT, hT, ti * 128)
        for nb in range(4):
            wb = wl.load(Win, 0, D, nb * 512, 512)
            for j in range(4):
                ps = PS[j % 2]
                for kt in range(KT):
                    p.mm(ps, ps[:, 0:TG], wb, wb[:, kt, j * 128:(j + 1) * 128], hT, hT[:, kt, :], start=(kt == 0), stop=(kt == KT - 1))
                o = qk_sb[j % 2]
                p.op("act", "activation", reads=[ps], writes=[o], out=o[:, 0:TG], in_=ps[:, 0:TG], func=AF.Copy, scale=(0.0625 if nb < 2 else 1.0))
                r0 = nb * 512 + j * 128
                p.dma("pool", QK[r0:r0 + 128, t0:t0 + TG], o[:, 0:TG], reads=[o])
        for nb in range(8):
            wb = wl.load(Win, 0, D, 2048 + nb * 512, 512)
            for ti in range(NT):
                ps = PS[2 + ti % 2]
                for kt in range(KT):
                    p.mm(ps, ps[:, :], hT, hT[:, kt, ti * 128:(ti + 1) * 128], wb, wb[:, kt, :], start=(kt == 0), stop=(kt == KT - 1))
                rows = slice(t0 + ti * 128, t0 + (ti + 1) * 128)
                if nb < 4:
                    o = vb_sb[ti % 2]
                    p.op("dve", "tensor_copy", reads=[ps], writes=[o], out=o[:], in_=ps[:])
                    p.dma("pool", Vb[rows, nb * 512:(nb + 1) * 512], o[:], reads=[o])
                else:
                    o = so_sb[ti % 2]
                    p.op("act", "activation", reads=[ps], writes=[o], out=o[:], in_=ps[:], func=AF.Sigmoid)
                    p.dma("pool", SO[rows, (nb - 4) * 512:(nb - 3) * 512], o[:], reads=[o])
        wb = wl.load(Win, 0, D, 6144, 8)
        for gi, (G_, ps) in enumerate([(GI, PS[4]), (GF, PS[5])]):
            for kt in range(KT):
                p.mm(ps, ps[0:4, 0:TG], wb, wb[:, kt, gi * 4:(gi + 1) * 4], hT, hT[:, kt, :], start=(kt == 0), stop=(kt == KT - 1))
            p.op("act", "copy", reads=[ps], writes=[G_], out=G_[:, t0:t0 + TG], in_=ps[0:4, 0:TG])
    p.pop()
    NB = S // 128
    bif = p.sb([4, 2], F32, "bif")
    p.dma("sp", bif[:], W["b_if"].rearrange("a h -> h a"), writes=[bif], allow_slow_non_contiguous=True)
    SCN = min(1024, S)
    onesr = p.sb([4, SCN], F32, "onesr")
    zeror = p.sb([4, SCN], F32, "zeror")
    p.op("pool", "memset", writes=[onesr], ap=onesr[:], constant=1.0)
    p.op("pool", "memset", writes=[zeror], ap=zeror[:], constant=0.0)
    Fr = p.sb([4, S], F32, "Fr")
    for G_, col in ((GI, 0), (GF, 1)):
        p.op("dve", "tensor_scalar", reads=[G_, bif], writes=[G_], out=G_[:], in0=G_[:], scalar1=bif[:, col:col + 1], scalar2=None, op0=ALU.add)
        p.op("act", "activation", reads=[G_], writes=[G_], out=G_[:], in_=G_[:], func=AF.Tanh, scale=1.0 / 15.0)
        p.op("dve", "tensor_scalar", reads=[G_], writes=[G_], out=G_[:], in0=G_[:], scalar1=15.0, scalar2=None, op0=ALU.mult)
    p.op("act", "activation", reads=[GF], writes=[GF], out=GF[:], in_=GF[:], func=AF.Exp, scale=-1.0)
    p.op("dve", "tensor_scalar", reads=[GF], writes=[GF], out=GF[:], in0=GF[:], scalar1=1.0, scalar2=None, op0=ALU.add)
    p.op("act", "activation", reads=[GF], writes=[GF], out=GF[:], in_=GF[:], func=AF.Ln)
    p.op("dve", "tensor_scalar", reads=[GF], writes=[GF], out=GF[:], in0=GF[:], scalar1=-1.0, scalar2=None, op0=ALU.mult)
    for s0 in range(0, S, SCN):
        init = 0.0 if s0 == 0 else Fr[:, s0 - 1:s0]
        p.op("dve", "tensor_tensor_scan", reads=[GF, onesr, Fr], writes=[Fr], out=Fr[:, s0:s0 + SCN], data0=onesr[:, 0:SCN], data1=GF[:, s0:s0 + SCN],
             initial=init, op0=ALU.mult, op1=ALU.add)
    Ur = GI
    p.op("dve", "tensor_tensor", reads=[GI, Fr], writes=[Ur], out=Ur[:], in0=GI[:], in1=Fr[:], op=ALU.subtract)
    Gr = GF
    for s0 in range(0, S, SCN):
        init = 0.0 if s0 == 0 else Gr[:, s0 - 1:s0]
        p.op("dve", "tensor_tensor_scan", reads=[Ur, zeror, Gr], writes=[Gr], out=Gr[:, s0:s0 + SCN], data0=Ur[:, s0:s0 + SCN], data1=zeror[:, 0:SCN],
             initial=init, op0=ALU.max, op1=ALU.add)
    NM = Fr
    p.op("dve", "tensor_tensor", reads=[Fr, Gr], writes=[NM], out=NM[:], in0=Fr[:], in1=Gr[:], op=ALU.add)
    p.op("dve", "tensor_scalar", reads=[NM], writes=[NM], out=NM[:], in0=NM[:], scalar1=-1.0, scalar2=None, op0=ALU.mult)
    p.op("dve", "tensor_scalar", reads=[Gr], writes=[Gr], out=Gr[:], in0=Gr[:], scalar1=-1.0, scalar2=None, op0=ALU.mult)
    Ucol = p.sb([128, NB, 4], F32, "Ucol")
    NMcol = p.sb([128, NB, 4], F32, "NMcol")
    for src, dst in ((Ur, Ucol), (NM, NMcol)):
        for b0 in range(0, NB, 64):
            nb_ = min(64, NB - b0)
            ps = PS[6]
            for b in range(nb_):
                p.op("pe", "transpose", reads=[src, ident], writes=[ps], out=ps[:, b * 4:(b + 1) * 4], in_=src[:, (b0 + b) * 128:(b0 + b + 1) * 128], identity=ident[0:4, 0:4])
            p.op("dve", "tensor_copy", reads=[ps], writes=[dst], out=dst[:, b0:b0 + nb_, :], in_=ps[:, 0:nb_ * 4].rearrange("p (b h) -> p b h", h=4))
    p.op("act", "activation", reads=[NMcol], writes=[NMcol], out=NMcol[:], in_=NMcol[:], func=AF.Exp)
    sel = p.sb([4, 4, 128], F32, "sel")
    p.op("pool", "memset", writes=[sel], ap=sel[:], constant=1.0)
    p.op("pool", "affine_select", reads=[sel], writes=[sel], out=sel[:], in_=sel[:], pattern=[[-1, 4], [0, 128]],
         compare_op=ALU.is_equal, fill=0.0, base=0, channel_multiplier=1)
    TB = min(512, S)
    NTS = TB // 128
    cms = p.sb([128, NTS, TB], F32, "cms")
    p.op("pool", "memset", writes=[cms], ap=cms[:], constant=1.0)
    for r_ in range(NTS):
        p.op("pool", "affine_select", reads=[cms], writes=[cms], out=cms[:, r_, :], in_=cms[:, r_, :], pattern=[[1, TB]], compare_op=ALU.is_ge,
             fill=0.0, base=-128 * r_, channel_multiplier=-1)
    onesb = p.sb([128, 1], BF16, "onesb")
    p.op("pool", "memset", writes=[onesb], ap=onesb[:], constant=1.0)
    qT = p.sb([128, 2, TB], BF16, "qT")
    kT = [p.sb([128, 2, 128], BF16, "kT") for _ in range(3)]
    Vs = [p.sb([128, MDV], BF16, "Vs") for _ in range(3)]
    ngb = p.sb([128, TB], F32, "ngb")
    Ex = [p.sb([128, TB], F32, "Ex") for _ in range(3)]
    Pb = [p.sb([128, TB], BF16, "Pb") for _ in range(3)]
    ng_w = p.sb([128, MDV], F32, "ng_w")
    hr = p.sb([128, MDV], F32, "hr")
    sqh = p.sb([128, MDV], F32, "sqh")
    sot = p.sb([128, MDV], F32, "sot")
    dn = p.sb([128, 8], F32, "dn")
    ssd = p.sb([128, 2], F32, "ssd")
    pnum = [PS[i] for i in range(4)]
    pden = PS[4]
    pdt = PS[7]
    drow = p.sb([1, TB], F32, "drow")
    for h in range(MH):
        bcast_row(p, ng_w, W["norm_g"][h * MDV:(h + 1) * MDV], MDV)
        for tb in range(S // TB):
            tcols = slice(tb * TB, (tb + 1) * TB)
            p.dma("sp", qT[:], QK[h * MDK:(h + 1) * MDK, tcols].rearrange("(kt p) t -> p kt t", p=128), writes=[qT])
            ps = PS[7]
            p.mm(ps, ps[:, 0:TB], sel, sel[:, h, :], Gr, Gr[:, tcols])
            p.op("act", "copy", reads=[ps], writes=[ngb], out=ngb[:], in_=ps[:, 0:TB])
            nsb = (tb + 1) * NTS

            def front(sb):
                i2 = sb % 3
                p.dma("sp", kT[i2][:], QK[1024 + h * MDK:1024 + (h + 1) * MDK, sb * 128:(sb + 1) * 128].rearrange("(kt p) t -> p kt t", p=128), writes=[kT[i2]])
                p.dma("sp", Vs[i2][:], Vb[sb * 128:(sb + 1) * 128, h * MDV:(h + 1) * MDV], writes=[Vs[i2]])
                pst = PS[5 + sb % 2]
                for kt in range(2):
                    p.mm(pst, pst[:, 0:TB], kT[i2], kT[i2][:, kt, :], qT, qT[:, kt, :], start=(kt == 0), stop=(kt == 1))
                e = Ex[i2]
                p.op("dve", "tensor_scalar", reads=[ngb, Ucol], writes=[e], out=e[:], in0=ngb[:], scalar1=Ucol[:, sb, h:h + 1], scalar2=0.0, op0=ALU.add, op1=ALU.min)
                p.op("act", "activation", reads=[e], writes=[e], out=e[:], in_=e[:], func=AF.Exp)
                r = sb - tb * NTS
                if r >= 0:
                    p.op("dve", "tensor_tensor", reads=[e, cms], writes=[e], out=e[:], in0=e[:], in1=cms[:, r, :], op=ALU.mult)
                pb = Pb[i2]
                p.op("dve", "tensor_tensor", reads=[pst, e], writes=[pb], out=pb[:], in0=pst[:, 0:TB], in1=e[:], op=ALU.mult)

            def back(sb):
                i2 = sb % 3
                r = sb - tb * NTS
                pb = Pb[i2]
                p.mm(pden, pden[0:1, 0:TB], onesb, onesb[:], pb, pb[:], start=(sb == 0), stop=(sb == nsb - 1))
                for ts in range(max(r, 0), NTS):
                    last = (sb == tb * NTS + ts)
                    p.mm(pnum[ts], pnum[ts][:, :], pb, pb[:, ts * 128:(ts + 1) * 128], Vs[i2], Vs[i2][:], start=(sb == 0), stop=last)

            LA = 2
            for i in range(nsb + LA):
                if i < nsb:
                    front(i)
                if i - LA >= 0:
                    back(i - LA)
            p.op("act", "activation", reads=[pden], writes=[drow], out=drow[:, 0:TB], in_=pden[0:1, 0:TB], func=AF.Abs)
            for ts in range(NTS):
                p.op("pe", "transpose", reads=[drow, ident], writes=[pdt], out=pdt[:, ts:ts + 1], in_=drow[0:1, ts * 128:(ts + 1) * 128], identity=ident[0:1, 0:1])
            for ts in range(NTS):
                blk = tb * NTS + ts
                rows = slice(blk * 128, (blk + 1) * 128)
                p.op("act", "copy", reads=[pdt], writes=[dn], out=dn[:, 0:1], in_=pdt[:, ts:ts + 1])
                p.op("dve", "tensor_tensor", reads=[dn, NMcol], writes=[dn], out=dn[:, 1:2], in0=dn[:, 0:1], in1=NMcol[:, blk, h:h + 1], op=ALU.max)
                p.op("dve", "reciprocal", reads=[dn], writes=[dn], out=dn[:, 2:3], in_=dn[:, 1:2])
                p.op("dve", "tensor_scalar", reads=[pnum[ts], dn], writes=[hr], out=hr[:], in0=pnum[ts][:], scalar1=dn[:, 2:3], scalar2=None, op0=ALU.mult)
                p.dma("sp", sot[:], SO[rows, h * MDV:(h + 1) * MDV], writes=[sot])
                rmsnorm_tile(p, hr, hr[:], ng_w, hr, hr[:], sqh, ssd, d=MDV)
                p.op("dve", "tensor_tensor", reads=[hr, sot], writes=[sqh], out=sqh[:], in0=hr[:], in1=sot[:], op=ALU.mult)
                p.dma("pool", MO[rows, h * MDV:(h + 1) * MDV], sqh[:], reads=[sqh])
    p.pop()
    p.push()
    def make_tile(r0, mo):
        p.dma("sp", mo[:], MO[r0:r0 + 128, :], writes=[mo])
    out_proj_phase(c, X, S, make_tile, W["w_o"])
    p.pop()


def dn_ss(p, dn):
    return T(dn.ap[:, 4:6], "dnss")


DH = 128
NHD = 16
NEG = -1.0e30


def dsa_layer(c, X, S, W, sc, dbg=None, skip=()):
    p = c.p
    PS = c.PS
    ident = c.ident
    TG = min(512, S)
    NT = TG // 128
    NB = S // 128
    Win = W["w_in"]
    QT, QIT, MO = sc["QT"], sc["QIT"], sc["MO"]
    p.push()
    kT = p.sb([128, S], BF16, "kT")
    kiT = p.sb([64, S], BF16, "kiT")
    Vs = p.sb([128, NB, 132], BF16, "Vs")
    wis = p.sb([128, NB, 16], F32, "wis")
    p.op("pool", "memset", writes=[Vs], ap=Vs[:, :, 128:129], constant=1.0)
    p.push()
    wl = WLoader(c)
    g_bc = p.sb([128, D], F32, "g_bc")
    bcast_row(p, g_bc, W["mix_g"], D)
    qg = p.sb([128, DH], F32, "qg")
    kg = p.sb([128, DH], F32, "kg")
    bcast_row(p, qg, W["q_norm_g"], DH)
    bcast_row(p, kg, W["k_norm_g"], DH)
    xt = p.sb([128, D], F32, "xt")
    hn = p.sb([128, D], F32, "hn")
    ss = p.sb([128, 2], F32, "ss")
    hT = p.sb([128, KT, TG], BF16, "hT")
    qs = p.sb([128, 512], F32, "qs")
    qsq = p.sb([128, 512], F32, "qsq")
    st4 = p.sb([128, 8], F32, "st4")
    qTg = p.sb([128, NHD, TG], BF16, "qTg")
    qi_sb = [p.sb([128, TG], BF16, "qi_sb") for _ in range(2)]
    for t0 in range(0, S, TG):
        for ti in range(NT):
            p.dma("sp", xt[:], X[t0 + ti * 128:t0 + (ti + 1) * 128, :], writes=[xt])
            rmsnorm_tile(p, xt, xt[:], g_bc, hn, hn[:], hn, ss)
            to_fm(c, hn, lambda kt: hn[:, kt * 128:(kt + 1) * 128], KT, hT, ti * 128)
        for nb in range(4):
            wb = wl.load(Win, 0, D, nb * 512, 512)
            for ti in range(NT):
                ps = PS[ti % 2]
                for kt in range(KT):
                    p.mm(ps, ps[:, :], hT, hT[:, kt, ti * 128:(ti + 1) * 128], wb, wb[:, kt, :], start=(kt == 0), stop=(kt == KT - 1))
                p.op("act", "copy", reads=[ps], writes=[qs], out=qs[:], in_=ps[:])
                p.op("act", "activation", reads=[qs], writes=[qsq], out=qsq[:], in_=qs[:], func=AF.Square)
                p.op("dve", "reduce_sum", reads=[qsq], writes=[st4], out=st4[:, 0:4], in_=qsq[:].rearrange("p (h d) -> p h d", d=DH), axis=AX.X)
                p.op("dve", "tensor_scalar", reads=[st4], writes=[st4], out=st4[:, 0:4], in0=st4[:, 0:4], scalar1=1.0 / DH, scalar2=EPS, op0=ALU.mult, op1=ALU.add)
                p.op("act", "activation", reads=[st4], writes=[st4], out=st4[:, 0:4], in_=st4[:, 0:4], func=AF.Sqrt)
                p.op("dve", "reciprocal", reads=[st4], writes=[st4], out=st4[:, 4:8], in_=st4[:, 0:4])
                p.op("dve", "tensor_scalar", reads=[st4], writes=[st4], out=st4[:, 4:8], in0=st4[:, 4:8], scalar1=DH ** -0.5, scalar2=None, op0=ALU.mult)
                q3 = qs[:].rearrange("p (h d) -> p h d", d=DH)
                p.op("dve", "tensor_tensor", reads=[qs, st4], writes=[qs], out=q3, in0=q3, in1=st4[:, 4:8].unsqueeze(2).to_broadcast([128, 4, DH]), op=ALU.mult)
                p.op("dve", "tensor_tensor", reads=[qs, qg], writes=[qs], out=q3, in0=q3, in1=qg[:].unsqueeze(1).to_broadcast([128, 4, DH]), op=ALU.mult)
                pst = PS[6 + ti % 2]
                for j in range(4):
                    p.op("pe", "transpose", reads=[qs, ident], writes=[pst], out=pst[:, j * 128:(j + 1) * 128], in_=qs[:, j * 128:(j + 1) * 128], identity=ident[:])
                p.op("dve", "tensor_copy", reads=[pst], writes=[qTg], out=qTg[:, nb * 4:(nb + 1) * 4, ti * 128:(ti + 1) * 128],
                     in_=pst[:].rearrange("p (a b) -> p a b", a=4))
        for hh in range(NHD):
            p.dma("pool", QT[hh * 128:(hh + 1) * 128, t0:t0 + TG], qTg[:, hh, :], reads=[qTg])
        wb = wl.load(Win, 0, D, 2048, 256)
        for ti in range(NT):
            blk = (t0 // 128) + ti
            ps = PS[2 + ti % 2]
            for kt in range(KT):
                p.mm(ps, ps[:, 0:256], hT, hT[:, kt, ti * 128:(ti + 1) * 128], wb, wb[:, kt, 0:256], start=(kt == 0), stop=(kt == KT - 1))
            p.op("act", "copy", reads=[ps], writes=[Vs], out=Vs[:, blk, 0:128], in_=ps[:, 128:256])
            p.op("act", "copy", reads=[ps], writes=[qs], out=qs[:, 0:128], in_=ps[:, 0:128])
            rmsnorm_tile(p, qs, qs[:, 0:128], kg, qs, qs[:, 0:128], qsq, ss, d=DH)
            pst = PS[6 + ti % 2]
            p.op("pe", "transpose", reads=[qs, ident], writes=[pst], out=pst[:, 0:128], in_=qs[:, 0:128], identity=ident[:])
            p.op("dve", "tensor_copy", reads=[pst], writes=[kT], out=kT[:, blk * 128:(blk + 1) * 128], in_=pst[:, 0:128])
        for nb in range(2):
            wb = wl.load(Win, 0, D, 2304 + nb * 512, 512)
            for j in range(4):
                ps = PS[j % 2]
                for kt in range(KT):
                    p.mm(ps, ps[:, 0:TG], wb, wb[:, kt, j * 128:(j + 1) * 128], hT, hT[:, kt, :], start=(kt == 0), stop=(kt == KT - 1))
                o = qi_sb[j % 2]
                p.op("act", "copy", reads=[ps], writes=[o], out=o[:, 0:TG], in_=ps[:, 0:TG])
                r0 = nb * 512 + j * 128
                p.dma("pool", QIT[r0:r0 + 128, t0:t0 + TG], o[:, 0:TG], reads=[o])
        wb = wl.load(Win, 0, D, 3328, 80)
        ps = PS[4]
        for kt in range(KT):
            p.mm(ps, ps[0:64, 0:TG], wb, wb[:, kt, 0:64], hT, hT[:, kt, :], start=(kt == 0), stop=(kt == KT - 1))
        p.op("act", "copy", reads=[ps], writes=[kiT], out=kiT[:, t0:t0 + TG], in_=ps[0:64, 0:TG])
        for ti in range(NT):
            blk = (t0 // 128) + ti
            ps = PS[5]
            for kt in range(KT):
                p.mm(ps, ps[:, 0:16], hT, hT[:, kt, ti * 128:(ti + 1) * 128], wb, wb[:, kt, 64:80], start=(kt == 0), stop=(kt == KT - 1))
            p.op("act", "activation", reads=[ps], writes=[wis], out=wis[:, blk, :], in_=ps[:, 0:16], func=AF.Copy, scale=1.0 / 32.0)
    p.pop()
    p.push()
    NBt = p.sb([128, 2, NHD, 128], F32, "NBt")
    CB = p.sb([128, NHD], F32, "CB")
    p.dma("sp", NBt[:], W["t5_bt"], writes=[NBt])
    p.dma("sp", CB[:], W["t5_cb"], writes=[CB])
    for k2 in range(2):
        p.op("dve", "tensor_tensor", reads=[NBt, CB], writes=[NBt], out=NBt[:, k2, :, :], in0=NBt[:, k2, :, :],
             in1=CB[:].unsqueeze(2).to_broadcast([128, NHD, 128]), op=ALU.subtract)
    identb = p.sb([128, 128], BF16, "identb")
    p.op("pool", "tensor_copy", reads=[ident], writes=[identb], out=identb[:], in_=ident[:])
    SC = p.sb([128, S], F32, "SC")
    CM = p.sb([128, 128], F32, "CM")
    p.op("pool", "memset", writes=[CM], ap=CM[:], constant=0.0)
    p.op("pool", "affine_select", reads=[CM], writes=[CM], out=CM[:], in_=CM[:], pattern=[[-1, 128]], compare_op=ALU.is_ge,
         fill=NEG, base=0, channel_multiplier=1)
    Wk = p.sb([128, S], F32, "Wk")
    m8 = p.sb([128, 16], F32, "m8")
    Mk = [p.sb([128, 512], BF16, "Mk") for _ in range(2)]
    MT = p.sb([128, NB, 128], BF16, "MT")
    qTb = p.sb([128, NHD, 128], BF16, "qTb")
    qiTb = p.sb([64, NHD, 128], BF16, "qiTb")
    rl = [p.sb([128, 512], F32, "rl") for _ in range(2)]
    Pe = [p.sb([128, 512], BF16, "Pe") for _ in range(4)]
    Pm = [p.sb([128, 512], BF16, "Pm") for _ in range(4)]
    numsb = p.sb([128, 4, 132], F32, "numsb")
    NBb = p.sb([128, 2, NHD, 128], BF16, "NBb")
    p.op("pool", "tensor_copy", reads=[NBt], writes=[NBb], out=NBb[:], in_=NBt[:])
    mo = p.sb([128, D], F32, "mo")
    rd = p.sb([128, 16], F32, "rd")
    acc = [PS[i] for i in range(4)]
    for qb in range(NB):
        nkb = qb + 1
        ncols = nkb * 128
        tcols = slice(qb * 128, (qb + 1) * 128)
        p.dma("sp", qTb[:], QT[:, tcols].rearrange("(h d) t -> d h t", d=128), writes=[qTb])
        p.dma("sp", qiTb[:], QIT[:, tcols].rearrange("(h d) t -> d h t", d=64), writes=[qiTb])
        for c0 in (() if 'scores' in skip else range(0, ncols, 512)):
            cw = min(512, ncols - c0)
            for hh in range(NHD):
                ps = PS[6 + hh % 2]
                p.mm(ps, ps[:, 0:cw], qiTb, qiTb[:, hh, :], kiT, kiT[:, c0:c0 + cw])
                r_ = rl[hh % 2]
                p.op("act", "activation", reads=[ps], writes=[r_], out=r_[:, 0:cw], in_=ps[:, 0:cw], func=AF.Relu)
                if hh == 0:
                    p.op("dve", "tensor_scalar", reads=[r_, wis], writes=[SC], out=SC[:, c0:c0 + cw], in0=r_[:, 0:cw], scalar1=wis[:, qb, 0:1], scalar2=None, op0=ALU.mult)
                else:
                    p.op("dve", "scalar_tensor_tensor", reads=[r_, wis, SC], writes=[SC], out=SC[:, c0:c0 + cw], in0=r_[:, 0:cw], scalar=wis[:, qb, hh:hh + 1],
                         in1=SC[:, c0:c0 + cw], op0=ALU.mult, op1=ALU.add)
        p.op("dve", "tensor_tensor", reads=[SC, CM], writes=[SC], out=SC[:, qb * 128:(qb + 1) * 128], in0=SC[:, qb * 128:(qb + 1) * 128],
             in1=CM[:], op=ALU.add)
        NSEL = min(256, S // 4)
        if ncols > NSEL and 'topk' not in skip:
            src = SC
            for r in range(NSEL // 8):
                p.op("dve", "max", reads=[src], writes=[m8], out=m8[:, 0:8], in_=src[:, 0:ncols])
                if r < NSEL // 8 - 1:
                    p.op("dve", "match_replace", reads=[m8, src], writes=[Wk], out=Wk[:, 0:ncols], in_to_replace=m8[:, 0:8], in_values=src[:, 0:ncols], imm_value=NEG)
                    src = Wk
            p.op("dve", "tensor_reduce", reads=[m8], writes=[m8], out=m8[:, 8:9], in_=m8[:, 0:8], axis=AX.X, op=ALU.min)
        else:
            p.op("pool", "memset", writes=[m8], ap=m8[:, 8:9], constant=-1.0e29)
        if dbg is not None and qb == 1:
            p.dma("pool", dbg["SC"][:, :], SC[:, 0:256], reads=[SC])
            p.dma("pool", dbg["m8"][:, :], m8[:], reads=[m8])
        for c0 in range(0, ncols, 512):
            cw = min(512, ncols - c0)
            i2 = (c0 // 512) % 2
            p.op("dve", "tensor_scalar", reads=[SC, m8], writes=[Mk[i2]], out=Mk[i2][:, 0:cw], in0=SC[:, c0:c0 + cw], scalar1=m8[:, 8:9], scalar2=None, op0=ALU.is_ge)
            pst = PS[6 + i2]
            pv = pst[:].bitcast(BF16)
            for j in range(cw // 128):
                p.op("pe", "transpose", reads=[Mk[i2], identb], writes=[pst], out=pv[:, j * 128:(j + 1) * 128], in_=Mk[i2][:, j * 128:(j + 1) * 128], identity=identb[:])
            p.op("act", "copy", reads=[pst], writes=[MT], out=MT[:, c0 // 128:c0 // 128 + cw // 128, :], in_=pv[:, 0:cw].rearrange("p (a b) -> p a b", b=128))
        if dbg is not None and qb == 1:
            p.dma("pool", dbg["MT"][:, :, :], MT[:, 0:2, :], reads=[MT])
        ixc = 0
        chunks = [] if 'attn' in skip else [(hp, kb) for hp in range(4) for kb in range(nkb)]

        def front(i):
            hp, kb = chunks[i]
            hs = slice(hp * 4, (hp + 1) * 4)
            near = (qb - kb) <= 1
            ix = i % 4
            ps = PS[4 + ix]
            p.mm(ps, ps[:, :], kT, kT[:, kb * 128:(kb + 1) * 128], qTb, qTb[:, hs, :].rearrange("p a b -> p (a b)"), start=True, stop=(not near))
            if near:
                p.mm(ps, ps[:, :], identb, identb[:], NBb, NBb[:, qb - kb, hs, :].rearrange("p a b -> p (a b)"), start=False, stop=True)
            pe_ = Pe[ix]
            p.op("act", "activation", reads=[ps], writes=[pe_], out=pe_[:], in_=ps[:], func=AF.Exp)
            pm_ = Pm[ix]
            p.op("dve", "tensor_tensor", reads=[pe_, MT], writes=[pm_], out=pm_[:].rearrange("p (a b) -> p a b", a=4),
                 in0=pe_[:].rearrange("p (a b) -> p a b", a=4), in1=MT[:, kb, :].unsqueeze(1).to_broadcast([128, 4, 128]), op=ALU.mult)

        def back(i):
            hp, kb = chunks[i]
            pm_ = Pm[i % 4]
            for j in range(4):
                a_ = acc[j]
                p.mm(a_, a_[:, 0:129], pm_, pm_[:, j * 128:(j + 1) * 128], Vs, Vs[:, kb, 0:129], start=(kb == 0), stop=(kb == nkb - 1))
            if kb == nkb - 1:
                for j in range(4):
                    p.op("act", "copy", reads=[acc[j]], writes=[numsb], out=numsb[:, j, 0:129], in_=acc[j][:, 0:129])
                p.op("dve", "reciprocal", reads=[numsb], writes=[rd], out=rd[:, 0:4], in_=numsb[:, :, 128:129].rearrange("p a b -> p (a b)"))
                p.op("dve", "tensor_tensor", reads=[numsb, rd], writes=[mo], out=mo[:, hp * 512:(hp + 1) * 512].rearrange("p (a b) -> p a b", a=4),
                     in0=numsb[:, :, 0:128], in1=rd[:, 0:4].unsqueeze(2).to_broadcast([128, 4, 128]), op=ALU.mult)

        LA = 3
        for i in range(len(chunks) + LA):
            if i < len(chunks):
                front(i)
            if i - LA >= 0:
                back(i - LA)
        p.dma("pool", MO[tcols, :], mo[:], reads=[mo])
    p.pop()
    p.pop()
    p.push()
    def make_tile(r0, mo_):
        p.dma("sp", mo_[:], MO[r0:r0 + 128, :], writes=[mo_])
    out_proj_phase(c, X, S, make_tile, W["w_o"])
    p.pop()


def t5_tables(t5_bias):
    n = np.arange(256)
    nf = np.maximum(n, 16).astype(np.float32)
    large = 16 + (np.log(nf / np.float32(16)) / np.float32(np.log(8.0)) * np.float32(16)).astype(np.int32)
    bucket = np.where(n < 16, n, np.minimum(large, 31))
    s = np.arange(128)[:, None]
    t = np.arange(128)[None, :]
    bt = np.empty((128, 2, 16, 128), np.float32)
    for k2 in range(2):
        d = np.clip(t - s + 128 * k2, 0, 255)
        bt[:, k2] = np.transpose(t5_bias[bucket[d]], (0, 2, 1))
    cb = np.ascontiguousarray(np.broadcast_to(t5_bias[31][None, :], (128, 16))).astype(np.float32)
    return bt, cb


SEQ = 8192
BATCH = 4
N_CORES = 4


def build_model(S):
    p = Prog()
    c = setup_consts(p)
    din = p.dram_in
    x = din("x", [S, D])
    A = {}
    shapes = {
        "rwkv_mu_fm": [2, 128, 96], "rwkv_w_rkv": [2, 3, D, D], "rwkv_w0": [2, D], "rwkv_w1": [2, D, 96], "rwkv_w2": [2, 96, D],
        "rwkv_a0": [2, D], "rwkv_a1": [2, D, 96], "rwkv_a2": [2, 96, D], "rwkv_v0": [1, D], "rwkv_v1": [1, D, 64], "rwkv_v2": [1, 64, D],
        "rwkv_g1": [2, D, 256], "rwkv_g2": [2, 256, D], "rwkv_k_k": [2, D], "rwkv_k_a": [2, D], "rwkv_r_k": [2, D],
        "rwkv_lnx_g": [2, D], "rwkv_lnx_b": [2, D], "rwkv_w_o": [2, D, D],
        "mlstm_w_in": [1, D, 6152], "mlstm_b_if": [1, 2, 4], "mlstm_norm_g": [1, D], "mlstm_w_o": [1, D, D],
        "dsa_w_in": [1, D, 3408], "dsa_q_norm_g": [1, 128], "dsa_k_norm_g": [1, 128], "dsa_w_o": [1, D, D],
        "t5_bt": [128, 2, 16, 128], "t5_cb": [128, 16],
        "mix_norm_g": [4, D], "ffn_norm_g": [4, D], "ffn_w_gate": [4, D, DFF], "ffn_w_up": [4, D, DFF], "ffn_w_down": [4, DFF, D],
    }
    for k, v in shapes.items():
        A[k] = din(k, v)
    y = p.dram_out("y", [S, D])
    sc = {k: p.dram_tmp("sc_" + k, [S, D]) for k in RW_SC}
    sc["QK"] = p.dram_tmp("sc_QK", [2048, S], BF16)
    sc["Vb"] = p.dram_tmp("sc_Vb", [S, D], BF16)
    sc["SO"] = sc["R"]
    sc["MO"] = sc["Y"]
    sc["QT"] = sc["QK"]
    sc["QIT"] = p.dram_tmp("sc_QIT", [1024, S], BF16)
    MATS = ["rwkv_w_rkv", "rwkv_w1", "rwkv_w2", "rwkv_a1", "rwkv_a2", "rwkv_v1", "rwkv_v2", "rwkv_g1", "rwkv_g2", "rwkv_w_o",
            "mlstm_w_in", "mlstm_w_o", "dsa_w_in", "dsa_w_o", "ffn_w_gate", "ffn_w_up", "ffn_w_down"]
    pairs = []
    for k in MATS:
        shp = shapes[k]
        b = p.dram_tmp("bf_" + k, shp, BF16)
        R = int(np.prod(shp[:-1]))
        pat = {3: "a k n -> (a k) n", 4: "a b k n -> (a b k) n"}[len(shp)]
        pairs.append((A[k].rearrange(pat), b.rearrange(pat), R, shp[-1]))
        A[k] = b
    convert_weights(c, pairs)
    TR = min(S, 1024)
    for r0 in range(0, S, TR):
        p.dma("sp", y[r0:r0 + TR, :], x[r0:r0 + TR, :])
    p.barrier()
    for i in range(4):
        kind, j = i % 3, i // 3
        if kind == 0:
            W = {"mix_g": A["mix_norm_g"][i], "mu_fm": A["rwkv_mu_fm"][j], "w_rkv": A["rwkv_w_rkv"][j]}
            for nm in ["w0", "w1", "w2", "a0", "a1", "a2", "g1", "g2", "k_k", "k_a", "r_k", "lnx_g", "lnx_b", "w_o"]:
                W[nm] = A["rwkv_" + nm][j]
            if j > 0:
                for nm in ["v0", "v1", "v2"]:
                    W[nm] = A["rwkv_" + nm][j - 1]
            rwkv_prep(c, y, S, W, sc, j == 0)
            rwkv_scan(c, S, sc)
            rwkv_post(c, y, S, W, sc)
        elif kind == 1:
            W = {"mix_g": A["mix_norm_g"][i], "w_in": A["mlstm_w_in"][j], "b_if": A["mlstm_b_if"][j], "norm_g": A["mlstm_norm_g"][j], "w_o": A["mlstm_w_o"][j]}
            mlstm_layer(c, y, S, W, sc)
        else:
            W = {"mix_g": A["mix_norm_g"][i], "w_in": A["dsa_w_in"][j], "q_norm_g": A["dsa_q_norm_g"][j], "k_norm_g": A["dsa_k_norm_g"][j],
                 "w_o": A["dsa_w_o"][j], "t5_bt": A["t5_bt"], "t5_cb": A["t5_cb"]}
            dsa_layer(c, y, S, W, sc)
        ffn_phase(c, y, S, A["ffn_norm_g"][i], A["ffn_w_gate"][i], A["ffn_w_up"][i], A["ffn_w_down"][i])
    return p


def host_inputs(inputs):
    f = lambda a: np.ascontiguousarray(np.asarray(a, dtype=np.float32))
    feed = {}
    mu = f(inputs["rwkv_mu"])
    feed["rwkv_mu_fm"] = np.ascontiguousarray(mu.reshape(mu.shape[0], 6, 16, 128).transpose(0, 3, 1, 2).reshape(mu.shape[0], 128, 96))
    for k in ["rwkv_w_rkv", "rwkv_w0", "rwkv_w1", "rwkv_w2", "rwkv_a0", "rwkv_a1", "rwkv_a2", "rwkv_v0", "rwkv_v1", "rwkv_v2", "rwkv_g1", "rwkv_g2",
              "rwkv_k_k", "rwkv_k_a", "rwkv_lnx_g", "rwkv_lnx_b", "rwkv_w_o", "mlstm_w_in", "mlstm_b_if", "mlstm_norm_g", "mlstm_w_o",
              "dsa_w_in", "dsa_q_norm_g", "dsa_k_norm_g", "dsa_w_o", "mix_norm_g", "ffn_norm_g", "ffn_w_gate", "ffn_w_up", "ffn_w_down"]:
        feed[k] = f(inputs[k])
    rk = f(inputs["rwkv_r_k"])
    feed["rwkv_r_k"] = rk.reshape(rk.shape[0], -1)
    bt, cb = t5_tables(f(inputs["t5_bias"]))
    feed["t5_bt"] = bt
    feed["t5_cb"] = cb
    return feed


_CACHE = {}


def kernel(**inputs):
    x = np.asarray(inputs["x"], dtype=np.float32)
    B, S, _ = x.shape
    if S not in _CACHE:
        _CACHE[S] = build_model(S).finish()
    nc = _CACHE[S]
    feed = host_inputs(inputs)
    in_maps = []
    for b in range(B):
        m = dict(feed)
        m["x"] = np.ascontiguousarray(x[b])
        in_maps.append(m)
    res = run_bass_kernel_spmd(nc, in_maps, core_ids=list(range(B)))
    return np.stack([np.asarray(res.results[b]["y"], dtype=np.float32) for b in range(B)], axis=0)
```
